# Optimizing a Trainium2 kernel written in Bass

```python
import jax, jax.numpy as jnp
from jax import lax
import numpy as np

D_MODEL = 1024
BATCH = 16
SEQ = 2048
DEPTH = 1

HEAD_DIM = 64
A_Q_HEADS = 8
A_KV_HEADS = 2
A_GROUP = A_Q_HEADS // A_KV_HEADS
A_WINDOW = 128
B_HEADS = 8
B_BRANCHES = ((128, 1), (512, 4), (2048, 16))
ROT_DIM = HEAD_DIM // 4
ROPE_THETA = 500000.0
D_FF = 4 * D_MODEL
A_Q_W = A_Q_HEADS * HEAD_DIM
A_KV_W = A_KV_HEADS * HEAD_DIM
B_W = B_HEADS * HEAD_DIM
MIX_W = A_Q_W + B_W
IN_W = A_Q_W + 2 * A_KV_W + 3 * B_W
N_MOD = 6
BLOCK = 128
EPS = 1e-6
NEG_INF = -1e30

kernel_name = 'hybrid_swa_sink_dilated_sqrelu_block'


def rms_norm(x, g):
    xf = x.astype(jnp.float32)
    y = xf * lax.rsqrt(jnp.mean(xf * xf, axis=-1, keepdims=True) + EPS)
    return (y * g.astype(jnp.float32)).astype(x.dtype)


def partial_rope(x, cos, sin):
    half = ROT_DIM // 2
    x1 = x[..., :half].astype(jnp.float32)
    x2 = x[..., half:ROT_DIM].astype(jnp.float32)
    rot = jnp.concatenate([x1 * cos - x2 * sin, x2 * cos + x1 * sin], axis=-1).astype(x.dtype)
    return jnp.concatenate([rot, x[..., ROT_DIM:]], axis=-1)


def banded_attention(q, k, v, max_dist, sink=None, return_lse=False):
    n, seq_len, hk, g, dh = q.shape
    blk = min(BLOCK, seq_len)
    nb = -(-seq_len // blk)
    lp = nb * blk
    pad = lp - seq_len
    if pad:
        q = jnp.pad(q, ((0, 0), (0, pad), (0, 0), (0, 0), (0, 0)))
        k = jnp.pad(k, ((0, 0), (0, pad), (0, 0), (0, 0)))
        v = jnp.pad(v, ((0, 0), (0, pad), (0, 0), (0, 0)))
    qb = q.reshape(n, nb, blk, hk, g, dh)
    kb = k.reshape(n, nb, blk, hk, dh)
    vb = v.reshape(n, nb, blk, hk, dh)
    kcat = jnp.concatenate([jnp.pad(kb, ((0, 0), (1, 0), (0, 0), (0, 0), (0, 0)))[:, :nb], kb], axis=2)
    vcat = jnp.concatenate([jnp.pad(vb, ((0, 0), (1, 0), (0, 0), (0, 0), (0, 0)))[:, :nb], vb], axis=2)
    s = jnp.einsum('ncqhgd,nckhd->nhgcqk', qb, kcat,
                   preferred_element_type=jnp.float32) * (1.0 / float(np.sqrt(dh)))
    blocks = jnp.arange(nb)[:, None, None]
    qpos = blocks * blk + jnp.arange(blk)[None, :, None]
    kpos = (blocks - 1) * blk + jnp.arange(2 * blk)[None, None, :]
    dist = qpos - kpos
    valid = (dist >= 0) & (dist <= max_dist) & (kpos >= 0)
    s = jnp.where(valid, s, NEG_INF)
    m = jnp.max(s, axis=-1, keepdims=True)
    if sink is not None:
        sk = sink.astype(jnp.float32).reshape(hk, g)[None, :, :, None, None, None]
        m = jnp.maximum(m, sk)
        p = jnp.exp(s - m)
        denom = jnp.sum(p, axis=-1, keepdims=True) + jnp.exp(sk - m)
    else:
        p = jnp.exp(s - m)
        denom = jnp.sum(p, axis=-1, keepdims=True)
    o = jnp.einsum('nhgcqk,nckhd->ncqhgd', p / denom, vcat.astype(jnp.float32))
    o = o.reshape(n, lp, hk, g, dh)[:, :seq_len]
    if return_lse:
        lse = (m + jnp.log(denom))[..., 0]
        lse = lse.transpose(0, 3, 4, 1, 2).reshape(n, lp, hk, g)[:, :seq_len]
        return o, lse
    return o


def dilated_mixture_attention(q, k, v):
    b, s, h, dh = q.shape
    outs, lses = [], []
    for window, d in B_BRANCHES:
        sub = s // d

        def to_sub(t):
            return t.reshape(b, sub, d, h, dh).transpose(0, 2, 1, 3, 4).reshape(b * d, sub, h, dh)

        o, lse = banded_attention(to_sub(q)[:, :, :, None, :], to_sub(k), to_sub(v),
                                  window // d, return_lse=True)
        outs.append(o[:, :, :, 0].reshape(b, d, sub, h, dh).transpose(0, 2, 1, 3, 4).reshape(b, s, h, dh))
        lses.append(lse[..., 0].reshape(b, d, sub, h).transpose(0, 2, 1, 3).reshape(b, s, h))
    w = jax.nn.softmax(jnp.stack(lses, axis=0), axis=0)
    return jnp.sum(w[..., None] * jnp.stack(outs, axis=0), axis=0)


def setup_inputs(seed: int = 0) -> dict:
    key = jax.random.key(seed)
    ks = jax.random.split(key, 17)
    f32 = jnp.float32

    def gain(k, n):
        return 1.0 + 0.05 * jax.random.normal(k, (DEPTH, n), f32)

    x = jax.random.normal(ks[0], (BATCH, SEQ, D_MODEL), f32)
    c = jax.random.normal(ks[1], (BATCH, D_MODEL), f32)
    offsets = jax.random.randint(ks[2], (BATCH, 1), 0, 1024, dtype=jnp.int32)
    positions = offsets + jnp.arange(SEQ, dtype=jnp.int32)[None, :]
    w_ada = jax.random.normal(ks[3], (DEPTH, D_MODEL, N_MOD * D_MODEL), f32) * D_MODEL ** -0.5
    b_ada = 0.02 * jax.random.normal(ks[4], (DEPTH, N_MOD * D_MODEL), f32)
    g_attn_pre = gain(ks[5], D_MODEL)
    g_attn_post = gain(ks[6], D_MODEL)
    w_in = jax.random.normal(ks[7], (DEPTH, D_MODEL, IN_W), f32) * D_MODEL ** -0.5
    sink_a = jax.random.normal(ks[8], (DEPTH, A_Q_HEADS), f32)
    g_mix_a = gain(ks[9], A_Q_W)
    g_mix_b = gain(ks[10], B_W)
    w_out = jax.random.normal(ks[11], (DEPTH, MIX_W, D_MODEL), f32) * MIX_W ** -0.5
    g_mlp_pre = gain(ks[12], D_MODEL)
    g_mlp_post = gain(ks[13], D_MODEL)
    w_up = jax.random.normal(ks[14], (DEPTH, D_MODEL, D_FF), f32) * D_MODEL ** -0.5
    w_down = jax.random.normal(ks[15], (DEPTH, D_FF, D_MODEL), f32) * D_FF ** -0.5
    return {'x': x, 'c': c, 'positions': positions, 'w_ada': w_ada, 'b_ada': b_ada,
            'g_attn_pre': g_attn_pre, 'g_attn_post': g_attn_post, 'w_in': w_in,
            'sink_a': sink_a, 'g_mix_a': g_mix_a, 'g_mix_b': g_mix_b, 'w_out': w_out,
            'g_mlp_pre': g_mlp_pre, 'g_mlp_post': g_mlp_post, 'w_up': w_up, 'w_down': w_down}


def reference(x, c, positions, w_ada, b_ada, g_attn_pre, g_attn_post, w_in, sink_a,
              g_mix_a, g_mix_b, w_out, g_mlp_pre, g_mlp_post, w_up, w_down):
    b, s, _ = x.shape
    inv_freq = ROPE_THETA ** (-jnp.arange(0, ROT_DIM, 2, dtype=jnp.float32) / ROT_DIM)
    ang = positions.astype(jnp.float32)[..., None] * inv_freq
    cos = jnp.cos(ang)[:, :, None, :]
    sin = jnp.sin(ang)[:, :, None, :]
    cond = jax.nn.silu(c)
    o1 = A_Q_W
    o2 = o1 + A_KV_W
    o3 = o2 + A_KV_W
    o4 = o3 + B_W
    o5 = o4 + B_W
    for l in range(DEPTH):
        mod = (cond @ w_ada[l] + b_ada[l]).astype(x.dtype)
        sh_a, sc_a, gt_a, sh_m, sc_m, gt_m = [m[:, None, :] for m in jnp.split(mod, N_MOD, axis=-1)]

        h = rms_norm(x, g_attn_pre[l]) * (1 + sc_a) + sh_a
        proj = h @ w_in[l]
        qa = partial_rope(proj[..., :o1].reshape(b, s, A_Q_HEADS, HEAD_DIM), cos, sin)
        ka = partial_rope(proj[..., o1:o2].reshape(b, s, A_KV_HEADS, HEAD_DIM), cos, sin)
        va = proj[..., o2:o3].reshape(b, s, A_KV_HEADS, HEAD_DIM)
        qb = partial_rope(proj[..., o3:o4].reshape(b, s, B_HEADS, HEAD_DIM), cos, sin)
        kb = partial_rope(proj[..., o4:o5].reshape(b, s, B_HEADS, HEAD_DIM), cos, sin)
        vb = proj[..., o5:].reshape(b, s, B_HEADS, HEAD_DIM)

        oa = banded_attention(qa.reshape(b, s, A_KV_HEADS, A_GROUP, HEAD_DIM), ka, va,
                              A_WINDOW - 1, sink=sink_a[l])
        oa = oa.reshape(b, s, A_Q_W).astype(x.dtype)
        ob = dilated_mixture_attention(qb, kb, vb).reshape(b, s, B_W).astype(x.dtype)

        mixed = jnp.concatenate([rms_norm(oa, g_mix_a[l]), rms_norm(ob, g_mix_b[l])], axis=-1)
        y = mixed @ w_out[l]
        x = x + gt_a * rms_norm(y, g_attn_post[l])

        h = rms_norm(x, g_mlp_pre[l]) * (1 + sc_m) + sh_m
        y = jnp.square(jax.nn.relu(h @ w_up[l])) @ w_down[l]
        x = x + gt_m * rms_norm(y, g_mlp_post[l])
    return x
```

```python
import contextlib
import numpy as np
import ml_dtypes
import concourse.bass as bass
import concourse.mybir as mybir
from concourse.alu_op_type import AluOpType as ALU
from concourse.bass_utils import run_bass_kernel_spmd

F32 = mybir.dt.float32
BF16 = mybir.dt.bfloat16
I32 = mybir.dt.int32
AF = mybir.ActivationFunctionType

NCORES = 8
SEQ = 2048
D = 1024
DFF = 4096
EPS = 1e-6
NW = 3
TWO_PI_HI = 6.28125
TWO_PI_LO = 2.0 * np.pi - 6.28125
PI_F = float(np.float32(np.pi))


class Sched:
    ENGS = ("pe", "act", "dve", "pool", "sp")

    def __init__(self, nc):
        self.nc = nc
        self.ops = {e: [] for e in self.ENGS}
        self.cnt = {e: 0 for e in self.ENGS}
        self.chan_cnt = {}
        self.lastw = {}
        self.readers = {}
        self.seen = {e: {} for e in self.ENGS}

    def _dep(self, eng, tok, waits):
        semkey, val = tok[0], tok[1]
        if self.seen[eng].get(semkey, 0) >= val:
            return
        self.seen[eng][semkey] = val
        waits[semkey] = max(waits.get(semkey, 0), val)

    def op(self, eng, fn, reads=(), writes=(), chan=None):
        waits = {}
        if eng != "pe":
            extra = [r for r in reads if isinstance(r, tuple) and r[0] == "ps" and r not in writes]
            if extra:
                writes = list(writes) + extra
        for r in reads:
            t = self.lastw.get(r)
            if t is not None:
                if t[0][0] == "e" and t[2] == eng and eng == "pe":
                    continue
                self._dep(eng, t, waits)
        for w in writes:
            t = self.lastw.get(w)
            if t is not None and not (t[0][0] == "e" and t[2] == eng and eng == "pe"):
                self._dep(eng, t, waits)
            for t in self.readers.get(w, ()):
                if t[0][0] == "e" and t[2] == eng and eng == "pe":
                    continue
                self._dep(eng, t, waits)
        if chan is None:
            self.cnt[eng] += 1
            tok = (("e", eng), self.cnt[eng], eng)
        else:
            self.chan_cnt[chan] = self.chan_cnt.get(chan, 0) + 1
            tok = (("c", chan), 16 * self.chan_cnt[chan], eng)
        self.ops[eng].append(dict(fn=fn, waits=waits, chan=chan))
        for r in reads:
            self.readers.setdefault(r, []).append(tok)
        for w in writes:
            self.lastw[w] = tok
            self.readers[w] = []
        return tok

    def emit(self):
        nc = self.nc
        with contextlib.ExitStack() as st:
            sems = {}
            for e in self.ENGS:
                sems[("e", e)] = st.enter_context(nc.semaphore("s_" + e))
            for i, c in enumerate(self.chan_cnt):
                sems[("c", c)] = st.enter_context(nc.semaphore("c%d" % i))
            block = st.enter_context(nc.Block())
            handles = dict(pe="tensor", act="scalar", dve="vector", pool="gpsimd", sp="sync")

            def make(e):
                def body(h):
                    for o in self.ops[e]:
                        for sk, v in o["waits"].items():
                            h.wait_ge(sems[sk], v)
                        if o["fn"] is None:
                            continue
                        ins = o["fn"](h)
                        if o["chan"] is None:
                            ins.then_inc(sems[("e", e)], 1)
                        else:
                            ins.then_inc(sems[("c", o["chan"])], 16)
                return body

            for e in self.ENGS:
                if self.ops[e]:
                    getattr(block, handles[e])(make(e))


def build_nc(debug=None, stop=None, nt=8):
    nc = bass.Bass("TRN2", target_bir_lowering=False)

    def din(name, shape, dt):
        return nc.dram_tensor(name, shape, dt, kind="ExternalInput").ap()

    x_d = din("x", [4096, D], F32)
    cT_d = din("cT", [128, 16], F32)
    pos_d = din("pos", [128, 32], I32)
    w_ada_d = din("w_ada", [D, 6 * D], F32)
    b_ada_d = din("b_ada", [1, 6 * D], F32)
    gpreT_d = din("gpreT", [128, 8], F32)
    gmlpT_d = din("gmlpT", [128, 8], F32)
    gmixT_d = din("gmixT", [128, 8], F32)
    gposta_d = din("gposta", [1, D], F32)
    gpostm_d = din("gpostm", [1, D], F32)
    w_in_d = din("w_in", [D, 2304], F32)
    w_out_d = din("w_out", [D, D], F32)
    w_up_d = din("w_up", [D, DFF], F32)
    w_down_d = din("w_down", [DFF, D], F32)
    sink_d = din("sink", [1, 8], F32)
    ident_d = din("ident", [128, 128], F32)
    maskB_d = din("maskB", [128, 9 * 512], BF16)
    maskA_d = din("maskA", [128, 256], BF16)
    sel_d = din("sel", [2, 256], F32)
    invf_d = din("invf", [1, 8], F32)
    out_d = nc.dram_tensor("out", [4096, D], F32, kind="ExternalOutput").ap()
    wscr = nc.dram_tensor("wscr", [23, 128, 4096], BF16, kind="Internal").ap()

    dbg_outs = {}
    S = Sched(nc)

    with contextlib.ExitStack() as st:
        def sb(name, shape, dt):
            return st.enter_context(nc.sbuf_tensor(name, shape, dt))

        def psum(name, shape, dt):
            return st.enter_context(nc.psum_tensor(name, shape, dt))

        xt = [sb("xt%d" % i, [128, 4, D], F32) for i in range(2)]
        big = sb("big", [128, 16384], BF16)
        uT = big[:].rearrange("p (c t) -> p c t", c=32)
        nbt = sb("nbt", [128, 4, D], F32)
        nb = nbt[:]
        qk = big[:, 8192:16384].rearrange("p (b d) -> p b d", b=4)
        stage = [big[:, 8192 * i:8192 * (i + 1)].bitcast(F32) for i in range(2)]
        junk = sb("junk", [128, D], BF16)
        hT = sb("hT", [128, 8, 512], BF16)
        qT = sb("qT", [128, 8, 512], BF16)
        kAT = sb("kAT", [128, SEQ], BF16)
        kBT = sb("kBT", [128, 4, SEQ], BF16)
        vA = sb("vA", [128, 16, 2, 65], BF16)
        vB = sb("vB", [128, 16, 8, 65], BF16)
        rtmp = [sb("rtmp%d" % i, [128, 512], F32) for i in range(2)]
        PT = [sb("PT%d" % i, [128, 512], BF16) for i in range(4)]
        ws = [sb("ws%d" % i, [128, 4096], BF16) for i in range(NW)]
        maskB = sb("maskB_s", [128, 9, 512], BF16)
        maskA = sb("maskA_s", [128, 2, 128], BF16)
        ident = sb("ident_s", [128, 128], F32)
        identb = sb("identb", [128, 128], BF16)
        ggt = sb("ggt", [128, 2, 2, D], F32)
        modT = sb("modT", [128, 2, 4, 8], F32)
        gT = sb("gT", [128, 3, 8], F32)
        CC = sb("CC", [128, 32, 16], F32)
        SS = sb("SS", [128, 32, 16], F32)
        esink = sb("esink", [128, 8], F32)
        mhalf = sb("mhalf", [128, 8], F32)
        zb = sb("zb", [128, 1], F32)
        cT = sb("cT_s", [128, 16], F32)
        cond = sb("cond", [128, 16], F32)
        ctmp = sb("ctmp", [128, 16], F32)
        posi = sb("posi", [128, 32], I32)
        posf = sb("posf", [128, 32], F32)
        invf = sb("invf_s", [128, 8], F32)
        sel = sb("sel_s", [2, 2, 128], F32)
        st_ss = sb("st_ss", [128, 8], F32)
        st_v = sb("st_v", [128, 8], F32)
        st_r = sb("st_r", [128, 8], F32)
        ssy = [sb("ssy%d" % i, [128, 4], F32) for i in range(4)]
        den = [sb("den%d" % i, [128, 4], F32) for i in range(4)]
        ropA = [sb("ropA%d" % i, [128, 128], F32) for i in range(4)]
        ropB = [sb("ropB%d" % i, [128, 128], F32) for i in range(4)]
        rpro = qT[:, 0:4, :].rearrange("p a b -> p (a b)").bitcast(F32).rearrange("p (a b) -> p a b", a=4)
        ang = rpro[:, 0, :].rearrange("p (a b) -> p a b", b=8)
        ra = rpro[:, 1, :].rearrange("p (a b) -> p a b", b=8)
        rb = rpro[:, 2, :].rearrange("p (a b) -> p a b", b=8)
        rki = rpro[:, 3, :].bitcast(I32).rearrange("p (a b) -> p a b", b=8)

        ps = [psum("ps%d" % i, [128, 512], F32) for i in range(8)]
        psb = [p[:].bitcast(BF16) for p in ps]

        gpb = xt[1]
        modp = [rtmp[i][0:2, :] for i in range(2)]
        bada = junk[0:2, :].bitcast(F32)

        def P(i):
            return ("ps", i)

        def nbres(b):
            return [("nb", b, i) for i in range(4)]

        def nbhalf(b, mx):
            return [("nb", b, 2 * mx), ("nb", b, 2 * mx + 1)]

        def qkres(b):
            return [("uT", 16 + 4 * b + i) for i in range(4)]

        def stres(i):
            return [("uT", 16 * i + j) for j in range(16)]

        rings = {}

        def ring(name, n):
            v = rings.get(name, 0)
            rings[name] = v + 1
            return v % n

        def dbg(name, src_ap, shape, dt, reads):
            if debug is None or name not in debug:
                return
            d = nc.dram_tensor("dbg_" + name, shape, dt, kind="ExternalOutput").ap()
            dbg_outs[name] = d
            S.op("sp", lambda e: e.dma_start(out=d, in_=src_ap), reads=reads, writes=[("dbg", name)],
                 chan=("dbg", name))

        def ld(dst, src, res, ch):
            S.op("sp", lambda e: e.dma_start(out=dst, in_=src), writes=[res], chan=ch)

        ld(ident[:], ident_d, "ident", "k0")
        ld(maskB[:].rearrange("p a b -> p (a b)"), maskB_d, "maskB", "k1")
        ld(maskA[:].rearrange("p a b -> p (a b)"), maskA_d, "maskA", "k2")
        ld(sel[:].rearrange("p a b -> p (a b)"), sel_d, "sel", "k3")
        ld(invf[:], invf_d.partition_broadcast(128), "invf", "k4")
        ld(esink[:], sink_d.partition_broadcast(128), "esink0", "k5")
        ld(cT[:], cT_d, "cT", "k6")
        ld(posi[:], pos_d, "posi", "k7")
        ld(gT[:, 0, :], gpreT_d, "gT0", "k8")
        ld(gT[:, 1, :], gmlpT_d, "gT1", "k9")
        ld(gT[:, 2, :], gmixT_d, "gT2", "k10")
        ld(gpb[:, 0, :], gposta_d.partition_broadcast(128), ("xt", 1, 0), "k11")
        ld(gpb[:, 1, :], gpostm_d.partition_broadcast(128), ("xt", 1, 1), "k12")

        S.op("pool", lambda e: e.memset(mhalf[:], -0.5), writes=["mhalf"])
        S.op("pool", lambda e: e.memset(zb[:], 0.0), writes=["zb"])
        S.op("pool", lambda e: e.memset(vA[:, :, :, 64:65], 1.0), writes=[("vA", i) for i in range(16)])
        S.op("pool", lambda e: e.memset(vB[:, :, :, 64:65], 1.0), writes=[("vB", i) for i in range(16)])
        def piece_src(q):
            if q < 5:
                w = 512 if q < 4 else 256
                return w_in_d.rearrange("(k p) c -> p k c", p=128)[:, :, q * 512:q * 512 + w], (8, w)
            if q < 7:
                h = q - 5
                return w_out_d.rearrange("(k p) c -> p k c", p=128)[:, :, h * 512:(h + 1) * 512], (8, 512)
            if q < 15:
                g = q - 7
                return w_up_d.rearrange("(k p) c -> p k c", p=128)[:, :, g * 512:(g + 1) * 512], (8, 512)
            g = q - 15
            return w_down_d[g * 512:(g + 1) * 512, :].rearrange("(f p) d -> p f d", p=128), (4, D)

        for q in range(23):
            src, (a, bdim) = piece_src(q)
            dstv = wscr[q, :, 0:a * bdim].rearrange("p (a b) -> p a b", a=a)
            S.op("pool", lambda e, src=src, dstv=dstv: e.dma_start(out=dstv, in_=src), writes=[("wscr", q)],
                 chan=("cv", q))

        S.op("dve", lambda e: e.tensor_copy(out=identb[:], in_=ident[:]), reads=["ident"], writes=["identb"])
        S.op("act", lambda e: e.activation(out=esink[:], in_=esink[:], func=AF.Exp), reads=["esink0"],
             writes=["esink"])

        S.op("act", lambda e: e.activation(out=ctmp[:], in_=cT[:], func=AF.Exp, scale=-1.0), reads=["cT"],
             writes=["ctmp"])
        S.op("dve", lambda e: e.tensor_scalar(out=ctmp[:], in0=ctmp[:], scalar1=1.0, scalar2=None, op0=ALU.add),
             reads=["ctmp"], writes=["ctmp"])
        S.op("dve", lambda e: e.reciprocal(out=ctmp[:], in_=ctmp[:]), reads=["ctmp"], writes=["ctmp"])
        S.op("dve", lambda e: e.tensor_tensor(out=cond[:], in0=cT[:], in1=ctmp[:], op=ALU.mult),
             reads=["ctmp", "cT"], writes=["cond"])
        condv = cond[:].rearrange("p (k b) -> p k b", b=2)

        S.op("dve", lambda e: e.tensor_copy(out=posf[:], in_=posi[:]), reads=["posi"], writes=["posf"])
        S.op("dve", lambda e: e.tensor_tensor(out=ang, in0=posf[:].unsqueeze(2).to_broadcast([128, 32, 8]),
                                              in1=invf[:].unsqueeze(1).to_broadcast([128, 32, 8]), op=ALU.mult),
             reads=["posf", "invf"], writes=[("qT", 0)])
        S.op("dve", lambda e: e.tensor_scalar(out=ra, in0=ang, scalar1=float(1.0 / (2 * np.pi)), scalar2=None,
                                              op0=ALU.mult), reads=[("qT", 0)], writes=[("qT", 1)])
        S.op("dve", lambda e: e.tensor_copy(out=rki, in_=ra), reads=[("qT", 1)], writes=[("qT", 3)])
        S.op("dve", lambda e: e.tensor_copy(out=ra, in_=rki), reads=[("qT", 3)], writes=[("qT", 1)])
        S.op("dve", lambda e: e.scalar_tensor_tensor(out=rb, in0=ra, scalar=-TWO_PI_HI, in1=ang,
                                                     op0=ALU.mult, op1=ALU.add), reads=[("qT", 1), ("qT", 0)], writes=[("qT", 2)])
        S.op("dve", lambda e: e.scalar_tensor_tensor(out=rb, in0=ra, scalar=-TWO_PI_LO, in1=rb,
                                                     op0=ALU.mult, op1=ALU.add), reads=[("qT", 1), ("qT", 2)], writes=[("qT", 2)])

        def wrap(buf, tmp, name, tname):
            S.op("dve", lambda e: e.tensor_scalar(out=tmp, in0=buf, scalar1=PI_F, scalar2=-2.0 * np.pi,
                                                  op0=ALU.is_gt, op1=ALU.mult), reads=[name], writes=[tname])
            S.op("dve", lambda e: e.tensor_tensor(out=buf, in0=buf, in1=tmp, op=ALU.add),
                 reads=[name, tname], writes=[name])
            S.op("dve", lambda e: e.tensor_scalar(out=tmp, in0=buf, scalar1=-PI_F, scalar2=2.0 * np.pi,
                                                  op0=ALU.is_lt, op1=ALU.mult), reads=[name], writes=[tname])
            S.op("dve", lambda e: e.tensor_tensor(out=buf, in0=buf, in1=tmp, op=ALU.add),
                 reads=[name, tname], writes=[name])

        wrap(rb, ra, ("qT", 2), ("qT", 1))
        S.op("act", lambda e: e.activation(out=SS[:, :, 8:16], in_=rb, func=AF.Sin), reads=[("qT", 2)], writes=["SSb"])
        S.op("dve", lambda e: e.tensor_scalar(out=SS[:, :, 0:8], in0=SS[:, :, 8:16], scalar1=-1.0, scalar2=None,
                                              op0=ALU.mult), reads=["SSb"], writes=["SSa"])
        S.op("dve", lambda e: e.tensor_scalar(out=rb, in0=rb, scalar1=float(np.pi / 2), scalar2=None,
                                              op0=ALU.add), reads=[("qT", 2), "SSb"], writes=[("qT", 2)])
        wrap(rb, ra, ("qT", 2), ("qT", 1))
        S.op("act", lambda e: e.activation(out=CC[:, :, 0:8], in_=rb, func=AF.Sin), reads=[("qT", 2)], writes=["CCa"])
        S.op("dve", lambda e: e.tensor_copy(out=CC[:, :, 8:16], in_=CC[:, :, 0:8]), reads=["CCa"], writes=["CCb"])
        ROPE_RES = ["CCa", "CCb", "SSa", "SSb"]

        w_ada_v = w_ada_d.rearrange("(k p) c -> p k c", p=128)
        for n in range(12):
            si = n % 2
            stg = stage[si].rearrange("p (k c) -> p k c", k=8)
            S.op("sp", lambda e, stg=stg, n=n: e.dma_start(out=stg, in_=w_ada_v[:, :, n * 512:(n + 1) * 512]),
                 writes=stres(si), chan=("stg", si))
            S.op("sp", lambda e, n=n: e.dma_start(out=bada, in_=b_ada_d[:, n * 512:(n + 1) * 512].partition_broadcast(2)),
                 writes=["junk"], chan="bada")
            pm = 2 + (n % 2)

            def mmf(e, stg=stg, pm=pm):
                for k in range(8):
                    ins = e.matmul(ps[pm][0:2, :], lhsT=condv[:, k, :], rhs=stg[:, k, :], start=(k == 0), stop=(k == 7))
                return ins
            S.op("pe", mmf, reads=stres(si) + ["cond"], writes=[P(pm)])
            mp = modp[n % 2]
            S.op("dve", lambda e, mp=mp, pm=pm: e.tensor_tensor(out=mp, in0=ps[pm][0:2, :], in1=bada, op=ALU.add),
                 reads=[P(pm), "junk"], writes=[("rtmp", n % 2)])
            v, half = n // 2, n % 2
            if v in (2, 5):
                am = 0 if v == 2 else 1
                for s in range(2):
                    pg = 4 + s
                    S.op("pe", lambda e, mp=mp, s=s, pg=pg: e.matmul(ps[pg][:, :], lhsT=sel[:, s, :], rhs=mp,
                                                                       start=True, stop=True),
                         reads=[("rtmp", n % 2), "sel"], writes=[P(pg)])
                    S.op("dve", lambda e, s=s, pg=pg, am=am, half=half: e.tensor_tensor(
                        out=ggt[:, s, am, half * 512:(half + 1) * 512], in0=ps[pg][:, :],
                        in1=gpb[:, am, half * 512:(half + 1) * 512], op=ALU.mult),
                        reads=[P(pg), ("xt", 1, 0), ("xt", 1, 1)], writes=[("ggt", s, am, half)])
            else:
                vi = {0: 0, 1: 1, 3: 2, 4: 3}[v]
                pg = 6

                def trf(e, mp=mp):
                    for q in range(4):
                        ins = e.transpose(out=ps[pg][:, 2 * q:2 * q + 2], in_=mp[:, q * 128:(q + 1) * 128],
                                          identity=ident[0:2, 0:2])
                    return ins
                S.op("pe", trf, reads=[("rtmp", n % 2), "ident"], writes=[P(pg)])
                S.op("dve", lambda e, vi=vi, half=half: e.tensor_copy(
                    out=modT[:, :, vi, half * 4:half * 4 + 4],
                    in_=ps[pg][:, 0:8].rearrange("p (q s) -> p s q", s=2)),
                    reads=[P(pg)], writes=[("modT", vi, half)])
        for vi, gi in ((1, 0), (3, 1)):
            S.op("dve", lambda e, vi=vi: e.tensor_scalar(out=modT[:, :, vi, :], in0=modT[:, :, vi, :], scalar1=1.0,
                                                         scalar2=None, op0=ALU.add),
                 reads=[("modT", vi, 0), ("modT", vi, 1)], writes=[("modT", vi, 0), ("modT", vi, 1)])
            S.op("dve", lambda e, vi=vi, gi=gi: e.tensor_tensor(
                out=modT[:, :, vi, :], in0=modT[:, :, vi, :], in1=gT[:, gi, :].unsqueeze(1).to_broadcast([128, 2, 8]),
                op=ALU.mult), reads=[("modT", vi, 0), ("modT", vi, 1), "gT%d" % gi],
                writes=[("modT", vi, 0), ("modT", vi, 1)])
        MODT_RES = [("modT", vi, h) for vi in range(4) for h in range(2)]
        GGT_RES = [("ggt", s, am, h) for s in range(2) for am in range(2) for h in range(2)]

        wstate = {"n": 0}

        def load_piece(q, n_el=4096):
            sl = wstate["n"] % NW
            wstate["n"] += 1
            S.op("sp", lambda e: e.dma_start(out=ws[sl][:, 0:n_el], in_=wscr[q, :, 0:n_el]), reads=[("wscr", q)],
                 writes=[("ws", sl)], chan=("wl", sl))
            return sl

        x_v = x_d.rearrange("(t b p) d -> t p b d", p=128, b=4)
        out_v = out_d.rearrange("(t b p) d -> t p b d", p=128, b=4)

        def load_x(t):
            buf = t % 2
            S.op("sp", lambda e: e.dma_start(out=xt[buf][:], in_=x_v[t]), writes=[("xt", buf, b) for b in range(4)],
                 chan=("xl", buf))

        def prenorm_stats(buf):
            X = xt[buf]
            for b in range(4):
                S.op("act", lambda e, b=b: e.activation(out=junk[:], in_=X[:, b, :], func=AF.Square,
                                                        accum_out=st_ss[:, b:b + 1]),
                     reads=[("xt", buf, b)], writes=["junk", ("ss", b)])
                S.op("dve", lambda e, b=b: e.tensor_scalar(out=st_v[:, b:b + 1], in0=st_ss[:, b:b + 1], scalar1=1.0 / D,
                                                           scalar2=EPS, op0=ALU.mult, op1=ALU.add),
                     reads=[("ss", b)], writes=[("stv", b)])
                S.op("pool", lambda e, b=b: e.tensor_tensor(out=st_r[:, b:b + 1], in0=st_v[:, b:b + 1],
                                                            in1=mhalf[:, 0:1], op=ALU.pow),
                     reads=[("stv", b), "mhalf"], writes=[("str", b)])
                if b % 2 == 0:
                    S.op("dve", lambda e, b=b: e.tensor_scalar(out=nb[:, b, :], in0=X[:, b, :], scalar1=st_r[:, b:b + 1],
                                                               scalar2=None, op0=ALU.mult),
                         reads=[("xt", buf, b), ("str", b)], writes=nbres(b))
                else:
                    S.op("act", lambda e, b=b: e.activation(out=nb[:, b, :], in_=X[:, b, :], func=AF.Identity,
                                                            scale=st_r[:, b:b + 1], bias=zb[:]),
                         reads=[("xt", buf, b), ("str", b), "zb"], writes=nbres(b))

        def prenorm_tr(s, vi_sh, vi_gsc, banks=(0, 1)):
            transposes_to_hT(lambda k: (modT[:, s, vi_gsc, k:k + 1], modT[:, s, vi_sh, k:k + 1]), MODT_RES, banks=banks)

        def transposes_to_hT(scale_bias, extra_reads, fine=False, banks=(0, 1)):
            for k in range(8):
                pb = banks[k % 2]

                def trf(e, k=k, pb=pb):
                    for b in range(4):
                        ins = e.transpose(out=ps[pb][:, b * 128:(b + 1) * 128], in_=nb[:, b, k * 128:(k + 1) * 128],
                                          identity=ident[:])
                    return ins
                if fine:
                    rr = [("nb", b, k // 2) for b in range(4)]
                else:
                    rr = [r for b in range(4) for r in nbres(b)]
                S.op("pe", trf, reads=rr + ["ident"], writes=[P(pb)])
                sc, bi = scale_bias(k)
                if k % 2 == 0:
                    S.op("act", lambda e, k=k, pb=pb, sc=sc, bi=bi: e.activation(
                        out=hT[:, k, :], in_=ps[pb][:, :], func=AF.Identity, scale=sc, bias=(bi if bi is not None else zb[:])),
                        reads=[P(pb), "zb"] + extra_reads, writes=[("hT", k)])
                else:
                    if bi is not None:
                        S.op("dve", lambda e, k=k, pb=pb, sc=sc, bi=bi: e.tensor_scalar(
                            out=hT[:, k, :], in0=ps[pb][:, :], scalar1=sc, scalar2=bi, op0=ALU.mult, op1=ALU.add),
                            reads=[P(pb)] + extra_reads, writes=[("hT", k)])
                    else:
                        S.op("dve", lambda e, k=k, pb=pb, sc=sc: e.tensor_scalar(
                            out=hT[:, k, :], in0=ps[pb][:, :], scalar1=sc, scalar2=None, op0=ALU.mult),
                            reads=[P(pb)] + extra_reads, writes=[("hT", k)])

        def postnorm_residual(buf, b, banks, s, am, store_rows=None):
            X = xt[buf]
            r = ring("ssy", 4)
            for h in range(2):
                S.op("act", lambda e, h=h: e.activation(out=junk[:, 0:512], in_=ps[banks[h]][:, :], func=AF.Square,
                                                        accum_out=ssy[r][:, h:h + 1]),
                     reads=[P(banks[h])], writes=["junk", ("ssy", r, h)])
            for h in range(2):
                S.op("dve", lambda e, h=h: e.tensor_tensor(
                    out=nb[:, b, h * 512:(h + 1) * 512], in0=ps[banks[h]][:, :],
                    in1=ggt[:, s, am, h * 512:(h + 1) * 512], op=ALU.mult),
                    reads=[P(banks[h])] + GGT_RES, writes=nbhalf(b, h))
            S.op("dve", lambda e: e.tensor_tensor(out=ssy[r][:, 2:3], in0=ssy[r][:, 0:1], in1=ssy[r][:, 1:2], op=ALU.add),
                 reads=[("ssy", r, 0), ("ssy", r, 1)], writes=[("ssy", r, 2)])
            S.op("dve", lambda e: e.tensor_scalar(out=ssy[r][:, 2:3], in0=ssy[r][:, 2:3], scalar1=1.0 / D, scalar2=EPS,
                                                  op0=ALU.mult, op1=ALU.add),
                 reads=[("ssy", r, 2)], writes=[("ssy", r, 2)])
            S.op("pool", lambda e: e.tensor_tensor(out=ssy[r][:, 3:4], in0=ssy[r][:, 2:3], in1=mhalf[:, 0:1], op=ALU.pow),
                 reads=[("ssy", r, 2), "mhalf"], writes=[("ssy", r, 3)])
            S.op("dve", lambda e: e.scalar_tensor_tensor(out=X[:, b, :], in0=nb[:, b, :], scalar=ssy[r][:, 3:4],
                                                         in1=X[:, b, :], op0=ALU.mult, op1=ALU.add),
                 reads=nbres(b) + [("xt", buf, b), ("ssy", r, 3)], writes=[("xt", buf, b)])
            if store_rows is not None:
                t = store_rows
                S.op("pool", lambda e: e.dma_start(out=out_v[t][:, b, :], in_=X[:, b, :]), reads=[("xt", buf, b)],
                     writes=[("out", t, b)], chan=("os", buf))
                stores_done.append(("out", t, b))

        NT = nt
        stores_done = []
        if stop != "prologue":
            load_x(0)
        for t in range(NT if stop != "prologue" else 0):
            s, tau = t // 4, t % 4
            buf = t % 2
            if t + 1 < NT:
                load_x(t + 1)
            if t == 0:
                prenorm_stats(buf)
                prenorm_tr(s, 0, 1)
            if t == 0:
                dbg("hT", hT[:].rearrange("p a b -> p (a b)"), [128, 4096], BF16, [("hT", k) for k in range(8)])

            if stop == "A":
                break
            for n in range(5):
                ncol = 512 if n < 4 else 256
                sl = load_piece(n, 8 * ncol)
                wv = ws[sl][:, 0:8 * ncol].rearrange("p (k c) -> p k c", k=8)
                for b in range(4):
                    pb = 2 + (n * 4 + b) % 4
                    blk = tau * 4 + b

                    def mmf(e, b=b, pb=pb, wv=wv, ncol=ncol):
                        for k in range(8):
                            ins = e.matmul(ps[pb][:, 0:ncol], lhsT=hT[:, k, b * 128:(b + 1) * 128], rhs=wv[:, k, :],
                                           start=(k == 0), stop=(k == 7))
                        return ins
                    S.op("pe", mmf, reads=[("ws", sl)] + [("hT", k) for k in range(8)], writes=[P(pb)])
                    if n == 3:
                        S.op("act", lambda e, pb=pb, blk=blk: e.activation(
                            out=vB[:, blk, :, 0:64], in_=ps[pb][:, :].rearrange("p (h d) -> p h d", h=8), func=AF.Copy),
                            reads=[P(pb)], writes=[("vB", blk)])
                        continue
                    nh = 8 if n < 3 else 2
                    col0 = {0: 0, 1: 512, 2: 1024, 4: 1536}[n]
                    pv = ps[pb][:, 0:nh * 64].rearrange("p (h d) -> p h d", h=nh)
                    dst = qk[:, b, col0:col0 + nh * 64].rearrange("p (h d) -> p h d", h=nh)
                    ri = ring("rop", 4)
                    tA = ropA[ri][:, 0:nh * 16].rearrange("p (h d) -> p h d", h=nh)
                    tB = ropB[ri][:, 0:nh * 16].rearrange("p (h d) -> p h d", h=nh)
                    cidx = s * 16 + blk
                    S.op("dve", lambda e, pv=pv, tA=tA, nh=nh, cidx=cidx: e.tensor_tensor(
                        out=tA, in0=pv[:, :, 0:16], in1=CC[:, cidx, :].unsqueeze(1).to_broadcast([128, nh, 16]),
                        op=ALU.mult), reads=[P(pb)] + ROPE_RES, writes=[("ropA", ri)])
                    S.op("dve", lambda e, pv=pv, tB=tB, nh=nh, cidx=cidx: e.tensor_tensor(
                        out=tB[:, :, 0:8], in0=pv[:, :, 8:16], in1=SS[:, cidx, 0:8].unsqueeze(1).to_broadcast([128, nh, 8]),
                        op=ALU.mult), reads=[P(pb)] + ROPE_RES, writes=[("ropB", ri, 0)])
                    S.op("dve", lambda e, pv=pv, tB=tB, nh=nh, cidx=cidx: e.tensor_tensor(
                        out=tB[:, :, 8:16], in0=pv[:, :, 0:8], in1=SS[:, cidx, 8:16].unsqueeze(1).to_broadcast([128, nh, 8]),
                        op=ALU.mult), reads=[P(pb)] + ROPE_RES, writes=[("ropB", ri, 1)])
                    S.op("pool", lambda e, dst=dst, tA=tA, tB=tB: e.tensor_tensor(out=dst[:, :, 0:16], in0=tA, in1=tB,
                                                                                  op=ALU.add),
                         reads=[("ropA", ri), ("ropB", ri, 0), ("ropB", ri, 1)], writes=[("qkr", b, n)])
                    S.op("act", lambda e, dst=dst, pv=pv: e.activation(out=dst[:, :, 16:64], in_=pv[:, :, 16:64],
                                                                       func=AF.Copy),
                         reads=[P(pb)], writes=[("qkc", b, n)])
                    if n == 4:
                        S.op("act", lambda e, pb=pb, blk=blk: e.activation(
                            out=vA[:, blk, :, 0:64], in_=ps[pb][:, 128:256].rearrange("p (h d) -> p h d", h=2),
                            func=AF.Copy), reads=[P(pb)], writes=[("vA", blk)])
            for cc in range(13):
                pb = cc % 2

                def trf(e, cc=cc, pb=pb):
                    for b in range(4):
                        ins = e.transpose(out=psb[pb][:, b * 128:(b + 1) * 128], in_=qk[:, b, cc * 128:(cc + 1) * 128],
                                          identity=identb[:])
                    return ins
                nn = 4 if cc == 12 else cc // 4
                S.op("pe", trf, reads=[(kind, b, nn) for b in range(4) for kind in ("qkr", "qkc")] + ["identb"],
                     writes=[P(pb)])
                if cc < 8:
                    dstT, wres = qT[:, cc, :], [("qT", cc)]
                elif cc < 12:
                    dstT, wres = kBT[:, cc - 8, tau * 512:(tau + 1) * 512], [("kBT", cc - 8, tau)]
                else:
                    dstT, wres = kAT[:, tau * 512:(tau + 1) * 512], [("kAT", tau)]
                if cc % 2 == 0:
                    S.op("act", lambda e, pb=pb, dstT=dstT: e.activation(out=dstT, in_=psb[pb][:, 0:512], func=AF.Copy),
                         reads=[P(pb)], writes=wres)
                else:
                    S.op("dve", lambda e, pb=pb, dstT=dstT: e.tensor_copy(out=dstT, in_=psb[pb][:, 0:512]),
                         reads=[P(pb)], writes=wres)
            if t == 0:
                dbg("qT", qT[:].rearrange("p a b -> p (a b)"), [128, 4096], BF16, [("qT", k) for k in range(8)])
                dbg("kAT", kAT[:, 0:512], [128, 512], BF16, [("kAT", 0)])
                dbg("vB", vB[:, 0:4, :, :].rearrange("p a b c -> p (a b c)"), [128, 4 * 8 * 65], BF16,
                    [("vB", i) for i in range(4)])

            if stop == "B":
                break
            tiles = []
            for g in range(2):
                for qb in range(4):
                    i = tau * 4 + qb
                    ob = 3 + ring("oA", 2)
                    js = [j for j in (i - 1, i) if j >= 0]
                    for jn, j in enumerate(js):
                        tiles.append(dict(kind="A", g=g, qb=qb, j=j, N=512, mask=(0 if j == i else 1), ob=ob,
                                          first=(jn == 0), last=(jn == len(js) - 1)))
            for h in range(8):
                ob = 5 + ring("oB", 2)
                nj = tau * 4 + 4
                for j in range(nj):
                    m = tau * 4 - j
                    q0 = max(0, -m) * 128
                    tiles.append(dict(kind="B", h=h, j=j, N=512 - q0, q0=q0, mi=(m + 3 if m <= 4 else 8), ob=ob,
                                      first=(j == 0), last=(j == nj - 1)))

            def issue_score(T):
                sb_ = (0, 1, 2, 7)[ring("sbank", 4)]
                pr = ring("PT", 4)
                T["pr"] = pr
                N = T["N"]
                if T["kind"] == "A":
                    g, qb, j = T["g"], T["qb"], T["j"]
                    S.op("pe", lambda e: e.matmul(
                        ps[sb_][:, 0:512].rearrange("p (c q) -> p c q", c=4),
                        lhsT=kAT[g * 64:(g + 1) * 64, j * 128:(j + 1) * 128],
                        rhs=qT[g * 64:(g + 1) * 64, 0:4, qb * 128:(qb + 1) * 128], start=True, stop=True),
                        reads=[("kAT", j // 4)] + [("qT", c) for c in range(4)], writes=[P(sb_)])
                else:
                    h, j, q0 = T["h"], T["j"], T["q0"]
                    hp, hc = (h % 2) * 64, h // 2
                    S.op("pe", lambda e: e.matmul(
                        ps[sb_][:, 0:N], lhsT=kBT[hp:hp + 64, hc, j * 128:(j + 1) * 128],
                        rhs=qT[hp:hp + 64, 4 + hc, q0:512], start=True, stop=True),
                        reads=[("kBT", hc, j // 4), ("qT", 4 + hc)], writes=[P(sb_)])
                S.op("act", lambda e: e.activation(out=PT[pr][:, 0:N], in_=ps[sb_][:, 0:N], func=AF.Exp, scale=0.125),
                     reads=[P(sb_)], writes=[("PT", pr)])
                if T["kind"] == "A":
                    S.op("dve", lambda e: e.tensor_tensor(
                        out=PT[pr][:, :].rearrange("p (c q) -> p c q", c=4),
                        in0=PT[pr][:, :].rearrange("p (c q) -> p c q", c=4),
                        in1=maskA[:, T["mask"], :].unsqueeze(1).to_broadcast([128, 4, 128]), op=ALU.mult),
                        reads=[("PT", pr), "maskA"], writes=[("PT", pr)])
                else:
                    S.op("dve", lambda e: e.tensor_tensor(out=PT[pr][:, 0:N], in0=PT[pr][:, 0:N],
                                                          in1=maskB[:, T["mi"], T["q0"]:512], op=ALU.mult),
                         reads=[("PT", pr), "maskB"], writes=[("PT", pr)])

            def issue_pv(T):
                pr, ob = T["pr"], T["ob"]
                if T["kind"] == "A":
                    g, qb, j = T["g"], T["qb"], T["j"]

                    def pvf(e):
                        for c in range(4):
                            ins = e.matmul(ps[ob][:, c * 65:(c + 1) * 65], lhsT=PT[pr][:, c * 128:(c + 1) * 128],
                                           rhs=vA[:, j, g, :], start=(T["first"] and c == 0), stop=(T["last"] and c == 3),
                                           skip_group_check=True)
                        return ins
                    S.op("pe", pvf, reads=[("PT", pr), ("vA", j)], writes=[P(ob)])
                    if T["last"]:
                        di = ring("den", 4)
                        psv = ps[ob][:, 0:260].rearrange("p (c e) -> p c e", c=4)
                        S.op("dve", lambda e: e.tensor_tensor(out=den[di][:, :].unsqueeze(2), in0=psv[:, :, 64:65],
                                                              in1=esink[:, 4 * g:4 * g + 4].unsqueeze(2), op=ALU.add),
                             reads=[P(ob), "esink"], writes=[("den", di)])
                        S.op("dve", lambda e: e.reciprocal(out=den[di][:, :], in_=den[di][:, :]), reads=[("den", di)],
                             writes=[("den", di)])
                        S.op("dve", lambda e: e.tensor_tensor(
                            out=nb[:, qb, g * 256:(g + 1) * 256].rearrange("p (c d) -> p c d", c=4),
                            in0=psv[:, :, 0:64], in1=den[di][:, :].unsqueeze(2).to_broadcast([128, 4, 64]), op=ALU.mult),
                            reads=[P(ob), ("den", di)], writes=[("nb", qb, g)])
                else:
                    h, j, q0 = T["h"], T["j"], T["q0"]
                    qb0 = q0 // 128

                    def pvf(e):
                        for qb in range(qb0, 4):
                            ins = e.matmul(ps[ob][:, qb * 65:(qb + 1) * 65],
                                           lhsT=PT[pr][:, qb * 128 - q0:(qb + 1) * 128 - q0], rhs=vB[:, j, h, :],
                                           start=(T["first"] and qb == qb0), stop=(T["last"] and qb == 3),
                                           skip_group_check=True)
                        return ins
                    S.op("pe", pvf, reads=[("PT", pr), ("vB", j)], writes=[P(ob)])
                    if T["last"]:
                        di = ring("den", 4)
                        psv = ps[ob][:, 0:260].rearrange("p (c e) -> p c e", c=4)
                        S.op("dve", lambda e: e.reciprocal(out=den[di][:, :].unsqueeze(2), in_=psv[:, :, 64:65]),
                             reads=[P(ob)], writes=[("den", di)])
                        S.op("dve", lambda e: e.tensor_tensor(
                            out=nb[:, :, 512 + h * 64:512 + (h + 1) * 64], in0=psv[:, :, 0:64],
                            in1=den[di][:, :].unsqueeze(2).to_broadcast([128, 4, 64]), op=ALU.mult),
                            reads=[P(ob), ("den", di)], writes=[("nb", b, 2 + h // 4) for b in range(4)])

            def phaseD_half(mx):
                for b in range(4):
                    S.op("act", lambda e, b=b: e.activation(
                        out=junk[:, 0:512], in_=nb[:, b, mx * 512:(mx + 1) * 512], func=AF.Square,
                        accum_out=st_ss[:, 4 * mx + b:4 * mx + b + 1]), reads=nbhalf(b, mx),
                        writes=["junk", ("ss", 4 * mx + b)])
                S.op("dve", lambda e: e.tensor_scalar(out=st_v[:, 4 * mx:4 * mx + 4], in0=st_ss[:, 4 * mx:4 * mx + 4],
                                                      scalar1=1.0 / 512, scalar2=EPS, op0=ALU.mult, op1=ALU.add),
                     reads=[("ss", 4 * mx + i) for i in range(4)], writes=[("stv", 4 * mx + i) for i in range(4)])
                S.op("pool", lambda e: e.tensor_tensor(out=st_r[:, 4 * mx:4 * mx + 4], in0=st_v[:, 4 * mx:4 * mx + 4],
                                                       in1=mhalf[:, 0:4], op=ALU.pow),
                     reads=[("stv", 4 * mx + i) for i in range(4)] + ["mhalf"],
                     writes=[("str", 4 * mx + i) for i in range(4)])
                for b in range(4):
                    S.op("dve", lambda e, b=b: e.tensor_scalar(
                        out=nb[:, b, mx * 512:(mx + 1) * 512], in0=nb[:, b, mx * 512:(mx + 1) * 512],
                        scalar1=st_r[:, 4 * mx + b:4 * mx + b + 1], scalar2=None, op0=ALU.mult),
                        reads=nbhalf(b, mx) + [("str", 4 * mx + b)], writes=nbhalf(b, mx))

            LOOK = 3
            n_a = sum(1 for T in tiles if T["kind"] == "A")
            for n in range(len(tiles) + LOOK):
                if n < len(tiles):
                    issue_score(tiles[n])
                if n >= LOOK:
                    issue_pv(tiles[n - LOOK])
                    if n - LOOK == n_a - 1:
                        phaseD_half(0)
            if t == 0:
                dbg("mix", nb.rearrange("p a b -> p (a b)"), [128, 4096], F32, [r for b in range(4) for r in nbres(b)])

            if stop == "C":
                break
            phaseD_half(1)
            transposes_to_hT(lambda k: (gT[:, 2, k:k + 1], None), ["gT2"], fine=True)

            sls = [load_piece(5 + h) for h in range(2)]
            for b in range(4):
                banks = (2 + 2 * (b % 3), 3 + 2 * (b % 3))
                for h in range(2):
                    wv = ws[sls[h]][:, :].rearrange("p (k c) -> p k c", k=8)

                    def mmf(e, b=b, h=h, wv=wv, banks=banks):
                        for k in range(8):
                            ins = e.matmul(ps[banks[h]][:, :], lhsT=hT[:, k, b * 128:(b + 1) * 128], rhs=wv[:, k, :],
                                           start=(k == 0), stop=(k == 7))
                        return ins
                    S.op("pe", mmf, reads=[("ws", sls[h])] + [("hT", k) for k in range(8)], writes=[P(banks[h])])
                postnorm_residual(buf, b, banks, s, 0)
            if t == 0:
                dbg("x1", xt[buf][:].rearrange("p a b -> p (a b)"), [128, 4096], F32, [("xt", buf, b) for b in range(4)])

            if stop == "E":
                break
            prenorm_stats(buf)
            prenorm_tr(s, 2, 3)
            hoist = (t + 1 < NT) and stop is None

            for g8 in range(8):
                if g8 == 1 and hoist:
                    prenorm_stats((t + 1) % 2)
                sl = load_piece(7 + g8)
                wv = ws[sl][:, :].rearrange("p (k c) -> p k c", k=8)
                for fc in range(4):
                    f = g8 * 4 + fc
                    pb = f % 4

                    def mmf(e, fc=fc, pb=pb, wv=wv):
                        for k in range(8):
                            ins = e.matmul(ps[pb][:, :], lhsT=wv[:, k, fc * 128:(fc + 1) * 128], rhs=hT[:, k, :],
                                           start=(k == 0), stop=(k == 7))
                        return ins
                    S.op("pe", mmf, reads=[("ws", sl)] + [("hT", k) for k in range(8)], writes=[P(pb)])
                    ri = f % 2
                    S.op("act", lambda e, pb=pb, ri=ri: e.activation(out=rtmp[ri][:], in_=ps[pb][:, :], func=AF.Relu),
                         reads=[P(pb)], writes=[("rtmp", ri)])
                    S.op("pool", lambda e, f=f, ri=ri: e.tensor_tensor(out=uT[:, f, :], in0=rtmp[ri][:], in1=rtmp[ri][:],
                                                                       op=ALU.mult),
                         reads=[("rtmp", ri)], writes=[("uT", f)])

            if stop == "G":
                break
            if hoist:
                prenorm_tr((t + 1) // 4, 0, 1, banks=(4, 5))
            for g8 in range(8):
                sl = load_piece(15 + g8)
                wv = ws[sl][:, :].rearrange("p (f d) -> p f d", f=4)
                for fc in range(4):
                    f = g8 * 4 + fc

                    def mmf(e, f=f, fc=fc, wv=wv):
                        for b in range(4):
                            for h in range(2):
                                ins = e.matmul(ps[2 * b + h][:, :], lhsT=uT[:, f, b * 128:(b + 1) * 128],
                                               rhs=wv[:, fc, h * 512:(h + 1) * 512], start=(f == 0), stop=(f == 31))
                        return ins
                    S.op("pe", mmf, reads=[("ws", sl), ("uT", f)], writes=[P(i) for i in range(8)])

            for b in range(4):
                postnorm_residual(buf, b, (2 * b, 2 * b + 1), s, 1, store_rows=t)
            if stop == "T0":
                break

        S.op("pool", None, reads=stores_done + [("dbg", n) for n in dbg_outs])
        S.emit()
    return nc, list(dbg_outs.keys())


def _consts():
    bf = ml_dtypes.bfloat16
    k = np.arange(128)[:, None]
    qi = np.arange(512)[None, :]
    mB = np.zeros((128, 9, 512), np.float32)
    for mi in range(9):
        m = mi - 3 if mi < 8 else 5
        d = 128 * m + qi - k
        mult = ((d >= 0) & (d <= 128)).astype(np.float32) \
            + ((d >= 0) & (d % 4 == 0) & (d <= 512)).astype(np.float32) \
            + ((d >= 0) & (d % 16 == 0) & (d <= 2048)).astype(np.float32)
        mB[:, mi, :] = mult
    q = np.arange(128)[None, :]
    mA = np.zeros((128, 2, 128), np.float32)
    mA[:, 0, :] = (k <= q)
    mA[:, 1, :] = (k > q)
    sel = np.zeros((2, 2, 128), np.float32)
    sel[0, 0, :] = 1.0
    sel[1, 1, :] = 1.0
    invf = (500000.0 ** (-np.arange(0, 16, 2, dtype=np.float32) / 16.0)).astype(np.float32)[None, :]
    return dict(ident=np.eye(128, dtype=np.float32), maskB=mB.reshape(128, -1).astype(bf),
                maskA=mA.reshape(128, -1).astype(bf), sel=sel.reshape(2, -1), invf=invf)


_QA_PERM = [0, 4, 1, 5, 2, 6, 3, 7]


def _prep_shared(w_ada, b_ada, g_attn_pre, g_attn_post, w_in, sink_a, g_mix_a, g_mix_b, w_out, g_mlp_pre,
                 g_mlp_post, w_up, w_down):
    w_in0 = w_in[0]
    qa = w_in0[:, 0:512].reshape(D, 8, 64)[:, _QA_PERM, :].reshape(D, 512)
    ka, va = w_in0[:, 512:640], w_in0[:, 640:768]
    qb, kb, vb = w_in0[:, 768:1280], w_in0[:, 1280:1792], w_in0[:, 1792:2304]
    w_in_p = np.ascontiguousarray(np.concatenate([qa, qb, kb, vb, ka, va], axis=1))

    def colT(v):
        return np.ascontiguousarray(v.reshape(8, 128).T)
    sh = dict(w_ada=np.ascontiguousarray(w_ada[0]), b_ada=np.ascontiguousarray(b_ada[0:1]),
              gpreT=colT(g_attn_pre[0]), gmlpT=colT(g_mlp_pre[0]),
              gmixT=colT(np.concatenate([g_mix_a[0], g_mix_b[0]])),
              gposta=np.ascontiguousarray(g_attn_post[0:1]), gpostm=np.ascontiguousarray(g_mlp_post[0:1]),
              w_in=w_in_p, w_out=np.ascontiguousarray(w_out[0]), w_up=np.ascontiguousarray(w_up[0]),
              w_down=np.ascontiguousarray(w_down[0]), sink=np.ascontiguousarray(sink_a[0:1]))
    sh.update(_consts())
    return sh


_NC_CACHE = {}


def kernel(x, c, positions, w_ada, b_ada, g_attn_pre, g_attn_post, w_in, sink_a, g_mix_a, g_mix_b, w_out,
           g_mlp_pre, g_mlp_post, w_up, w_down, _debug=None):
    x = np.asarray(x, np.float32)
    c = np.asarray(c, np.float32)
    positions = np.asarray(positions, np.int32)
    args = [np.asarray(a, np.float32) for a in (w_ada, b_ada, g_attn_pre, g_attn_post, w_in, sink_a, g_mix_a,
                                                g_mix_b, w_out, g_mlp_pre, g_mlp_post, w_up, w_down)]
    shared = _prep_shared(*args)
    key = tuple(_debug) if _debug else None
    if key not in _NC_CACHE:
        _NC_CACHE[key] = build_nc(_debug)
    nc, dbg_names = _NC_CACHE[key]
    in_maps = []
    for i in range(NCORES):
        m = dict(shared)
        m["x"] = np.ascontiguousarray(x[2 * i:2 * i + 2].reshape(2 * SEQ, D))
        cc = c[2 * i:2 * i + 2]
        m["cT"] = np.ascontiguousarray(cc.reshape(2, 8, 128).transpose(2, 1, 0).reshape(128, 16))
        pp = positions[2 * i:2 * i + 2]
        m["pos"] = np.ascontiguousarray(pp.reshape(2, 16, 128).transpose(2, 0, 1).reshape(128, 32))
        in_maps.append(m)
    res = run_bass_kernel_spmd(nc, in_maps, core_ids=list(range(NCORES)))
    out = np.stack([r["out"].reshape(2, SEQ, D) for r in res.results], axis=0).reshape(2 * NCORES, SEQ, D)
    if _debug:
        return out.astype(np.float32), [{n: r["dbg_" + n] for n in dbg_names} for r in res.results]
    return out.astype(np.float32)
```

```python
import contextlib
import numpy as np
import ml_dtypes
import concourse.bass as bass
import concourse.mybir as mybir
from concourse.alu_op_type import AluOpType as ALU
from concourse.bass_utils import run_bass_kernel_spmd

F32 = mybir.dt.float32
BF16 = mybir.dt.bfloat16
I32 = mybir.dt.int32
AF = mybir.ActivationFunctionType

NCORES = 8
SEQ = 2048
D = 1024
DFF = 4096
EPS = 1e-6
NW = 3
TWO_PI_HI = 6.28125
TWO_PI_LO = 2.0 * np.pi - 6.28125
PI_F = float(np.float32(np.pi))


class Sched:
    ENGS = ("pe", "act", "dve", "pool", "sp")

    def __init__(self, nc):
        self.nc = nc
        self.ops = {e: [] for e in self.ENGS}
        self.cnt = {e: 0 for e in self.ENGS}
        self.chan_cnt = {}
        self.lastw = {}
        self.readers = {}
        self.seen = {e: {} for e in self.ENGS}

    def _dep(self, eng, tok, waits):
        semkey, val = tok[0], tok[1]
        if self.seen[eng].get(semkey, 0) >= val:
            return
        self.seen[eng][semkey] = val
        waits[semkey] = max(waits.get(semkey, 0), val)

    def op(self, eng, fn, reads=(), writes=(), chan=None):
        waits = {}
        if eng != "pe":
            extra = [r for r in reads if isinstance(r, tuple) and r[0] == "ps" and r not in writes]
            if extra:
                writes = list(writes) + extra
        for r in reads:
            t = self.lastw.get(r)
            if t is not None:
                if t[0][0] == "e" and t[2] == eng and eng == "pe":
                    continue
                self._dep(eng, t, waits)
        for w in writes:
            t = self.lastw.get(w)
            if t is not None and not (t[0][0] == "e" and t[2] == eng and eng == "pe"):
                self._dep(eng, t, waits)
            for t in self.readers.get(w, ()):
                if t[0][0] == "e" and t[2] == eng and eng == "pe":
                    continue
                self._dep(eng, t, waits)
        if chan is None:
            self.cnt[eng] += 1
            tok = (("e", eng), self.cnt[eng], eng)
        else:
            self.chan_cnt[chan] = self.chan_cnt.get(chan, 0) + 1
            tok = (("c", chan), 16 * self.chan_cnt[chan], eng)
        self.ops[eng].append(dict(fn=fn, waits=waits, chan=chan))
        for r in reads:
            self.readers.setdefault(r, []).append(tok)
        for w in writes:
            self.lastw[w] = tok
            self.readers[w] = []
        return tok

    def emit(self):
        nc = self.nc
        with contextlib.ExitStack() as st:
            sems = {}
            for e in self.ENGS:
                sems[("e", e)] = st.enter_context(nc.semaphore("s_" + e))
            for i, c in enumerate(self.chan_cnt):
                sems[("c", c)] = st.enter_context(nc.semaphore("c%d" % i))
            block = st.enter_context(nc.Block())
            handles = dict(pe="tensor", act="scalar", dve="vector", pool="gpsimd", sp="sync")

            def make(e):
                def body(h):
                    for o in self.ops[e]:
                        for sk, v in o["waits"].items():
                            h.wait_ge(sems[sk], v)
                        if o["fn"] is None:
                            continue
                        ins = o["fn"](h)
                        if o["chan"] is None:
                            ins.then_inc(sems[("e", e)], 1)
                        else:
                            ins.then_inc(sems[("c", o["chan"])], 16)
                return body

            for e in self.ENGS:
                if self.ops[e]:
                    getattr(block, handles[e])(make(e))


def build_nc(debug=None, stop=None, nt=8):
    nc = bass.Bass("TRN2", target_bir_lowering=False)

    def din(name, shape, dt):
        return nc.dram_tensor(name, shape, dt, kind="ExternalInput").ap()

    x_d = din("x", [4096, D], F32)
    cT_d = din("cT", [128, 16], F32)
    pos_d = din("pos", [128, 32], I32)
    w_ada_d = din("w_ada", [D, 6 * D], F32)
    b_ada_d = din("b_ada", [1, 6 * D], F32)
    gpreT_d = din("gpreT", [128, 8], F32)
    gmlpT_d = din("gmlpT", [128, 8], F32)
    gmixT_d = din("gmixT", [128, 8], F32)
    gposta_d = din("gposta", [1, D], F32)
    gpostm_d = din("gpostm", [1, D], F32)
    w_in_d = din("w_in", [D, 2304], F32)
    w_out_d = din("w_out", [D, D], F32)
    w_up_d = din("w_up", [D, DFF], F32)
    w_down_d = din("w_down", [DFF, D], F32)
    sink_d = din("sink", [1, 8], F32)
    ident_d = din("ident", [128, 128], F32)
    maskB_d = din("maskB", [128, 9 * 512], BF16)
    maskA_d = din("maskA", [128, 256], BF16)
    sel_d = din("sel", [2, 256], F32)
    invf_d = din("invf", [1, 8], F32)
    out_d = nc.dram_tensor("out", [4096, D], F32, kind="ExternalOutput").ap()
    wscr = nc.dram_tensor("wscr", [23, 128, 4096], BF16, kind="Internal").ap()

    dbg_outs = {}
    S = Sched(nc)

    with contextlib.ExitStack() as st:
        def sb(name, shape, dt):
            return st.enter_context(nc.sbuf_tensor(name, shape, dt))

        def psum(name, shape, dt):
            return st.enter_context(nc.psum_tensor(name, shape, dt))

        xt = [sb("xt%d" % i, [128, 4, D], F32) for i in range(2)]
        big = sb("big", [128, 16384], BF16)
        uT = big[:].rearrange("p (c t) -> p c t", c=32)
        nbt = sb("nbt", [128, 4, D], F32)
        nb = nbt[:]
        qk = big[:, 8192:16384].rearrange("p (b d) -> p b d", b=4)
        stage = [big[:, 8192 * i:8192 * (i + 1)].bitcast(F32) for i in range(2)]
        junk = sb("junk", [128, D], BF16)
        hT = sb("hT", [128, 8, 512], BF16)
        qT = sb("qT", [128, 8, 512], BF16)
        kAT = sb("kAT", [128, SEQ], BF16)
        kBT = sb("kBT", [128, 4, SEQ], BF16)
        vA = sb("vA", [128, 16, 2, 65], BF16)
        vB = sb("vB", [128, 16, 8, 65], BF16)
        rtmp = [sb("rtmp%d" % i, [128, 512], F32) for i in range(2)]
        PT = [sb("PT%d" % i, [128, 512], BF16) for i in range(4)]
        ws = [sb("ws%d" % i, [128, 4096], BF16) for i in range(NW)]
        maskB = sb("maskB_s", [128, 9, 512], BF16)
        maskA = sb("maskA_s", [128, 2, 128], BF16)
        ident = sb("ident_s", [128, 128], F32)
        identb = sb("identb", [128, 128], BF16)
        ggt = sb("ggt", [128, 2, 2, D], F32)
        modT = sb("modT", [128, 2, 4, 8], F32)
        gT = sb("gT", [128, 3, 8], F32)
        CC = sb("CC", [128, 32, 16], F32)
        SS = sb("SS", [128, 32, 16], F32)
        esink = sb("esink", [128, 8], F32)
        mhalf = sb("mhalf", [128, 8], F32)
        zb = sb("zb", [128, 1], F32)
        cT = sb("cT_s", [128, 16], F32)
        cond = sb("cond", [128, 16], F32)
        ctmp = sb("ctmp", [128, 16], F32)
        posi = sb("posi", [128, 32], I32)
        posf = sb("posf", [128, 32], F32)
        invf = sb("invf_s", [128, 8], F32)
        sel = sb("sel_s", [2, 2, 128], F32)
        st_ss = sb("st_ss", [128, 8], F32)
        st_v = sb("st_v", [128, 8], F32)
        st_r = sb("st_r", [128, 8], F32)
        ssy = [sb("ssy%d" % i, [128, 4], F32) for i in range(4)]
        den = [sb("den%d" % i, [128, 4], F32) for i in range(4)]
        ropA = [sb("ropA%d" % i, [128, 128], F32) for i in range(4)]
        ropB = [sb("ropB%d" % i, [128, 128], F32) for i in range(4)]
        rpro = qT[:, 0:4, :].rearrange("p a b -> p (a b)").bitcast(F32).rearrange("p (a b) -> p a b", a=4)
        ang = rpro[:, 0, :].rearrange("p (a b) -> p a b", b=8)
        ra = rpro[:, 1, :].rearrange("p (a b) -> p a b", b=8)
        rb = rpro[:, 2, :].rearrange("p (a b) -> p a b", b=8)
        rki = rpro[:, 3, :].bitcast(I32).rearrange("p (a b) -> p a b", b=8)

        ps = [psum("ps%d" % i, [128, 512], F32) for i in range(8)]
        psb = [p[:].bitcast(BF16) for p in ps]

        gpb = xt[1]
        modp = [rtmp[i][0:2, :] for i in range(2)]
        bada = junk[0:2, :].bitcast(F32)

        def P(i):
            return ("ps", i)

        def nbres(b):
            return [("nb", b, i) for i in range(4)]

        def nbhalf(b, mx):
            return [("nb", b, 2 * mx), ("nb", b, 2 * mx + 1)]

        def qkres(b):
            return [("uT", 16 + 4 * b + i) for i in range(4)]

        def stres(i):
            return [("uT", 16 * i + j) for j in range(16)]

        rings = {}

        def ring(name, n):
            v = rings.get(name, 0)
            rings[name] = v + 1
            return v % n

        def dbg(name, src_ap, shape, dt, reads):
            if debug is None or name not in debug:
                return
            d = nc.dram_tensor("dbg_" + name, shape, dt, kind="ExternalOutput").ap()
            dbg_outs[name] = d
            S.op("sp", lambda e: e.dma_start(out=d, in_=src_ap), reads=reads, writes=[("dbg", name)],
                 chan=("dbg", name))

        def ld(dst, src, res, ch):
            S.op("sp", lambda e: e.dma_start(out=dst, in_=src), writes=[res], chan=ch)

        ld(ident[:], ident_d, "ident", "k0")
        ld(maskB[:].rearrange("p a b -> p (a b)"), maskB_d, "maskB", "k1")
        ld(maskA[:].rearrange("p a b -> p (a b)"), maskA_d, "maskA", "k2")
        ld(sel[:].rearrange("p a b -> p (a b)"), sel_d, "sel", "k3")
        ld(invf[:], invf_d.partition_broadcast(128), "invf", "k4")
        ld(esink[:], sink_d.partition_broadcast(128), "esink0", "k5")
        ld(cT[:], cT_d, "cT", "k6")
        ld(posi[:], pos_d, "posi", "k7")
        ld(gT[:, 0, :], gpreT_d, "gT0", "k8")
        ld(gT[:, 1, :], gmlpT_d, "gT1", "k9")
        ld(gT[:, 2, :], gmixT_d, "gT2", "k10")
        ld(gpb[:, 0, :], gposta_d.partition_broadcast(128), ("xt", 1, 0), "k11")
        ld(gpb[:, 1, :], gpostm_d.partition_broadcast(128), ("xt", 1, 1), "k12")

        S.op("pool", lambda e: e.memset(mhalf[:], -0.5), writes=["mhalf"])
        S.op("pool", lambda e: e.memset(zb[:], 0.0), writes=["zb"])
        S.op("pool", lambda e: e.memset(vA[:, :, :, 64:65], 1.0), writes=[("vA", i) for i in range(16)])
        S.op("pool", lambda e: e.memset(vB[:, :, :, 64:65], 1.0), writes=[("vB", i) for i in range(16)])
        def piece_src(q):
            if q < 5:
                w = 512 if q < 4 else 256
                return w_in_d.rearrange("(k p) c -> p k c", p=128)[:, :, q * 512:q * 512 + w], (8, w)
            if q < 7:
                h = q - 5
                return w_out_d.rearrange("(k p) c -> p k c", p=128)[:, :, h * 512:(h + 1) * 512], (8, 512)
            if q < 15:
                g = q - 7
                return w_up_d.rearrange("(k p) c -> p k c", p=128)[:, :, g * 512:(g + 1) * 512], (8, 512)
            g = q - 15
            return w_down_d[g * 512:(g + 1) * 512, :].rearrange("(f p) d -> p f d", p=128), (4, D)

        for q in range(23):
            src, (a, bdim) = piece_src(q)
            dstv = wscr[q, :, 0:a * bdim].rearrange("p (a b) -> p a b", a=a)
            S.op("pool", lambda e, src=src, dstv=dstv: e.dma_start(out=dstv, in_=src), writes=[("wscr", q)],
                 chan=("cv", q))

        S.op("dve", lambda e: e.tensor_copy(out=identb[:], in_=ident[:]), reads=["ident"], writes=["identb"])
        S.op("act", lambda e: e.activation(out=esink[:], in_=esink[:], func=AF.Exp), reads=["esink0"],
             writes=["esink"])

        S.op("act", lambda e: e.activation(out=ctmp[:], in_=cT[:], func=AF.Exp, scale=-1.0), reads=["cT"],
             writes=["ctmp"])
        S.op("dve", lambda e: e.tensor_scalar(out=ctmp[:], in0=ctmp[:], scalar1=1.0, scalar2=None, op0=ALU.add),
             reads=["ctmp"], writes=["ctmp"])
        S.op("dve", lambda e: e.reciprocal(out=ctmp[:], in_=ctmp[:]), reads=["ctmp"], writes=["ctmp"])
        S.op("dve", lambda e: e.tensor_tensor(out=cond[:], in0=cT[:], in1=ctmp[:], op=ALU.mult),
             reads=["ctmp", "cT"], writes=["cond"])
        condv = cond[:].rearrange("p (k b) -> p k b", b=2)

        S.op("dve", lambda e: e.tensor_copy(out=posf[:], in_=posi[:]), reads=["posi"], writes=["posf"])
        S.op("dve", lambda e: e.tensor_tensor(out=ang, in0=posf[:].unsqueeze(2).to_broadcast([128, 32, 8]),
                                              in1=invf[:].unsqueeze(1).to_broadcast([128, 32, 8]), op=ALU.mult),
             reads=["posf", "invf"], writes=[("qT", 0)])
        S.op("dve", lambda e: e.tensor_scalar(out=ra, in0=ang, scalar1=float(1.0 / (2 * np.pi)), scalar2=None,
                                              op0=ALU.mult), reads=[("qT", 0)], writes=[("qT", 1)])
        S.op("dve", lambda e: e.tensor_copy(out=rki, in_=ra), reads=[("qT", 1)], writes=[("qT", 3)])
        S.op("dve", lambda e: e.tensor_copy(out=ra, in_=rki), reads=[("qT", 3)], writes=[("qT", 1)])
        S.op("dve", lambda e: e.scalar_tensor_tensor(out=rb, in0=ra, scalar=-TWO_PI_HI, in1=ang,
                                                     op0=ALU.mult, op1=ALU.add), reads=[("qT", 1), ("qT", 0)], writes=[("qT", 2)])
        S.op("dve", lambda e: e.scalar_tensor_tensor(out=rb, in0=ra, scalar=-TWO_PI_LO, in1=rb,
                                                     op0=ALU.mult, op1=ALU.add), reads=[("qT", 1), ("qT", 2)], writes=[("qT", 2)])

        def wrap(buf, tmp, name, tname):
            S.op("dve", lambda e: e.tensor_scalar(out=tmp, in0=buf, scalar1=PI_F, scalar2=-2.0 * np.pi,
                                                  op0=ALU.is_gt, op1=ALU.mult), reads=[name], writes=[tname])
            S.op("dve", lambda e: e.tensor_tensor(out=buf, in0=buf, in1=tmp, op=ALU.add),
                 reads=[name, tname], writes=[name])
            S.op("dve", lambda e: e.tensor_scalar(out=tmp, in0=buf, scalar1=-PI_F, scalar2=2.0 * np.pi,
                                                  op0=ALU.is_lt, op1=ALU.mult), reads=[name], writes=[tname])
            S.op("dve", lambda e: e.tensor_tensor(out=buf, in0=buf, in1=tmp, op=ALU.add),
                 reads=[name, tname], writes=[name])

        wrap(rb, ra, ("qT", 2), ("qT", 1))
        S.op("act", lambda e: e.activation(out=SS[:, :, 8:16], in_=rb, func=AF.Sin), reads=[("qT", 2)], writes=["SSb"])
        S.op("dve", lambda e: e.tensor_scalar(out=SS[:, :, 0:8], in0=SS[:, :, 8:16], scalar1=-1.0, scalar2=None,
                                              op0=ALU.mult), reads=["SSb"], writes=["SSa"])
        S.op("dve", lambda e: e.tensor_scalar(out=rb, in0=rb, scalar1=float(np.pi / 2), scalar2=None,
                                              op0=ALU.add), reads=[("qT", 2), "SSb"], writes=[("qT", 2)])
        wrap(rb, ra, ("qT", 2), ("qT", 1))
        S.op("act", lambda e: e.activation(out=CC[:, :, 0:8], in_=rb, func=AF.Sin), reads=[("qT", 2)], writes=["CCa"])
        S.op("dve", lambda e: e.tensor_copy(out=CC[:, :, 8:16], in_=CC[:, :, 0:8]), reads=["CCa"], writes=["CCb"])
        ROPE_RES = ["CCa", "CCb", "SSa", "SSb"]

        w_ada_v = w_ada_d.rearrange("(k p) c -> p k c", p=128)
        for n in range(12):
            si = n % 2
            stg = stage[si].rearrange("p (k c) -> p k c", k=8)
            S.op("sp", lambda e, stg=stg, n=n: e.dma_start(out=stg, in_=w_ada_v[:, :, n * 512:(n + 1) * 512]),
                 writes=stres(si), chan=("stg", si))
            S.op("sp", lambda e, n=n: e.dma_start(out=bada, in_=b_ada_d[:, n * 512:(n + 1) * 512].partition_broadcast(2)),
                 writes=["junk"], chan="bada")
            pm = 2 + (n % 2)

            def mmf(e, stg=stg, pm=pm):
                for k in range(8):
                    ins = e.matmul(ps[pm][0:2, :], lhsT=condv[:, k, :], rhs=stg[:, k, :], start=(k == 0), stop=(k == 7))
                return ins
            S.op("pe", mmf, reads=stres(si) + ["cond"], writes=[P(pm)])
            mp = modp[n % 2]
            S.op("dve", lambda e, mp=mp, pm=pm: e.tensor_tensor(out=mp, in0=ps[pm][0:2, :], in1=bada, op=ALU.add),
                 reads=[P(pm), "junk"], writes=[("rtmp", n % 2)])
            v, half = n // 2, n % 2
            if v in (2, 5):
                am = 0 if v == 2 else 1
                for s in range(2):
                    pg = 4 + s
                    S.op("pe", lambda e, mp=mp, s=s, pg=pg: e.matmul(ps[pg][:, :], lhsT=sel[:, s, :], rhs=mp,
                                                                       start=True, stop=True),
                         reads=[("rtmp", n % 2), "sel"], writes=[P(pg)])
                    S.op("dve", lambda e, s=s, pg=pg, am=am, half=half: e.tensor_tensor(
                        out=ggt[:, s, am, half * 512:(half + 1) * 512], in0=ps[pg][:, :],
                        in1=gpb[:, am, half * 512:(half + 1) * 512], op=ALU.mult),
                        reads=[P(pg), ("xt", 1, 0), ("xt", 1, 1)], writes=[("ggt", s, am, half)])
            else:
                vi = {0: 0, 1: 1, 3: 2, 4: 3}[v]
                pg = 6

                def trf(e, mp=mp):
                    for q in range(4):
                        ins = e.transpose(out=ps[pg][:, 2 * q:2 * q + 2], in_=mp[:, q * 128:(q + 1) * 128],
                                          identity=ident[0:2, 0:2])
                    return ins
                S.op("pe", trf, reads=[("rtmp", n % 2), "ident"], writes=[P(pg)])
                S.op("dve", lambda e, vi=vi, half=half: e.tensor_copy(
                    out=modT[:, :, vi, half * 4:half * 4 + 4],
                    in_=ps[pg][:, 0:8].rearrange("p (q s) -> p s q", s=2)),
                    reads=[P(pg)], writes=[("modT", vi, half)])
        for vi, gi in ((1, 0), (3, 1)):
            S.op("dve", lambda e, vi=vi: e.tensor_scalar(out=modT[:, :, vi, :], in0=modT[:, :, vi, :], scalar1=1.0,
                                                         scalar2=None, op0=ALU.add),
                 reads=[("modT", vi, 0), ("modT", vi, 1)], writes=[("modT", vi, 0), ("modT", vi, 1)])
            S.op("dve", lambda e, vi=vi, gi=gi: e.tensor_tensor(
                out=modT[:, :, vi, :], in0=modT[:, :, vi, :], in1=gT[:, gi, :].unsqueeze(1).to_broadcast([128, 2, 8]),
                op=ALU.mult), reads=[("modT", vi, 0), ("modT", vi, 1), "gT%d" % gi],
                writes=[("modT", vi, 0), ("modT", vi, 1)])
        MODT_RES = [("modT", vi, h) for vi in range(4) for h in range(2)]
        GGT_RES = [("ggt", s, am, h) for s in range(2) for am in range(2) for h in range(2)]

        wstate = {"n": 0}

        def load_piece(q, n_el=4096):
            sl = wstate["n"] % NW
            wstate["n"] += 1
            S.op("sp", lambda e: e.dma_start(out=ws[sl][:, 0:n_el], in_=wscr[q, :, 0:n_el]), reads=[("wscr", q)],
                 writes=[("ws", sl)], chan=("wl", sl))
            return sl

        x_v = x_d.rearrange("(t b p) d -> t p b d", p=128, b=4)
        out_v = out_d.rearrange("(t b p) d -> t p b d", p=128, b=4)

        def load_x(t):
            buf = t % 2
            S.op("sp", lambda e: e.dma_start(out=xt[buf][:], in_=x_v[t]), writes=[("xt", buf, b) for b in range(4)],
                 chan=("xl", buf))

        def prenorm_stats(buf):
            X = xt[buf]
            for b in range(4):
                S.op("act", lambda e, b=b: e.activation(out=junk[:], in_=X[:, b, :], func=AF.Square,
                                                        accum_out=st_ss[:, b:b + 1]),
                     reads=[("xt", buf, b)], writes=["junk", ("ss", b)])
                S.op("dve", lambda e, b=b: e.tensor_scalar(out=st_v[:, b:b + 1], in0=st_ss[:, b:b + 1], scalar1=1.0 / D,
                                                           scalar2=EPS, op0=ALU.mult, op1=ALU.add),
                     reads=[("ss", b)], writes=[("stv", b)])
                S.op("pool", lambda e, b=b: e.tensor_tensor(out=st_r[:, b:b + 1], in0=st_v[:, b:b + 1],
                                                            in1=mhalf[:, 0:1], op=ALU.pow),
                     reads=[("stv", b), "mhalf"], writes=[("str", b)])
                if b % 2 == 0:
                    S.op("dve", lambda e, b=b: e.tensor_scalar(out=nb[:, b, :], in0=X[:, b, :], scalar1=st_r[:, b:b + 1],
                                                               scalar2=None, op0=ALU.mult),
                         reads=[("xt", buf, b), ("str", b)], writes=nbres(b))
                else:
                    S.op("act", lambda e, b=b: e.activation(out=nb[:, b, :], in_=X[:, b, :], func=AF.Identity,
                                                            scale=st_r[:, b:b + 1], bias=zb[:]),
                         reads=[("xt", buf, b), ("str", b), "zb"], writes=nbres(b))

        def prenorm_tr(s, vi_sh, vi_gsc, banks=(0, 1)):
            transposes_to_hT(lambda k: (modT[:, s, vi_gsc, k:k + 1], modT[:, s, vi_sh, k:k + 1]), MODT_RES, banks=banks)

        def transposes_to_hT(scale_bias, extra_reads, fine=False, banks=(0, 1)):
            for k in range(8):
                pb = banks[k % 2]

                def trf(e, k=k, pb=pb):
                    for b in range(4):
                        ins = e.transpose(out=ps[pb][:, b * 128:(b + 1) * 128], in_=nb[:, b, k * 128:(k + 1) * 128],
                                          identity=ident[:])
                    return ins
                if fine:
                    rr = [("nb", b, k // 2) for b in range(4)]
                else:
                    rr = [r for b in range(4) for r in nbres(b)]
                S.op("pe", trf, reads=rr + ["ident"], writes=[P(pb)])
                sc, bi = scale_bias(k)
                if k % 2 == 0:
                    S.op("act", lambda e, k=k, pb=pb, sc=sc, bi=bi: e.activation(
                        out=hT[:, k, :], in_=ps[pb][:, :], func=AF.Identity, scale=sc, bias=(bi if bi is not None else zb[:])),
                        reads=[P(pb), "zb"] + extra_reads, writes=[("hT", k)])
                else:
                    if bi is not None:
                        S.op("dve", lambda e, k=k, pb=pb, sc=sc, bi=bi: e.tensor_scalar(
                            out=hT[:, k, :], in0=ps[pb][:, :], scalar1=sc, scalar2=bi, op0=ALU.mult, op1=ALU.add),
                            reads=[P(pb)] + extra_reads, writes=[("hT", k)])
                    else:
                        S.op("dve", lambda e, k=k, pb=pb, sc=sc: e.tensor_scalar(
                            out=hT[:, k, :], in0=ps[pb][:, :], scalar1=sc, scalar2=None, op0=ALU.mult),
                            reads=[P(pb)] + extra_reads, writes=[("hT", k)])

        def postnorm_residual(buf, b, banks, s, am, store_rows=None):
            X = xt[buf]
            r = ring("ssy", 4)
            for h in range(2):
                S.op("act", lambda e, h=h: e.activation(out=junk[:, 0:512], in_=ps[banks[h]][:, :], func=AF.Square,
                                                        accum_out=ssy[r][:, h:h + 1]),
                     reads=[P(banks[h])], writes=["junk", ("ssy", r, h)])
            for h in range(2):
                S.op("dve", lambda e, h=h: e.tensor_tensor(
                    out=nb[:, b, h * 512:(h + 1) * 512], in0=ps[banks[h]][:, :],
                    in1=ggt[:, s, am, h * 512:(h + 1) * 512], op=ALU.mult),
                    reads=[P(banks[h])] + GGT_RES, writes=nbhalf(b, h))
            S.op("dve", lambda e: e.tensor_tensor(out=ssy[r][:, 2:3], in0=ssy[r][:, 0:1], in1=ssy[r][:, 1:2], op=ALU.add),
                 reads=[("ssy", r, 0), ("ssy", r, 1)], writes=[("ssy", r, 2)])
            S.op("dve", lambda e: e.tensor_scalar(out=ssy[r][:, 2:3], in0=ssy[r][:, 2:3], scalar1=1.0 / D, scalar2=EPS,
                                                  op0=ALU.mult, op1=ALU.add),
                 reads=[("ssy", r, 2)], writes=[("ssy", r, 2)])
            S.op("pool", lambda e: e.tensor_tensor(out=ssy[r][:, 3:4], in0=ssy[r][:, 2:3], in1=mhalf[:, 0:1], op=ALU.pow),
                 reads=[("ssy", r, 2), "mhalf"], writes=[("ssy", r, 3)])
            S.op("dve", lambda e: e.scalar_tensor_tensor(out=X[:, b, :], in0=nb[:, b, :], scalar=ssy[r][:, 3:4],
                                                         in1=X[:, b, :], op0=ALU.mult, op1=ALU.add),
                 reads=nbres(b) + [("xt", buf, b), ("ssy", r, 3)], writes=[("xt", buf, b)])
            if store_rows is not None:
                t = store_rows
                S.op("pool", lambda e: e.dma_start(out=out_v[t][:, b, :], in_=X[:, b, :]), reads=[("xt", buf, b)],
                     writes=[("out", t, b)], chan=("os", buf))
                stores_done.append(("out", t, b))

        NT = nt
        stores_done = []
        if stop != "prologue":
            load_x(0)
        for t in range(NT if stop != "prologue" else 0):
            s, tau = t // 4, t % 4
            buf = t % 2
            if t == 0:
                prenorm_stats(buf)
                prenorm_tr(s, 0, 1)
            if t == 0:
                dbg("hT", hT[:].rearrange("p a b -> p (a b)"), [128, 4096], BF16, [("hT", k) for k in range(8)])

            if stop == "A":
                break
            for n in range(5):
                ncol = 512 if n < 4 else 256
                sl = load_piece(n, 8 * ncol)
                wv = ws[sl][:, 0:8 * ncol].rearrange("p (k c) -> p k c", k=8)
                for b in range(4):
                    pb = 2 + (n * 4 + b) % 4
                    blk = tau * 4 + b

                    def mmf(e, b=b, pb=pb, wv=wv, ncol=ncol):
                        for k in range(8):
                            ins = e.matmul(ps[pb][:, 0:ncol], lhsT=hT[:, k, b * 128:(b + 1) * 128], rhs=wv[:, k, :],
                                           start=(k == 0), stop=(k == 7))
                        return ins
                    S.op("pe", mmf, reads=[("ws", sl)] + [("hT", k) for k in range(8)], writes=[P(pb)])
                    if n == 3:
                        S.op("act", lambda e, pb=pb, blk=blk: e.activation(
                            out=vB[:, blk, :, 0:64], in_=ps[pb][:, :].rearrange("p (h d) -> p h d", h=8), func=AF.Copy),
                            reads=[P(pb)], writes=[("vB", blk)])
                        continue
                    nh = 8 if n < 3 else 2
                    col0 = {0: 0, 1: 512, 2: 1024, 4: 1536}[n]
                    pv = ps[pb][:, 0:nh * 64].rearrange("p (h d) -> p h d", h=nh)
                    dst = qk[:, b, col0:col0 + nh * 64].rearrange("p (h d) -> p h d", h=nh)
                    ri = ring("rop", 4)
                    tA = ropA[ri][:, 0:nh * 16].rearrange("p (h d) -> p h d", h=nh)
                    tB = ropB[ri][:, 0:nh * 16].rearrange("p (h d) -> p h d", h=nh)
                    cidx = s * 16 + blk
                    S.op("dve", lambda e, pv=pv, tA=tA, nh=nh, cidx=cidx: e.tensor_tensor(
                        out=tA, in0=pv[:, :, 0:16], in1=CC[:, cidx, :].unsqueeze(1).to_broadcast([128, nh, 16]),
                        op=ALU.mult), reads=[P(pb)] + ROPE_RES, writes=[("ropA", ri)])
                    S.op("dve", lambda e, pv=pv, tB=tB, nh=nh, cidx=cidx: e.tensor_tensor(
                        out=tB[:, :, 0:8], in0=pv[:, :, 8:16], in1=SS[:, cidx, 0:8].unsqueeze(1).to_broadcast([128, nh, 8]),
                        op=ALU.mult), reads=[P(pb)] + ROPE_RES, writes=[("ropB", ri, 0)])
                    S.op("dve", lambda e, pv=pv, tB=tB, nh=nh, cidx=cidx: e.tensor_tensor(
                        out=tB[:, :, 8:16], in0=pv[:, :, 0:8], in1=SS[:, cidx, 8:16].unsqueeze(1).to_broadcast([128, nh, 8]),
                        op=ALU.mult), reads=[P(pb)] + ROPE_RES, writes=[("ropB", ri, 1)])
                    S.op("pool", lambda e, dst=dst, tA=tA, tB=tB: e.tensor_tensor(out=dst[:, :, 0:16], in0=tA, in1=tB,
                                                                                  op=ALU.add),
                         reads=[("ropA", ri), ("ropB", ri, 0), ("ropB", ri, 1)], writes=[("qkr", b, n)])
                    S.op("act", lambda e, dst=dst, pv=pv: e.activation(out=dst[:, :, 16:64], in_=pv[:, :, 16:64],
                                                                       func=AF.Copy),
                         reads=[P(pb)], writes=[("qkc", b, n)])
                    if n == 4:
                        S.op("act", lambda e, pb=pb, blk=blk: e.activation(
                            out=vA[:, blk, :, 0:64], in_=ps[pb][:, 128:256].rearrange("p (h d) -> p h d", h=2),
                            func=AF.Copy), reads=[P(pb)], writes=[("vA", blk)])
            for cc in range(13):
                pb = cc % 2

                def trf(e, cc=cc, pb=pb):
                    for b in range(4):
                        ins = e.transpose(out=psb[pb][:, b * 128:(b + 1) * 128], in_=qk[:, b, cc * 128:(cc + 1) * 128],
                                          identity=identb[:])
                    return ins
                nn = 4 if cc == 12 else cc // 4
                S.op("pe", trf, reads=[(kind, b, nn) for b in range(4) for kind in ("qkr", "qkc")] + ["identb"],
                     writes=[P(pb)])
                if cc < 8:
                    dstT, wres = qT[:, cc, :], [("qT", cc)]
                elif cc < 12:
                    dstT, wres = kBT[:, cc - 8, tau * 512:(tau + 1) * 512], [("kBT", cc - 8, tau)]
                else:
                    dstT, wres = kAT[:, tau * 512:(tau + 1) * 512], [("kAT", tau)]
                if cc % 2 == 0:
                    S.op("act", lambda e, pb=pb, dstT=dstT: e.activation(out=dstT, in_=psb[pb][:, 0:512], func=AF.Copy),
                         reads=[P(pb)], writes=wres)
                else:
                    S.op("dve", lambda e, pb=pb, dstT=dstT: e.tensor_copy(out=dstT, in_=psb[pb][:, 0:512]),
                         reads=[P(pb)], writes=wres)
            if t == 0:
                dbg("qT", qT[:].rearrange("p a b -> p (a b)"), [128, 4096], BF16, [("qT", k) for k in range(8)])
                dbg("kAT", kAT[:, 0:512], [128, 512], BF16, [("kAT", 0)])
                dbg("vB", vB[:, 0:4, :, :].rearrange("p a b c -> p (a b c)"), [128, 4 * 8 * 65], BF16,
                    [("vB", i) for i in range(4)])

            if stop == "B":
                break
            tiles = []
            for g in range(2):
                for qb in range(4):
                    i = tau * 4 + qb
                    ob = 3 + ring("oA", 2)
                    js = [j for j in (i - 1, i) if j >= 0]
                    for jn, j in enumerate(js):
                        tiles.append(dict(kind="A", g=g, qb=qb, j=j, N=512, mask=(0 if j == i else 1), ob=ob,
                                          first=(jn == 0), last=(jn == len(js) - 1)))
            for h in range(8):
                ob = 5 + ring("oB", 2)
                nj = tau * 4 + 4
                for j in range(nj):
                    m = tau * 4 - j
                    q0 = max(0, -m) * 128
                    tiles.append(dict(kind="B", h=h, j=j, N=512 - q0, q0=q0, mi=(m + 3 if m <= 4 else 8), ob=ob,
                                      first=(j == 0), last=(j == nj - 1)))

            def issue_score(T):
                sb_ = (0, 1, 2, 7)[ring("sbank", 4)]
                pr = ring("PT", 4)
                T["pr"] = pr
                N = T["N"]
                if T["kind"] == "A":
                    g, qb, j = T["g"], T["qb"], T["j"]
                    S.op("pe", lambda e: e.matmul(
                        ps[sb_][:, 0:512].rearrange("p (c q) -> p c q", c=4),
                        lhsT=kAT[g * 64:(g + 1) * 64, j * 128:(j + 1) * 128],
                        rhs=qT[g * 64:(g + 1) * 64, 0:4, qb * 128:(qb + 1) * 128], start=True, stop=True),
                        reads=[("kAT", j // 4)] + [("qT", c) for c in range(4)], writes=[P(sb_)])
                else:
                    h, j, q0 = T["h"], T["j"], T["q0"]
                    hp, hc = (h % 2) * 64, h // 2
                    S.op("pe", lambda e: e.matmul(
                        ps[sb_][:, 0:N], lhsT=kBT[hp:hp + 64, hc, j * 128:(j + 1) * 128],
                        rhs=qT[hp:hp + 64, 4 + hc, q0:512], start=True, stop=True),
                        reads=[("kBT", hc, j // 4), ("qT", 4 + hc)], writes=[P(sb_)])
                S.op("act", lambda e: e.activation(out=PT[pr][:, 0:N], in_=ps[sb_][:, 0:N], func=AF.Exp, scale=0.125),
                     reads=[P(sb_)], writes=[("PT", pr)])
                if T["kind"] == "A":
                    S.op("dve", lambda e: e.tensor_tensor(
                        out=PT[pr][:, :].rearrange("p (c q) -> p c q", c=4),
                        in0=PT[pr][:, :].rearrange("p (c q) -> p c q", c=4),
                        in1=maskA[:, T["mask"], :].unsqueeze(1).to_broadcast([128, 4, 128]), op=ALU.mult),
                        reads=[("PT", pr), "maskA"], writes=[("PT", pr)])
                else:
                    S.op("dve", lambda e: e.tensor_tensor(out=PT[pr][:, 0:N], in0=PT[pr][:, 0:N],
                                                          in1=maskB[:, T["mi"], T["q0"]:512], op=ALU.mult),
                         reads=[("PT", pr), "maskB"], writes=[("PT", pr)])

            def issue_pv(T):
                pr, ob = T["pr"], T["ob"]
                if T["kind"] == "A":
                    g, qb, j = T["g"], T["qb"], T["j"]

                    def pvf(e):
                        for c in range(4):
                            ins = e.matmul(ps[ob][:, c * 65:(c + 1) * 65], lhsT=PT[pr][:, c * 128:(c + 1) * 128],
                                           rhs=vA[:, j, g, :], start=(T["first"] and c == 0), stop=(T["last"] and c == 3),
                                           skip_group_check=True)
                        return ins
                    S.op("pe", pvf, reads=[("PT", pr), ("vA", j)], writes=[P(ob)])
                    if T["last"]:
                        di = ring("den", 4)
                        psv = ps[ob][:, 0:260].rearrange("p (c e) -> p c e", c=4)
                        S.op("dve", lambda e: e.tensor_tensor(out=den[di][:, :].unsqueeze(2), in0=psv[:, :, 64:65],
                                                              in1=esink[:, 4 * g:4 * g + 4].unsqueeze(2), op=ALU.add),
                             reads=[P(ob), "esink"], writes=[("den", di)])
                        S.op("dve", lambda e: e.reciprocal(out=den[di][:, :], in_=den[di][:, :]), reads=[("den", di)],
                             writes=[("den", di)])
                        S.op("dve", lambda e: e.tensor_tensor(
                            out=nb[:, qb, g * 256:(g + 1) * 256].rearrange("p (c d) -> p c d", c=4),
                            in0=psv[:, :, 0:64], in1=den[di][:, :].unsqueeze(2).to_broadcast([128, 4, 64]), op=ALU.mult),
                            reads=[P(ob), ("den", di)], writes=[("nb", qb, g)])
                else:
                    h, j, q0 = T["h"], T["j"], T["q0"]
                    qb0 = q0 // 128

                    def pvf(e):
                        for qb in range(qb0, 4):
                            ins = e.matmul(ps[ob][:, qb * 65:(qb + 1) * 65],
                                           lhsT=PT[pr][:, qb * 128 - q0:(qb + 1) * 128 - q0], rhs=vB[:, j, h, :],
                                           start=(T["first"] and qb == qb0), stop=(T["last"] and qb == 3),
                                           skip_group_check=True)
                        return ins
                    S.op("pe", pvf, reads=[("PT", pr), ("vB", j)], writes=[P(ob)])
                    if T["last"]:
                        di = ring("den", 4)
                        psv = ps[ob][:, 0:260].rearrange("p (c e) -> p c e", c=4)
                        S.op("dve", lambda e: e.reciprocal(out=den[di][:, :].unsqueeze(2), in_=psv[:, :, 64:65]),
                             reads=[P(ob)], writes=[("den", di)])
                        S.op("dve", lambda e: e.tensor_tensor(
                            out=nb[:, :, 512 + h * 64:512 + (h + 1) * 64], in0=psv[:, :, 0:64],
                            in1=den[di][:, :].unsqueeze(2).to_broadcast([128, 4, 64]), op=ALU.mult),
                            reads=[P(ob), ("den", di)], writes=[("nb", b, 2 + h // 4) for b in range(4)])

            def phaseD_half(mx):
                for b in range(4):
                    S.op("act", lambda e, b=b: e.activation(
                        out=junk[:, 0:512], in_=nb[:, b, mx * 512:(mx + 1) * 512], func=AF.Square,
                        accum_out=st_ss[:, 4 * mx + b:4 * mx + b + 1]), reads=nbhalf(b, mx),
                        writes=["junk", ("ss", 4 * mx + b)])
                S.op("dve", lambda e: e.tensor_scalar(out=st_v[:, 4 * mx:4 * mx + 4], in0=st_ss[:, 4 * mx:4 * mx + 4],
                                                      scalar1=1.0 / 512, scalar2=EPS, op0=ALU.mult, op1=ALU.add),
                     reads=[("ss", 4 * mx + i) for i in range(4)], writes=[("stv", 4 * mx + i) for i in range(4)])
                S.op("pool", lambda e: e.tensor_tensor(out=st_r[:, 4 * mx:4 * mx + 4], in0=st_v[:, 4 * mx:4 * mx + 4],
                                                       in1=mhalf[:, 0:4], op=ALU.pow),
                     reads=[("stv", 4 * mx + i) for i in range(4)] + ["mhalf"],
                     writes=[("str", 4 * mx + i) for i in range(4)])
                for b in range(4):
                    S.op("dve", lambda e, b=b: e.tensor_scalar(
                        out=nb[:, b, mx * 512:(mx + 1) * 512], in0=nb[:, b, mx * 512:(mx + 1) * 512],
                        scalar1=st_r[:, 4 * mx + b:4 * mx + b + 1], scalar2=None, op0=ALU.mult),
                        reads=nbhalf(b, mx) + [("str", 4 * mx + b)], writes=nbhalf(b, mx))

            LOOK = 3
            n_a = sum(1 for T in tiles if T["kind"] == "A")
            for n in range(len(tiles) + LOOK):
                if n < len(tiles):
                    issue_score(tiles[n])
                if n >= LOOK:
                    issue_pv(tiles[n - LOOK])
                    if n - LOOK == n_a - 1:
                        phaseD_half(0)
            if t == 0:
                dbg("mix", nb.rearrange("p a b -> p (a b)"), [128, 4096], F32, [r for b in range(4) for r in nbres(b)])

            if stop == "C":
                break
            if t + 1 < NT:
                load_x(t + 1)
            phaseD_half(1)
            transposes_to_hT(lambda k: (gT[:, 2, k:k + 1], None), ["gT2"], fine=True)

            sls = [load_piece(5 + h) for h in range(2)]
            for b in range(4):
                banks = (2 + 2 * (b % 3), 3 + 2 * (b % 3))
                for h in range(2):
                    wv = ws[sls[h]][:, :].rearrange("p (k c) -> p k c", k=8)

                    def mmf(e, b=b, h=h, wv=wv, banks=banks):
                        for k in range(8):
                            ins = e.matmul(ps[banks[h]][:, :], lhsT=hT[:, k, b * 128:(b + 1) * 128], rhs=wv[:, k, :],
                                           start=(k == 0), stop=(k == 7))
                        return ins
                    S.op("pe", mmf, reads=[("ws", sls[h])] + [("hT", k) for k in range(8)], writes=[P(banks[h])])
                postnorm_residual(buf, b, banks, s, 0)
            if t == 0:
                dbg("x1", xt[buf][:].rearrange("p a b -> p (a b)"), [128, 4096], F32, [("xt", buf, b) for b in range(4)])

            if stop == "E":
                break
            prenorm_stats(buf)
            prenorm_tr(s, 2, 3)
            hoist = (t + 1 < NT) and stop is None

            for g8 in range(8):
                if g8 == 1 and hoist:
                    prenorm_stats((t + 1) % 2)
                sl = load_piece(7 + g8)
                wv = ws[sl][:, :].rearrange("p (k c) -> p k c", k=8)
                for fc in range(4):
                    f = g8 * 4 + fc
                    pb = f % 4

                    def mmf(e, fc=fc, pb=pb, wv=wv):
                        for k in range(8):
                            ins = e.matmul(ps[pb][:, :], lhsT=wv[:, k, fc * 128:(fc + 1) * 128], rhs=hT[:, k, :],
                                           start=(k == 0), stop=(k == 7))
                        return ins
                    S.op("pe", mmf, reads=[("ws", sl)] + [("hT", k) for k in range(8)], writes=[P(pb)])
                    ri = f % 2
                    S.op("act", lambda e, pb=pb, ri=ri: e.activation(out=rtmp[ri][:], in_=ps[pb][:, :], func=AF.Relu),
                         reads=[P(pb)], writes=[("rtmp", ri)])
                    S.op("pool", lambda e, f=f, ri=ri: e.tensor_tensor(out=uT[:, f, :], in0=rtmp[ri][:], in1=rtmp[ri][:],
                                                                       op=ALU.mult),
                         reads=[("rtmp", ri)], writes=[("uT", f)])

            if stop == "G":
                break
            if hoist:
                prenorm_tr((t + 1) // 4, 0, 1, banks=(4, 5))
            for g8 in range(8):
                sl = load_piece(15 + g8)
                wv = ws[sl][:, :].rearrange("p (f d) -> p f d", f=4)
                for fc in range(4):
                    f = g8 * 4 + fc

                    def mmf(e, f=f, fc=fc, wv=wv):
                        for b in range(4):
                            for h in range(2):
                                ins = e.matmul(ps[2 * b + h][:, :], lhsT=uT[:, f, b * 128:(b + 1) * 128],
                                               rhs=wv[:, fc, h * 512:(h + 1) * 512], start=(f == 0), stop=(f == 31))
                        return ins
                    S.op("pe", mmf, reads=[("ws", sl), ("uT", f)], writes=[P(i) for i in range(8)])

            for b in range(4):
                postnorm_residual(buf, b, (2 * b, 2 * b + 1), s, 1, store_rows=t)
            if stop == "T0":
                break

        S.op("pool", None, reads=stores_done + [("dbg", n) for n in dbg_outs])
        S.emit()
    return nc, list(dbg_outs.keys())


def _consts():
    bf = ml_dtypes.bfloat16
    k = np.arange(128)[:, None]
    qi = np.arange(512)[None, :]
    mB = np.zeros((128, 9, 512), np.float32)
    for mi in range(9):
        m = mi - 3 if mi < 8 else 5
        d = 128 * m + qi - k
        mult = ((d >= 0) & (d <= 128)).astype(np.float32) \
            + ((d >= 0) & (d % 4 == 0) & (d <= 512)).astype(np.float32) \
            + ((d >= 0) & (d % 16 == 0) & (d <= 2048)).astype(np.float32)
        mB[:, mi, :] = mult
    q = np.arange(128)[None, :]
    mA = np.zeros((128, 2, 128), np.float32)
    mA[:, 0, :] = (k <= q)
    mA[:, 1, :] = (k > q)
    sel = np.zeros((2, 2, 128), np.float32)
    sel[0, 0, :] = 1.0
    sel[1, 1, :] = 1.0
    invf = (500000.0 ** (-np.arange(0, 16, 2, dtype=np.float32) / 16.0)).astype(np.float32)[None, :]
    return dict(ident=np.eye(128, dtype=np.float32), maskB=mB.reshape(128, -1).astype(bf),
                maskA=mA.reshape(128, -1).astype(bf), sel=sel.reshape(2, -1), invf=invf)


_QA_PERM = [0, 4, 1, 5, 2, 6, 3, 7]


def _prep_shared(w_ada, b_ada, g_attn_pre, g_attn_post, w_in, sink_a, g_mix_a, g_mix_b, w_out, g_mlp_pre,
                 g_mlp_post, w_up, w_down):
    w_in0 = w_in[0]
    qa = w_in0[:, 0:512].reshape(D, 8, 64)[:, _QA_PERM, :].reshape(D, 512)
    ka, va = w_in0[:, 512:640], w_in0[:, 640:768]
    qb, kb, vb = w_in0[:, 768:1280], w_in0[:, 1280:1792], w_in0[:, 1792:2304]
    w_in_p = np.ascontiguousarray(np.concatenate([qa, qb, kb, vb, ka, va], axis=1))

    def colT(v):
        return np.ascontiguousarray(v.reshape(8, 128).T)
    sh = dict(w_ada=np.ascontiguousarray(w_ada[0]), b_ada=np.ascontiguousarray(b_ada[0:1]),
              gpreT=colT(g_attn_pre[0]), gmlpT=colT(g_mlp_pre[0]),
              gmixT=colT(np.concatenate([g_mix_a[0], g_mix_b[0]])),
              gposta=np.ascontiguousarray(g_attn_post[0:1]), gpostm=np.ascontiguousarray(g_mlp_post[0:1]),
              w_in=w_in_p, w_out=np.ascontiguousarray(w_out[0]), w_up=np.ascontiguousarray(w_up[0]),
              w_down=np.ascontiguousarray(w_down[0]), sink=np.ascontiguousarray(sink_a[0:1]))
    sh.update(_consts())
    return sh


_NC_CACHE = {}


def kernel(x, c, positions, w_ada, b_ada, g_attn_pre, g_attn_post, w_in, sink_a, g_mix_a, g_mix_b, w_out,
           g_mlp_pre, g_mlp_post, w_up, w_down, _debug=None):
    x = np.asarray(x, np.float32)
    c = np.asarray(c, np.float32)
    positions = np.asarray(positions, np.int32)
    args = [np.asarray(a, np.float32) for a in (w_ada, b_ada, g_attn_pre, g_attn_post, w_in, sink_a, g_mix_a,
                                                g_mix_b, w_out, g_mlp_pre, g_mlp_post, w_up, w_down)]
    shared = _prep_shared(*args)
    key = tuple(_debug) if _debug else None
    if key not in _NC_CACHE:
        _NC_CACHE[key] = build_nc(_debug)
    nc, dbg_names = _NC_CACHE[key]
    in_maps = []
    for i in range(NCORES):
        m = dict(shared)
        m["x"] = np.ascontiguousarray(x[2 * i:2 * i + 2].reshape(2 * SEQ, D))
        cc = c[2 * i:2 * i + 2]
        m["cT"] = np.ascontiguousarray(cc.reshape(2, 8, 128).transpose(2, 1, 0).reshape(128, 16))
        pp = positions[2 * i:2 * i + 2]
        m["pos"] = np.ascontiguousarray(pp.reshape(2, 16, 128).transpose(2, 0, 1).reshape(128, 32))
        in_maps.append(m)
    res = run_bass_kernel_spmd(nc, in_maps, core_ids=list(range(NCORES)))
    out = np.stack([r["out"].reshape(2, SEQ, D) for r in res.results], axis=0).reshape(2 * NCORES, SEQ, D)
    if _debug:
        return out.astype(np.float32), [{n: r["dbg_" + n] for n in dbg_names} for r in res.results]
    return out.astype(np.float32)
```

```python
import contextlib
import numpy as np
import ml_dtypes
import concourse.bass as bass
import concourse.mybir as mybir
from concourse.alu_op_type import AluOpType as ALU
from concourse.bass_utils import run_bass_kernel_spmd

F32 = mybir.dt.float32
BF16 = mybir.dt.bfloat16
I32 = mybir.dt.int32
AF = mybir.ActivationFunctionType

NCORES = 8
SEQ = 2048
D = 1024
DFF = 4096
EPS = 1e-6
NW = 3
TWO_PI_HI = 6.28125
TWO_PI_LO = 2.0 * np.pi - 6.28125
PI_F = float(np.float32(np.pi))


class Sched:
    ENGS = ("pe", "act", "dve", "pool", "sp")

    def __init__(self, nc):
        self.nc = nc
        self.ops = {e: [] for e in self.ENGS}
        self.cnt = {e: 0 for e in self.ENGS}
        self.chan_cnt = {}
        self.lastw = {}
        self.readers = {}
        self.seen = {e: {} for e in self.ENGS}

    def _dep(self, eng, tok, waits):
        semkey, val = tok[0], tok[1]
        if self.seen[eng].get(semkey, 0) >= val:
            return
        self.seen[eng][semkey] = val
        waits[semkey] = max(waits.get(semkey, 0), val)

    def op(self, eng, fn, reads=(), writes=(), chan=None):
        waits = {}
        if eng != "pe":
            extra = [r for r in reads if isinstance(r, tuple) and r[0] == "ps" and r not in writes]
            if extra:
                writes = list(writes) + extra
        for r in reads:
            t = self.lastw.get(r)
            if t is not None:
                if t[0][0] == "e" and t[2] == eng and eng == "pe":
                    continue
                self._dep(eng, t, waits)
        for w in writes:
            t = self.lastw.get(w)
            if t is not None and not (t[0][0] == "e" and t[2] == eng and eng == "pe"):
                self._dep(eng, t, waits)
            for t in self.readers.get(w, ()):
                if t[0][0] == "e" and t[2] == eng and eng == "pe":
                    continue
                self._dep(eng, t, waits)
        if chan is None:
            self.cnt[eng] += 1
            tok = (("e", eng), self.cnt[eng], eng)
        else:
            self.chan_cnt[chan] = self.chan_cnt.get(chan, 0) + 1
            tok = (("c", chan), 16 * self.chan_cnt[chan], eng)
        self.ops[eng].append(dict(fn=fn, waits=waits, chan=chan))
        for r in reads:
            self.readers.setdefault(r, []).append(tok)
        for w in writes:
            self.lastw[w] = tok
            self.readers[w] = []
        return tok

    def emit(self):
        nc = self.nc
        with contextlib.ExitStack() as st:
            sems = {}
            for e in self.ENGS:
                sems[("e", e)] = st.enter_context(nc.semaphore("s_" + e))
            for i, c in enumerate(self.chan_cnt):
                sems[("c", c)] = st.enter_context(nc.semaphore("c%d" % i))
            block = st.enter_context(nc.Block())
            handles = dict(pe="tensor", act="scalar", dve="vector", pool="gpsimd", sp="sync")

            def make(e):
                def body(h):
                    for o in self.ops[e]:
                        for sk, v in o["waits"].items():
                            h.wait_ge(sems[sk], v)
                        if o["fn"] is None:
                            continue
                        ins = o["fn"](h)
                        if o["chan"] is None:
                            ins.then_inc(sems[("e", e)], 1)
                        else:
                            ins.then_inc(sems[("c", o["chan"])], 16)
                return body

            for e in self.ENGS:
                if self.ops[e]:
                    getattr(block, handles[e])(make(e))


def build_nc(debug=None, stop=None, nt=8):
    nc = bass.Bass("TRN2", target_bir_lowering=False)

    def din(name, shape, dt):
        return nc.dram_tensor(name, shape, dt, kind="ExternalInput").ap()

    x_d = din("x", [4096, D], F32)
    cT_d = din("cT", [128, 16], F32)
    pos_d = din("pos", [128, 32], I32)
    w_ada_d = din("w_ada", [D, 6 * D], F32)
    b_ada_d = din("b_ada", [1, 6 * D], F32)
    gpreT_d = din("gpreT", [128, 8], F32)
    gmlpT_d = din("gmlpT", [128, 8], F32)
    gmixT_d = din("gmixT", [128, 8], F32)
    gposta_d = din("gposta", [1, D], F32)
    gpostm_d = din("gpostm", [1, D], F32)
    w_in_d = din("w_in", [D, 2304], F32)
    w_out_d = din("w_out", [D, D], F32)
    w_up_d = din("w_up", [D, DFF], F32)
    w_down_d = din("w_down", [DFF, D], F32)
    sink_d = din("sink", [1, 8], F32)
    ident_d = din("ident", [128, 128], F32)
    maskB_d = din("maskB", [128, 9 * 512], BF16)
    maskA_d = din("maskA", [128, 256], BF16)
    sel_d = din("sel", [2, 256], F32)
    invf_d = din("invf", [1, 8], F32)
    out_d = nc.dram_tensor("out", [4096, D], F32, kind="ExternalOutput").ap()
    wscr = nc.dram_tensor("wscr", [23, 128, 4096], BF16, kind="Internal").ap()

    dbg_outs = {}
    S = Sched(nc)

    with contextlib.ExitStack() as st:
        def sb(name, shape, dt):
            return st.enter_context(nc.sbuf_tensor(name, shape, dt))

        def psum(name, shape, dt):
            return st.enter_context(nc.psum_tensor(name, shape, dt))

        xt = [sb("xt%d" % i, [128, 4, D], F32) for i in range(2)]
        big = sb("big", [128, 16384], BF16)
        uT = big[:].rearrange("p (c t) -> p c t", c=32)
        nbt = sb("nbt", [128, 4, D], F32)
        nb = nbt[:]
        qk = big[:, 8192:16384].rearrange("p (b d) -> p b d", b=4)
        stage = [big[:, 8192 * i:8192 * (i + 1)].bitcast(F32) for i in range(2)]
        junk = sb("junk", [128, D], BF16)
        hT = sb("hT", [128, 8, 512], BF16)
        qT = sb("qT", [128, 8, 512], BF16)
        kAT = sb("kAT", [128, SEQ], BF16)
        kBT = sb("kBT", [128, 4, SEQ], BF16)
        vA = sb("vA", [128, 16, 2, 65], BF16)
        vB = sb("vB", [128, 16, 8, 65], BF16)
        rtmp = [sb("rtmp%d" % i, [128, 512], F32) for i in range(2)]
        PT = [sb("PT%d" % i, [128, 512], BF16) for i in range(4)]
        ws = [sb("ws%d" % i, [128, 4096], BF16) for i in range(NW)]
        maskB = sb("maskB_s", [128, 9, 512], BF16)
        maskA = sb("maskA_s", [128, 2, 128], BF16)
        ident = sb("ident_s", [128, 128], F32)
        identb = sb("identb", [128, 128], BF16)
        ggt = sb("ggt", [128, 2, 2, D], F32)
        modT = sb("modT", [128, 2, 4, 8], F32)
        gT = sb("gT", [128, 3, 8], F32)
        CC = sb("CC", [128, 32, 16], F32)
        SS = sb("SS", [128, 32, 16], F32)
        esink = sb("esink", [128, 8], F32)
        mhalf = sb("mhalf", [128, 8], F32)
        zb = sb("zb", [128, 1], F32)
        cT = sb("cT_s", [128, 16], F32)
        cond = sb("cond", [128, 16], F32)
        ctmp = sb("ctmp", [128, 16], F32)
        posi = sb("posi", [128, 32], I32)
        posf = sb("posf", [128, 32], F32)
        invf = sb("invf_s", [128, 8], F32)
        sel = sb("sel_s", [2, 2, 128], F32)
        st_ss = sb("st_ss", [128, 8], F32)
        st_v = sb("st_v", [128, 8], F32)
        st_r = sb("st_r", [128, 8], F32)
        ssy = [sb("ssy%d" % i, [128, 4], F32) for i in range(4)]
        den = [sb("den%d" % i, [128, 4], F32) for i in range(4)]
        ropA = [sb("ropA%d" % i, [128, 128], F32) for i in range(4)]
        ropB = [sb("ropB%d" % i, [128, 128], F32) for i in range(4)]
        ropX = [PT[i][:, 0:256].bitcast(F32) for i in range(4)]
        rpro = qT[:, 0:4, :].rearrange("p a b -> p (a b)").bitcast(F32).rearrange("p (a b) -> p a b", a=4)
        ang = rpro[:, 0, :].rearrange("p (a b) -> p a b", b=8)
        ra = rpro[:, 1, :].rearrange("p (a b) -> p a b", b=8)
        rb = rpro[:, 2, :].rearrange("p (a b) -> p a b", b=8)
        rki = rpro[:, 3, :].bitcast(I32).rearrange("p (a b) -> p a b", b=8)

        ps = [psum("ps%d" % i, [128, 512], F32) for i in range(8)]
        psb = [p[:].bitcast(BF16) for p in ps]

        gpb = xt[1]
        modp = [rtmp[i][0:2, :] for i in range(2)]
        bada = junk[0:2, :].bitcast(F32)

        def P(i):
            return ("ps", i)

        def nbres(b):
            return [("nb", b, i) for i in range(4)]

        def nbhalf(b, mx):
            return [("nb", b, 2 * mx), ("nb", b, 2 * mx + 1)]

        def qkres(b):
            return [("uT", 16 + 4 * b + i) for i in range(4)]

        def stres(i):
            return [("uT", 16 * i + j) for j in range(16)]

        rings = {}

        def ring(name, n):
            v = rings.get(name, 0)
            rings[name] = v + 1
            return v % n

        def dbg(name, src_ap, shape, dt, reads):
            if debug is None or name not in debug:
                return
            d = nc.dram_tensor("dbg_" + name, shape, dt, kind="ExternalOutput").ap()
            dbg_outs[name] = d
            S.op("sp", lambda e: e.dma_start(out=d, in_=src_ap), reads=reads, writes=[("dbg", name)],
                 chan=("dbg", name))

        def ld(dst, src, res, ch):
            S.op("sp", lambda e: e.dma_start(out=dst, in_=src), writes=[res], chan=ch)

        ld(ident[:], ident_d, "ident", "k0")
        ld(maskB[:].rearrange("p a b -> p (a b)"), maskB_d, "maskB", "k1")
        ld(maskA[:].rearrange("p a b -> p (a b)"), maskA_d, "maskA", "k2")
        ld(sel[:].rearrange("p a b -> p (a b)"), sel_d, "sel", "k3")
        ld(invf[:], invf_d.partition_broadcast(128), "invf", "k4")
        ld(esink[:], sink_d.partition_broadcast(128), "esink0", "k5")
        ld(cT[:], cT_d, "cT", "k6")
        ld(posi[:], pos_d, "posi", "k7")
        ld(gT[:, 0, :], gpreT_d, "gT0", "k8")
        ld(gT[:, 1, :], gmlpT_d, "gT1", "k9")
        ld(gT[:, 2, :], gmixT_d, "gT2", "k10")
        ld(gpb[:, 0, :], gposta_d.partition_broadcast(128), ("xt", 1, 0), "k11")
        ld(gpb[:, 1, :], gpostm_d.partition_broadcast(128), ("xt", 1, 1), "k12")

        S.op("pool", lambda e: e.memset(mhalf[:], -0.5), writes=["mhalf"])
        S.op("pool", lambda e: e.memset(zb[:], 0.0), writes=["zb"])
        S.op("pool", lambda e: e.memset(vA[:, :, :, 64:65], 1.0), writes=[("vA", i) for i in range(16)])
        S.op("pool", lambda e: e.memset(vB[:, :, :, 64:65], 1.0), writes=[("vB", i) for i in range(16)])
        def piece_src(q):
            if q < 5:
                w = 512 if q < 4 else 256
                return w_in_d.rearrange("(k p) c -> p k c", p=128)[:, :, q * 512:q * 512 + w], (8, w)
            if q < 7:
                h = q - 5
                return w_out_d.rearrange("(k p) c -> p k c", p=128)[:, :, h * 512:(h + 1) * 512], (8, 512)
            if q < 15:
                g = q - 7
                return w_up_d.rearrange("(k p) c -> p k c", p=128)[:, :, g * 512:(g + 1) * 512], (8, 512)
            g = q - 15
            return w_down_d[g * 512:(g + 1) * 512, :].rearrange("(f p) d -> p f d", p=128), (4, D)

        for q in range(23):
            src, (a, bdim) = piece_src(q)
            dstv = wscr[q, :, 0:a * bdim].rearrange("p (a b) -> p a b", a=a)
            S.op("pool", lambda e, src=src, dstv=dstv: e.dma_start(out=dstv, in_=src), writes=[("wscr", q)],
                 chan=("cv", q))

        S.op("dve", lambda e: e.tensor_copy(out=identb[:], in_=ident[:]), reads=["ident"], writes=["identb"])
        S.op("act", lambda e: e.activation(out=esink[:], in_=esink[:], func=AF.Exp), reads=["esink0"],
             writes=["esink"])

        S.op("act", lambda e: e.activation(out=ctmp[:], in_=cT[:], func=AF.Exp, scale=-1.0), reads=["cT"],
             writes=["ctmp"])
        S.op("dve", lambda e: e.tensor_scalar(out=ctmp[:], in0=ctmp[:], scalar1=1.0, scalar2=None, op0=ALU.add),
             reads=["ctmp"], writes=["ctmp"])
        S.op("dve", lambda e: e.reciprocal(out=ctmp[:], in_=ctmp[:]), reads=["ctmp"], writes=["ctmp"])
        S.op("dve", lambda e: e.tensor_tensor(out=cond[:], in0=cT[:], in1=ctmp[:], op=ALU.mult),
             reads=["ctmp", "cT"], writes=["cond"])
        condv = cond[:].rearrange("p (k b) -> p k b", b=2)

        S.op("dve", lambda e: e.tensor_copy(out=posf[:], in_=posi[:]), reads=["posi"], writes=["posf"])
        S.op("dve", lambda e: e.tensor_tensor(out=ang, in0=posf[:].unsqueeze(2).to_broadcast([128, 32, 8]),
                                              in1=invf[:].unsqueeze(1).to_broadcast([128, 32, 8]), op=ALU.mult),
             reads=["posf", "invf"], writes=[("qT", 0)])
        S.op("dve", lambda e: e.tensor_scalar(out=ra, in0=ang, scalar1=float(1.0 / (2 * np.pi)), scalar2=None,
                                              op0=ALU.mult), reads=[("qT", 0)], writes=[("qT", 1)])
        S.op("dve", lambda e: e.tensor_copy(out=rki, in_=ra), reads=[("qT", 1)], writes=[("qT", 3)])
        S.op("dve", lambda e: e.tensor_copy(out=ra, in_=rki), reads=[("qT", 3)], writes=[("qT", 1)])
        S.op("dve", lambda e: e.scalar_tensor_tensor(out=rb, in0=ra, scalar=-TWO_PI_HI, in1=ang,
                                                     op0=ALU.mult, op1=ALU.add), reads=[("qT", 1), ("qT", 0)], writes=[("qT", 2)])
        S.op("dve", lambda e: e.scalar_tensor_tensor(out=rb, in0=ra, scalar=-TWO_PI_LO, in1=rb,
                                                     op0=ALU.mult, op1=ALU.add), reads=[("qT", 1), ("qT", 2)], writes=[("qT", 2)])

        def wrap(buf, tmp, name, tname):
            S.op("dve", lambda e: e.tensor_scalar(out=tmp, in0=buf, scalar1=PI_F, scalar2=-2.0 * np.pi,
                                                  op0=ALU.is_gt, op1=ALU.mult), reads=[name], writes=[tname])
            S.op("dve", lambda e: e.tensor_tensor(out=buf, in0=buf, in1=tmp, op=ALU.add),
                 reads=[name, tname], writes=[name])
            S.op("dve", lambda e: e.tensor_scalar(out=tmp, in0=buf, scalar1=-PI_F, scalar2=2.0 * np.pi,
                                                  op0=ALU.is_lt, op1=ALU.mult), reads=[name], writes=[tname])
            S.op("dve", lambda e: e.tensor_tensor(out=buf, in0=buf, in1=tmp, op=ALU.add),
                 reads=[name, tname], writes=[name])

        wrap(rb, ra, ("qT", 2), ("qT", 1))
        S.op("act", lambda e: e.activation(out=SS[:, :, 8:16], in_=rb, func=AF.Sin), reads=[("qT", 2)], writes=["SSb"])
        S.op("dve", lambda e: e.tensor_scalar(out=SS[:, :, 0:8], in0=SS[:, :, 8:16], scalar1=-1.0, scalar2=None,
                                              op0=ALU.mult), reads=["SSb"], writes=["SSa"])
        S.op("dve", lambda e: e.tensor_scalar(out=rb, in0=rb, scalar1=float(np.pi / 2), scalar2=None,
                                              op0=ALU.add), reads=[("qT", 2), "SSb"], writes=[("qT", 2)])
        wrap(rb, ra, ("qT", 2), ("qT", 1))
        S.op("act", lambda e: e.activation(out=CC[:, :, 0:8], in_=rb, func=AF.Sin), reads=[("qT", 2)], writes=["CCa"])
        S.op("dve", lambda e: e.tensor_copy(out=CC[:, :, 8:16], in_=CC[:, :, 0:8]), reads=["CCa"], writes=["CCb"])
        ROPE_RES = ["CCa", "CCb", "SSa", "SSb"]

        w_ada_v = w_ada_d.rearrange("(k p) c -> p k c", p=128)
        for n in range(12):
            si = n % 2
            stg = stage[si].rearrange("p (k c) -> p k c", k=8)
            S.op("sp", lambda e, stg=stg, n=n: e.dma_start(out=stg, in_=w_ada_v[:, :, n * 512:(n + 1) * 512]),
                 writes=stres(si), chan=("stg", si))
            S.op("sp", lambda e, n=n: e.dma_start(out=bada, in_=b_ada_d[:, n * 512:(n + 1) * 512].partition_broadcast(2)),
                 writes=["junk"], chan="bada")
            pm = 2 + (n % 2)

            def mmf(e, stg=stg, pm=pm):
                for k in range(8):
                    ins = e.matmul(ps[pm][0:2, :], lhsT=condv[:, k, :], rhs=stg[:, k, :], start=(k == 0), stop=(k == 7))
                return ins
            S.op("pe", mmf, reads=stres(si) + ["cond"], writes=[P(pm)])
            mp = modp[n % 2]
            S.op("dve", lambda e, mp=mp, pm=pm: e.tensor_tensor(out=mp, in0=ps[pm][0:2, :], in1=bada, op=ALU.add),
                 reads=[P(pm), "junk"], writes=[("rtmp", n % 2)])
            v, half = n // 2, n % 2
            if v in (2, 5):
                am = 0 if v == 2 else 1
                for s in range(2):
                    pg = 4 + s
                    S.op("pe", lambda e, mp=mp, s=s, pg=pg: e.matmul(ps[pg][:, :], lhsT=sel[:, s, :], rhs=mp,
                                                                       start=True, stop=True),
                         reads=[("rtmp", n % 2), "sel"], writes=[P(pg)])
                    S.op("dve", lambda e, s=s, pg=pg, am=am, half=half: e.tensor_tensor(
                        out=ggt[:, s, am, half * 512:(half + 1) * 512], in0=ps[pg][:, :],
                        in1=gpb[:, am, half * 512:(half + 1) * 512], op=ALU.mult),
                        reads=[P(pg), ("xt", 1, 0), ("xt", 1, 1)], writes=[("ggt", s, am, half)])
            else:
                vi = {0: 0, 1: 1, 3: 2, 4: 3}[v]
                pg = 6

                def trf(e, mp=mp):
                    for q in range(4):
                        ins = e.transpose(out=ps[pg][:, 2 * q:2 * q + 2], in_=mp[:, q * 128:(q + 1) * 128],
                                          identity=ident[0:2, 0:2])
                    return ins
                S.op("pe", trf, reads=[("rtmp", n % 2), "ident"], writes=[P(pg)])
                S.op("dve", lambda e, vi=vi, half=half: e.tensor_copy(
                    out=modT[:, :, vi, half * 4:half * 4 + 4],
                    in_=ps[pg][:, 0:8].rearrange("p (q s) -> p s q", s=2)),
                    reads=[P(pg)], writes=[("modT", vi, half)])
        for vi, gi in ((1, 0), (3, 1)):
            S.op("dve", lambda e, vi=vi: e.tensor_scalar(out=modT[:, :, vi, :], in0=modT[:, :, vi, :], scalar1=1.0,
                                                         scalar2=None, op0=ALU.add),
                 reads=[("modT", vi, 0), ("modT", vi, 1)], writes=[("modT", vi, 0), ("modT", vi, 1)])
            S.op("dve", lambda e, vi=vi, gi=gi: e.tensor_tensor(
                out=modT[:, :, vi, :], in0=modT[:, :, vi, :], in1=gT[:, gi, :].unsqueeze(1).to_broadcast([128, 2, 8]),
                op=ALU.mult), reads=[("modT", vi, 0), ("modT", vi, 1), "gT%d" % gi],
                writes=[("modT", vi, 0), ("modT", vi, 1)])
        MODT_RES = [("modT", vi, h) for vi in range(4) for h in range(2)]
        GGT_RES = [("ggt", s, am, h) for s in range(2) for am in range(2) for h in range(2)]

        wstate = {"n": 0}

        def load_piece(q, n_el=4096):
            sl = wstate["n"] % NW
            wstate["n"] += 1
            S.op("sp", lambda e: e.dma_start(out=ws[sl][:, 0:n_el], in_=wscr[q, :, 0:n_el]), reads=[("wscr", q)],
                 writes=[("ws", sl)], chan=("wl", sl))
            return sl

        x_v = x_d.rearrange("(t b p) d -> t p b d", p=128, b=4)
        out_v = out_d.rearrange("(t b p) d -> t p b d", p=128, b=4)

        def load_x(t):
            buf = t % 2
            S.op("sp", lambda e: e.dma_start(out=xt[buf][:], in_=x_v[t]), writes=[("xt", buf, b) for b in range(4)],
                 chan=("xl", buf))

        def prenorm_stats(buf):
            X = xt[buf]
            for b in range(4):
                S.op("act", lambda e, b=b: e.activation(out=junk[:], in_=X[:, b, :], func=AF.Square,
                                                        accum_out=st_ss[:, b:b + 1]),
                     reads=[("xt", buf, b)], writes=["junk", ("ss", b)])
                S.op("pool", lambda e, b=b: e.tensor_scalar(out=st_v[:, b:b + 1], in0=st_ss[:, b:b + 1], scalar1=1.0 / D,
                                                            scalar2=EPS, op0=ALU.mult, op1=ALU.add),
                     reads=[("ss", b)], writes=[("stv", b)])
                S.op("pool", lambda e, b=b: e.tensor_tensor(out=st_r[:, b:b + 1], in0=st_v[:, b:b + 1],
                                                            in1=mhalf[:, 0:1], op=ALU.pow),
                     reads=[("stv", b), "mhalf"], writes=[("str", b)])
                if b % 2 == 0:
                    S.op("dve", lambda e, b=b: e.tensor_scalar(out=nb[:, b, :], in0=X[:, b, :], scalar1=st_r[:, b:b + 1],
                                                               scalar2=None, op0=ALU.mult),
                         reads=[("xt", buf, b), ("str", b)], writes=nbres(b))
                else:
                    S.op("act", lambda e, b=b: e.activation(out=nb[:, b, :], in_=X[:, b, :], func=AF.Identity,
                                                            scale=st_r[:, b:b + 1], bias=zb[:]),
                         reads=[("xt", buf, b), ("str", b), "zb"], writes=nbres(b))

        def prenorm_tr(s, vi_sh, vi_gsc, banks=(0, 1)):
            transposes_to_hT(lambda k: (modT[:, s, vi_gsc, k:k + 1], modT[:, s, vi_sh, k:k + 1]), MODT_RES, banks=banks)

        def transposes_to_hT(scale_bias, extra_reads, fine=False, banks=(0, 1)):
            for k in range(8):
                pb = banks[k % 2]

                def trf(e, k=k, pb=pb):
                    for b in range(4):
                        ins = e.transpose(out=ps[pb][:, b * 128:(b + 1) * 128], in_=nb[:, b, k * 128:(k + 1) * 128],
                                          identity=ident[:])
                    return ins
                if fine:
                    rr = [("nb", b, k // 2) for b in range(4)]
                else:
                    rr = [r for b in range(4) for r in nbres(b)]
                S.op("pe", trf, reads=rr + ["ident"], writes=[P(pb)])
                sc, bi = scale_bias(k)
                if k % 2 == 0:
                    S.op("act", lambda e, k=k, pb=pb, sc=sc, bi=bi: e.activation(
                        out=hT[:, k, :], in_=ps[pb][:, :], func=AF.Identity, scale=sc, bias=(bi if bi is not None else zb[:])),
                        reads=[P(pb), "zb"] + extra_reads, writes=[("hT", k)])
                else:
                    if bi is not None:
                        S.op("dve", lambda e, k=k, pb=pb, sc=sc, bi=bi: e.tensor_scalar(
                            out=hT[:, k, :], in0=ps[pb][:, :], scalar1=sc, scalar2=bi, op0=ALU.mult, op1=ALU.add),
                            reads=[P(pb)] + extra_reads, writes=[("hT", k)])
                    else:
                        S.op("dve", lambda e, k=k, pb=pb, sc=sc: e.tensor_scalar(
                            out=hT[:, k, :], in0=ps[pb][:, :], scalar1=sc, scalar2=None, op0=ALU.mult),
                            reads=[P(pb)] + extra_reads, writes=[("hT", k)])

        def postnorm_residual(buf, b, banks, s, am, store_rows=None):
            X = xt[buf]
            r = ring("ssy", 4)
            for h in range(2):
                S.op("act", lambda e, h=h: e.activation(out=junk[:, 0:512], in_=ps[banks[h]][:, :], func=AF.Square,
                                                        accum_out=ssy[r][:, h:h + 1]),
                     reads=[P(banks[h])], writes=["junk", ("ssy", r, h)])
            for h in range(2):
                S.op("dve", lambda e, h=h: e.tensor_tensor(
                    out=nb[:, b, h * 512:(h + 1) * 512], in0=ps[banks[h]][:, :],
                    in1=ggt[:, s, am, h * 512:(h + 1) * 512], op=ALU.mult),
                    reads=[P(banks[h])] + GGT_RES, writes=nbhalf(b, h))
            S.op("pool", lambda e: e.tensor_tensor(out=ssy[r][:, 2:3], in0=ssy[r][:, 0:1], in1=ssy[r][:, 1:2], op=ALU.add),
                 reads=[("ssy", r, 0), ("ssy", r, 1)], writes=[("ssy", r, 2)])
            S.op("pool", lambda e: e.tensor_scalar(out=ssy[r][:, 2:3], in0=ssy[r][:, 2:3], scalar1=1.0 / D, scalar2=EPS,
                                                   op0=ALU.mult, op1=ALU.add),
                 reads=[("ssy", r, 2)], writes=[("ssy", r, 2)])
            S.op("pool", lambda e: e.tensor_tensor(out=ssy[r][:, 3:4], in0=ssy[r][:, 2:3], in1=mhalf[:, 0:1], op=ALU.pow),
                 reads=[("ssy", r, 2), "mhalf"], writes=[("ssy", r, 3)])
            S.op("dve", lambda e: e.scalar_tensor_tensor(out=X[:, b, :], in0=nb[:, b, :], scalar=ssy[r][:, 3:4],
                                                         in1=X[:, b, :], op0=ALU.mult, op1=ALU.add),
                 reads=nbres(b) + [("xt", buf, b), ("ssy", r, 3)], writes=[("xt", buf, b)])
            if store_rows is not None:
                t = store_rows
                S.op("pool", lambda e: e.dma_start(out=out_v[t][:, b, :], in_=X[:, b, :]), reads=[("xt", buf, b)],
                     writes=[("out", t, b)], chan=("os", buf))
                stores_done.append(("out", t, b))

        NT = nt
        stores_done = []
        if stop != "prologue":
            load_x(0)
        for t in range(NT if stop != "prologue" else 0):
            s, tau = t // 4, t % 4
            buf = t % 2
            if t == 0:
                prenorm_stats(buf)
                prenorm_tr(s, 0, 1)
            if t == 0:
                dbg("hT", hT[:].rearrange("p a b -> p (a b)"), [128, 4096], BF16, [("hT", k) for k in range(8)])

            if stop == "A":
                break
            for n in range(5):
                ncol = 512 if n < 4 else 256
                sl = load_piece(n, 8 * ncol)
                wv = ws[sl][:, 0:8 * ncol].rearrange("p (k c) -> p k c", k=8)
                for b in range(4):
                    pb = (n * 4 + b) % 4
                    blk = tau * 4 + b

                    def mmf(e, b=b, pb=pb, wv=wv, ncol=ncol):
                        for k in range(8):
                            ins = e.matmul(ps[pb][:, 0:ncol], lhsT=hT[:, k, b * 128:(b + 1) * 128], rhs=wv[:, k, :],
                                           start=(k == 0), stop=(k == 7))
                        return ins
                    S.op("pe", mmf, reads=[("ws", sl)] + [("hT", k) for k in range(8)], writes=[P(pb)])
                    if n == 3:
                        S.op("act", lambda e, pb=pb, blk=blk: e.activation(
                            out=vB[:, blk, :, 0:64], in_=ps[pb][:, :].rearrange("p (h d) -> p h d", h=8), func=AF.Copy),
                            reads=[P(pb)], writes=[("vB", blk)])
                        continue
                    nh = 8 if n < 3 else 2
                    col0 = {0: 0, 1: 512, 2: 1024, 4: 1536}[n]
                    pv = ps[pb][:, 0:nh * 64].rearrange("p (h d) -> p h d", h=nh)
                    dst = qk[:, b, col0:col0 + nh * 64].rearrange("p (h d) -> p h d", h=nh)
                    ri = ring("rop", 4)
                    tA = ropA[ri][:, 0:nh * 16].rearrange("p (h d) -> p h d", h=nh)
                    tB = ropB[ri][:, 0:nh * 16].rearrange("p (h d) -> p h d", h=nh)
                    cidx = s * 16 + blk
                    rX = ropX[ri][:, 0:nh * 16].rearrange("p (h d) -> p h d", h=nh)
                    S.op("act", lambda e, pv=pv, rX=rX: e.activation(out=rX, in_=pv[:, :, 0:16], func=AF.Copy),
                         reads=[P(pb)], writes=[("PT", ri)])
                    S.op("pool", lambda e, rX=rX, tA=tA, nh=nh, cidx=cidx: e.tensor_tensor(
                        out=tA, in0=rX, in1=CC[:, cidx, :].unsqueeze(1).to_broadcast([128, nh, 16]),
                        op=ALU.mult), reads=[("PT", ri)] + ROPE_RES, writes=[("ropA", ri)])
                    S.op("pool", lambda e, rX=rX, tB=tB, nh=nh, cidx=cidx: e.tensor_tensor(
                        out=tB[:, :, 0:8], in0=rX[:, :, 8:16], in1=SS[:, cidx, 0:8].unsqueeze(1).to_broadcast([128, nh, 8]),
                        op=ALU.mult), reads=[("PT", ri)] + ROPE_RES, writes=[("ropB", ri, 0)])
                    S.op("pool", lambda e, rX=rX, tB=tB, nh=nh, cidx=cidx: e.tensor_tensor(
                        out=tB[:, :, 8:16], in0=rX[:, :, 0:8], in1=SS[:, cidx, 8:16].unsqueeze(1).to_broadcast([128, nh, 8]),
                        op=ALU.mult), reads=[("PT", ri)] + ROPE_RES, writes=[("ropB", ri, 1)])
                    S.op("pool", lambda e, dst=dst, tA=tA, tB=tB: e.tensor_tensor(out=dst[:, :, 0:16], in0=tA, in1=tB,
                                                                                  op=ALU.add),
                         reads=[("ropA", ri), ("ropB", ri, 0), ("ropB", ri, 1)], writes=[("qkr", b, n)])
                    S.op("act", lambda e, dst=dst, pv=pv: e.activation(out=dst[:, :, 16:64], in_=pv[:, :, 16:64],
                                                                       func=AF.Copy),
                         reads=[P(pb)], writes=[("qkc", b, n)])
                    if n == 4:
                        S.op("act", lambda e, pb=pb, blk=blk: e.activation(
                            out=vA[:, blk, :, 0:64], in_=ps[pb][:, 128:256].rearrange("p (h d) -> p h d", h=2),
                            func=AF.Copy), reads=[P(pb)], writes=[("vA", blk)])
            for cc in range(13):
                pb = 4 + cc % 2

                def trf(e, cc=cc, pb=pb):
                    for b in range(4):
                        ins = e.transpose(out=psb[pb][:, b * 128:(b + 1) * 128], in_=qk[:, b, cc * 128:(cc + 1) * 128],
                                          identity=identb[:])
                    return ins
                nn = 4 if cc == 12 else cc // 4
                S.op("pe", trf, reads=[(kind, b, nn) for b in range(4) for kind in ("qkr", "qkc")] + ["identb"],
                     writes=[P(pb)])
                if cc < 8:
                    dstT, wres = qT[:, cc, :], [("qT", cc)]
                elif cc < 12:
                    dstT, wres = kBT[:, cc - 8, tau * 512:(tau + 1) * 512], [("kBT", cc - 8, tau)]
                else:
                    dstT, wres = kAT[:, tau * 512:(tau + 1) * 512], [("kAT", tau)]
                if cc % 2 == 0:
                    S.op("act", lambda e, pb=pb, dstT=dstT: e.activation(out=dstT, in_=psb[pb][:, 0:512], func=AF.Copy),
                         reads=[P(pb)], writes=wres)
                else:
                    S.op("dve", lambda e, pb=pb, dstT=dstT: e.tensor_copy(out=dstT, in_=psb[pb][:, 0:512]),
                         reads=[P(pb)], writes=wres)
            if t == 0:
                dbg("qT", qT[:].rearrange("p a b -> p (a b)"), [128, 4096], BF16, [("qT", k) for k in range(8)])
                dbg("kAT", kAT[:, 0:512], [128, 512], BF16, [("kAT", 0)])
                dbg("vB", vB[:, 0:4, :, :].rearrange("p a b c -> p (a b c)"), [128, 4 * 8 * 65], BF16,
                    [("vB", i) for i in range(4)])

            if stop == "B":
                break
            tiles = []
            for g in range(2):
                for qb in range(4):
                    i = tau * 4 + qb
                    ob = 3 + ring("oA", 2)
                    js = [j for j in (i - 1, i) if j >= 0]
                    for jn, j in enumerate(js):
                        tiles.append(dict(kind="A", g=g, qb=qb, j=j, N=512, mask=(0 if j == i else 1), ob=ob,
                                          first=(jn == 0), last=(jn == len(js) - 1)))
            for h in range(8):
                ob = 5 + ring("oB", 2)
                nj = tau * 4 + 4
                for j in range(nj):
                    m = tau * 4 - j
                    q0 = max(0, -m) * 128
                    tiles.append(dict(kind="B", h=h, j=j, N=512 - q0, q0=q0, mi=(m + 3 if m <= 4 else 8), ob=ob,
                                      first=(j == 0), last=(j == nj - 1)))

            def issue_score(T):
                sb_ = (0, 1, 2, 7)[ring("sbank", 4)]
                pr = ring("PT", 4)
                T["pr"] = pr
                N = T["N"]
                if T["kind"] == "A":
                    g, qb, j = T["g"], T["qb"], T["j"]
                    S.op("pe", lambda e: e.matmul(
                        ps[sb_][:, 0:512].rearrange("p (c q) -> p c q", c=4),
                        lhsT=kAT[g * 64:(g + 1) * 64, j * 128:(j + 1) * 128],
                        rhs=qT[g * 64:(g + 1) * 64, 0:4, qb * 128:(qb + 1) * 128], start=True, stop=True),
                        reads=[("kAT", j // 4)] + [("qT", c) for c in range(4)], writes=[P(sb_)])
                else:
                    h, j, q0 = T["h"], T["j"], T["q0"]
                    hp, hc = (h % 2) * 64, h // 2
                    S.op("pe", lambda e: e.matmul(
                        ps[sb_][:, 0:N], lhsT=kBT[hp:hp + 64, hc, j * 128:(j + 1) * 128],
                        rhs=qT[hp:hp + 64, 4 + hc, q0:512], start=True, stop=True),
                        reads=[("kBT", hc, j // 4), ("qT", 4 + hc)], writes=[P(sb_)])
                S.op("act", lambda e: e.activation(out=PT[pr][:, 0:N], in_=ps[sb_][:, 0:N], func=AF.Exp, scale=0.125),
                     reads=[P(sb_)], writes=[("PT", pr)])
                if T["kind"] == "A":
                    S.op("dve", lambda e: e.tensor_tensor(
                        out=PT[pr][:, :].rearrange("p (c q) -> p c q", c=4),
                        in0=PT[pr][:, :].rearrange("p (c q) -> p c q", c=4),
                        in1=maskA[:, T["mask"], :].unsqueeze(1).to_broadcast([128, 4, 128]), op=ALU.mult),
                        reads=[("PT", pr), "maskA"], writes=[("PT", pr)])
                else:
                    S.op("dve", lambda e: e.tensor_tensor(out=PT[pr][:, 0:N], in0=PT[pr][:, 0:N],
                                                          in1=maskB[:, T["mi"], T["q0"]:512], op=ALU.mult),
                         reads=[("PT", pr), "maskB"], writes=[("PT", pr)])

            def issue_pv(T):
                pr, ob = T["pr"], T["ob"]
                if T["kind"] == "A":
                    g, qb, j = T["g"], T["qb"], T["j"]

                    def pvf(e):
                        for c in range(4):
                            ins = e.matmul(ps[ob][:, c * 65:(c + 1) * 65], lhsT=PT[pr][:, c * 128:(c + 1) * 128],
                                           rhs=vA[:, j, g, :], start=(T["first"] and c == 0), stop=(T["last"] and c == 3),
                                           skip_group_check=True)
                        return ins
                    S.op("pe", pvf, reads=[("PT", pr), ("vA", j)], writes=[P(ob)])
                    if T["last"]:
                        di = ring("den", 4)
                        psv = ps[ob][:, 0:260].rearrange("p (c e) -> p c e", c=4)
                        S.op("dve", lambda e: e.tensor_tensor(out=den[di][:, :].unsqueeze(2), in0=psv[:, :, 64:65],
                                                              in1=esink[:, 4 * g:4 * g + 4].unsqueeze(2), op=ALU.add),
                             reads=[P(ob), "esink"], writes=[("den", di)])
                        S.op("dve", lambda e: e.reciprocal(out=den[di][:, :], in_=den[di][:, :]), reads=[("den", di)],
                             writes=[("den", di)])
                        S.op("dve", lambda e: e.tensor_tensor(
                            out=nb[:, qb, g * 256:(g + 1) * 256].rearrange("p (c d) -> p c d", c=4),
                            in0=psv[:, :, 0:64], in1=den[di][:, :].unsqueeze(2).to_broadcast([128, 4, 64]), op=ALU.mult),
                            reads=[P(ob), ("den", di)], writes=[("nb", qb, g)])
                else:
                    h, j, q0 = T["h"], T["j"], T["q0"]
                    qb0 = q0 // 128

                    def pvf(e):
                        for qb in range(qb0, 4):
                            ins = e.matmul(ps[ob][:, qb * 65:(qb + 1) * 65],
                                           lhsT=PT[pr][:, qb * 128 - q0:(qb + 1) * 128 - q0], rhs=vB[:, j, h, :],
                                           start=(T["first"] and qb == qb0), stop=(T["last"] and qb == 3),
                                           skip_group_check=True)
                        return ins
                    S.op("pe", pvf, reads=[("PT", pr), ("vB", j)], writes=[P(ob)])
                    if T["last"]:
                        di = ring("den", 4)
                        psv = ps[ob][:, 0:260].rearrange("p (c e) -> p c e", c=4)
                        S.op("dve", lambda e: e.reciprocal(out=den[di][:, :].unsqueeze(2), in_=psv[:, :, 64:65]),
                             reads=[P(ob)], writes=[("den", di)])
                        S.op("dve", lambda e: e.tensor_tensor(
                            out=nb[:, :, 512 + h * 64:512 + (h + 1) * 64], in0=psv[:, :, 0:64],
                            in1=den[di][:, :].unsqueeze(2).to_broadcast([128, 4, 64]), op=ALU.mult),
                            reads=[P(ob), ("den", di)], writes=[("nb", b, 2 + h // 4) for b in range(4)])

            def phaseD_half(mx):
                for b in range(4):
                    S.op("act", lambda e, b=b: e.activation(
                        out=junk[:, 0:512], in_=nb[:, b, mx * 512:(mx + 1) * 512], func=AF.Square,
                        accum_out=st_ss[:, 4 * mx + b:4 * mx + b + 1]), reads=nbhalf(b, mx),
                        writes=["junk", ("ss", 4 * mx + b)])
                S.op("dve", lambda e: e.tensor_scalar(out=st_v[:, 4 * mx:4 * mx + 4], in0=st_ss[:, 4 * mx:4 * mx + 4],
                                                      scalar1=1.0 / 512, scalar2=EPS, op0=ALU.mult, op1=ALU.add),
                     reads=[("ss", 4 * mx + i) for i in range(4)], writes=[("stv", 4 * mx + i) for i in range(4)])
                S.op("pool", lambda e: e.tensor_tensor(out=st_r[:, 4 * mx:4 * mx + 4], in0=st_v[:, 4 * mx:4 * mx + 4],
                                                       in1=mhalf[:, 0:4], op=ALU.pow),
                     reads=[("stv", 4 * mx + i) for i in range(4)] + ["mhalf"],
                     writes=[("str", 4 * mx + i) for i in range(4)])
                for b in range(4):
                    S.op("dve", lambda e, b=b: e.tensor_scalar(
                        out=nb[:, b, mx * 512:(mx + 1) * 512], in0=nb[:, b, mx * 512:(mx + 1) * 512],
                        scalar1=st_r[:, 4 * mx + b:4 * mx + b + 1], scalar2=None, op0=ALU.mult),
                        reads=nbhalf(b, mx) + [("str", 4 * mx + b)], writes=nbhalf(b, mx))

            LOOK = 3
            n_a = sum(1 for T in tiles if T["kind"] == "A")
            for n in range(len(tiles) + LOOK):
                if n < len(tiles):
                    issue_score(tiles[n])
                if n >= LOOK:
                    issue_pv(tiles[n - LOOK])
                    if n - LOOK == n_a - 1:
                        phaseD_half(0)
            if t == 0:
                dbg("mix", nb.rearrange("p a b -> p (a b)"), [128, 4096], F32, [r for b in range(4) for r in nbres(b)])

            if stop == "C":
                break
            if t + 1 < NT:
                load_x(t + 1)
            phaseD_half(1)
            transposes_to_hT(lambda k: (gT[:, 2, k:k + 1], None), ["gT2"], fine=True)

            sls = [load_piece(5 + h) for h in range(2)]
            for b in range(4):
                banks = (2 + 2 * (b % 3), 3 + 2 * (b % 3))
                for h in range(2):
                    wv = ws[sls[h]][:, :].rearrange("p (k c) -> p k c", k=8)

                    def mmf(e, b=b, h=h, wv=wv, banks=banks):
                        for k in range(8):
                            ins = e.matmul(ps[banks[h]][:, :], lhsT=hT[:, k, b * 128:(b + 1) * 128], rhs=wv[:, k, :],
                                           start=(k == 0), stop=(k == 7))
                        return ins
                    S.op("pe", mmf, reads=[("ws", sls[h])] + [("hT", k) for k in range(8)], writes=[P(banks[h])])
                postnorm_residual(buf, b, banks, s, 0)
            if t == 0:
                dbg("x1", xt[buf][:].rearrange("p a b -> p (a b)"), [128, 4096], F32, [("xt", buf, b) for b in range(4)])

            if stop == "E":
                break
            prenorm_stats(buf)
            prenorm_tr(s, 2, 3)
            hoist = (t + 1 < NT) and stop is None

            for g8 in range(8):
                if g8 == 1 and hoist:
                    prenorm_stats((t + 1) % 2)
                sl = load_piece(7 + g8)
                wv = ws[sl][:, :].rearrange("p (k c) -> p k c", k=8)
                for fc in range(4):
                    f = g8 * 4 + fc
                    pb = f % 4

                    def mmf(e, fc=fc, pb=pb, wv=wv):
                        for k in range(8):
                            ins = e.matmul(ps[pb][:, :], lhsT=wv[:, k, fc * 128:(fc + 1) * 128], rhs=hT[:, k, :],
                                           start=(k == 0), stop=(k == 7))
                        return ins
                    S.op("pe", mmf, reads=[("ws", sl)] + [("hT", k) for k in range(8)], writes=[P(pb)])
                    ri = f % 2
                    S.op("act", lambda e, pb=pb, ri=ri: e.activation(out=rtmp[ri][:], in_=ps[pb][:, :], func=AF.Relu),
                         reads=[P(pb)], writes=[("rtmp", ri)])
                    S.op("pool", lambda e, f=f, ri=ri: e.tensor_tensor(out=uT[:, f, :], in0=rtmp[ri][:], in1=rtmp[ri][:],
                                                                       op=ALU.mult),
                         reads=[("rtmp", ri)], writes=[("uT", f)])

            if stop == "G":
                break
            if hoist:
                prenorm_tr((t + 1) // 4, 0, 1, banks=(4, 5))
            for g8 in range(8):
                sl = load_piece(15 + g8)
                wv = ws[sl][:, :].rearrange("p (f d) -> p f d", f=4)
                for fc in range(4):
                    f = g8 * 4 + fc

                    def mmf(e, f=f, fc=fc, wv=wv):
                        for b in range(4):
                            for h in range(2):
                                ins = e.matmul(ps[2 * b + h][:, :], lhsT=uT[:, f, b * 128:(b + 1) * 128],
                                               rhs=wv[:, fc, h * 512:(h + 1) * 512], start=(f == 0), stop=(f == 31))
                        return ins
                    S.op("pe", mmf, reads=[("ws", sl), ("uT", f)], writes=[P(i) for i in range(8)])

            for b in range(4):
                postnorm_residual(buf, b, (2 * b, 2 * b + 1), s, 1, store_rows=t)
            if stop == "T0":
                break

        S.op("pool", None, reads=stores_done + [("dbg", n) for n in dbg_outs])
        S.emit()
    return nc, list(dbg_outs.keys())


def _consts():
    bf = ml_dtypes.bfloat16
    k = np.arange(128)[:, None]
    qi = np.arange(512)[None, :]
    mB = np.zeros((128, 9, 512), np.float32)
    for mi in range(9):
        m = mi - 3 if mi < 8 else 5
        d = 128 * m + qi - k
        mult = ((d >= 0) & (d <= 128)).astype(np.float32) \
            + ((d >= 0) & (d % 4 == 0) & (d <= 512)).astype(np.float32) \
            + ((d >= 0) & (d % 16 == 0) & (d <= 2048)).astype(np.float32)
        mB[:, mi, :] = mult
    q = np.arange(128)[None, :]
    mA = np.zeros((128, 2, 128), np.float32)
    mA[:, 0, :] = (k <= q)
    mA[:, 1, :] = (k > q)
    sel = np.zeros((2, 2, 128), np.float32)
    sel[0, 0, :] = 1.0
    sel[1, 1, :] = 1.0
    invf = (500000.0 ** (-np.arange(0, 16, 2, dtype=np.float32) / 16.0)).astype(np.float32)[None, :]
    return dict(ident=np.eye(128, dtype=np.float32), maskB=mB.reshape(128, -1).astype(bf),
                maskA=mA.reshape(128, -1).astype(bf), sel=sel.reshape(2, -1), invf=invf)


_QA_PERM = [0, 4, 1, 5, 2, 6, 3, 7]


def _prep_shared(w_ada, b_ada, g_attn_pre, g_attn_post, w_in, sink_a, g_mix_a, g_mix_b, w_out, g_mlp_pre,
                 g_mlp_post, w_up, w_down):
    w_in0 = w_in[0]
    qa = w_in0[:, 0:512].reshape(D, 8, 64)[:, _QA_PERM, :].reshape(D, 512)
    ka, va = w_in0[:, 512:640], w_in0[:, 640:768]
    qb, kb, vb = w_in0[:, 768:1280], w_in0[:, 1280:1792], w_in0[:, 1792:2304]
    w_in_p = np.ascontiguousarray(np.concatenate([qa, qb, kb, vb, ka, va], axis=1))

    def colT(v):
        return np.ascontiguousarray(v.reshape(8, 128).T)
    sh = dict(w_ada=np.ascontiguousarray(w_ada[0]), b_ada=np.ascontiguousarray(b_ada[0:1]),
              gpreT=colT(g_attn_pre[0]), gmlpT=colT(g_mlp_pre[0]),
              gmixT=colT(np.concatenate([g_mix_a[0], g_mix_b[0]])),
              gposta=np.ascontiguousarray(g_attn_post[0:1]), gpostm=np.ascontiguousarray(g_mlp_post[0:1]),
              w_in=w_in_p, w_out=np.ascontiguousarray(w_out[0]), w_up=np.ascontiguousarray(w_up[0]),
              w_down=np.ascontiguousarray(w_down[0]), sink=np.ascontiguousarray(sink_a[0:1]))
    sh.update(_consts())
    return sh


_NC_CACHE = {}


def kernel(x, c, positions, w_ada, b_ada, g_attn_pre, g_attn_post, w_in, sink_a, g_mix_a, g_mix_b, w_out,
           g_mlp_pre, g_mlp_post, w_up, w_down, _debug=None):
    x = np.asarray(x, np.float32)
    c = np.asarray(c, np.float32)
    positions = np.asarray(positions, np.int32)
    args = [np.asarray(a, np.float32) for a in (w_ada, b_ada, g_attn_pre, g_attn_post, w_in, sink_a, g_mix_a,
                                                g_mix_b, w_out, g_mlp_pre, g_mlp_post, w_up, w_down)]
    shared = _prep_shared(*args)
    key = tuple(_debug) if _debug else None
    if key not in _NC_CACHE:
        _NC_CACHE[key] = build_nc(_debug)
    nc, dbg_names = _NC_CACHE[key]
    in_maps = []
    for i in range(NCORES):
        m = dict(shared)
        m["x"] = np.ascontiguousarray(x[2 * i:2 * i + 2].reshape(2 * SEQ, D))
        cc = c[2 * i:2 * i + 2]
        m["cT"] = np.ascontiguousarray(cc.reshape(2, 8, 128).transpose(2, 1, 0).reshape(128, 16))
        pp = positions[2 * i:2 * i + 2]
        m["pos"] = np.ascontiguousarray(pp.reshape(2, 16, 128).transpose(2, 0, 1).reshape(128, 32))
        in_maps.append(m)
    res = run_bass_kernel_spmd(nc, in_maps, core_ids=list(range(NCORES)))
    out = np.stack([r["out"].reshape(2, SEQ, D) for r in res.results], axis=0).reshape(2 * NCORES, SEQ, D)
    if _debug:
        return out.astype(np.float32), [{n: r["dbg_" + n] for n in dbg_names} for r in res.results]
    return out.astype(np.float32)
```

```python
import contextlib
import numpy as np
import ml_dtypes
import concourse.bass as bass
import concourse.mybir as mybir
from concourse.alu_op_type import AluOpType as ALU
from concourse.bass_utils import run_bass_kernel_spmd

F32 = mybir.dt.float32
BF16 = mybir.dt.bfloat16
I32 = mybir.dt.int32
AF = mybir.ActivationFunctionType

NCORES = 8
SEQ = 2048
D = 1024
DFF = 4096
EPS = 1e-6
NW = 3
TWO_PI_HI = 6.28125
TWO_PI_LO = 2.0 * np.pi - 6.28125
PI_F = float(np.float32(np.pi))


class Sched:
    ENGS = ("pe", "act", "dve", "pool", "sp")

    def __init__(self, nc):
        self.nc = nc
        self.ops = {e: [] for e in self.ENGS}
        self.cnt = {e: 0 for e in self.ENGS}
        self.chan_cnt = {}
        self.lastw = {}
        self.readers = {}
        self.seen = {e: {} for e in self.ENGS}

    def _dep(self, eng, tok, waits):
        semkey, val = tok[0], tok[1]
        if self.seen[eng].get(semkey, 0) >= val:
            return
        self.seen[eng][semkey] = val
        waits[semkey] = max(waits.get(semkey, 0), val)

    def op(self, eng, fn, reads=(), writes=(), chan=None):
        waits = {}
        if eng != "pe":
            extra = [r for r in reads if isinstance(r, tuple) and r[0] == "ps" and r not in writes]
            if extra:
                writes = list(writes) + extra
        for r in reads:
            t = self.lastw.get(r)
            if t is not None:
                if t[0][0] == "e" and t[2] == eng and eng == "pe":
                    continue
                self._dep(eng, t, waits)
        for w in writes:
            t = self.lastw.get(w)
            if t is not None and not (t[0][0] == "e" and t[2] == eng and eng == "pe"):
                self._dep(eng, t, waits)
            for t in self.readers.get(w, ()):
                if t[0][0] == "e" and t[2] == eng and eng == "pe":
                    continue
                self._dep(eng, t, waits)
        if chan is None:
            self.cnt[eng] += 1
            tok = (("e", eng), self.cnt[eng], eng)
        else:
            self.chan_cnt[chan] = self.chan_cnt.get(chan, 0) + 1
            tok = (("c", chan), 16 * self.chan_cnt[chan], eng)
        self.ops[eng].append(dict(fn=fn, waits=waits, chan=chan))
        for r in reads:
            self.readers.setdefault(r, []).append(tok)
        for w in writes:
            self.lastw[w] = tok
            self.readers[w] = []
        return tok

    def emit(self):
        nc = self.nc
        needed = {e: set() for e in self.ENGS}
        for e in self.ENGS:
            for o in self.ops[e]:
                for sk, v in o["waits"].items():
                    if sk[0] == "e":
                        needed[sk[1]].add(v)
        rank = {e: {v: i + 1 for i, v in enumerate(sorted(needed[e]))} for e in self.ENGS}
        with contextlib.ExitStack() as st:
            sems = {}
            for e in self.ENGS:
                sems[("e", e)] = st.enter_context(nc.semaphore("s_" + e))
            for i, c in enumerate(self.chan_cnt):
                sems[("c", c)] = st.enter_context(nc.semaphore("c%d" % i))
            block = st.enter_context(nc.Block())
            handles = dict(pe="tensor", act="scalar", dve="vector", pool="gpsimd", sp="sync")

            def make(e):
                def body(h):
                    ordinal = 0
                    for o in self.ops[e]:
                        for sk, v in o["waits"].items():
                            h.wait_ge(sems[sk], rank[sk[1]][v] if sk[0] == "e" else v)
                        if o["chan"] is None:
                            ordinal += 1
                        if o["fn"] is None:
                            continue
                        ins = o["fn"](h)
                        if o["chan"] is None:
                            if ordinal in rank[e]:
                                ins.then_inc(sems[("e", e)], 1)
                        else:
                            ins.then_inc(sems[("c", o["chan"])], 16)
                return body

            for e in self.ENGS:
                if self.ops[e]:
                    getattr(block, handles[e])(make(e))


def build_nc(debug=None, stop=None, nt=8):
    nc = bass.Bass("TRN2", target_bir_lowering=False)

    def din(name, shape, dt):
        return nc.dram_tensor(name, shape, dt, kind="ExternalInput").ap()

    x_d = din("x", [4096, D], F32)
    cT_d = din("cT", [128, 16], F32)
    pos_d = din("pos", [128, 32], I32)
    w_ada_d = din("w_ada", [D, 6 * D], F32)
    b_ada_d = din("b_ada", [1, 6 * D], F32)
    gpreT_d = din("gpreT", [128, 8], F32)
    gmlpT_d = din("gmlpT", [128, 8], F32)
    gmixT_d = din("gmixT", [128, 8], F32)
    gposta_d = din("gposta", [1, D], F32)
    gpostm_d = din("gpostm", [1, D], F32)
    w_in_d = din("w_in", [D, 2304], F32)
    w_out_d = din("w_out", [D, D], F32)
    w_up_d = din("w_up", [D, DFF], F32)
    w_down_d = din("w_down", [DFF, D], F32)
    sink_d = din("sink", [1, 8], F32)
    ident_d = din("ident", [128, 128], F32)
    maskB_d = din("maskB", [128, 9 * 512], BF16)
    maskA_d = din("maskA", [128, 256], BF16)
    sel_d = din("sel", [2, 256], F32)
    invf_d = din("invf", [1, 8], F32)
    out_d = nc.dram_tensor("out", [4096, D], F32, kind="ExternalOutput").ap()
    wscr = nc.dram_tensor("wscr", [23, 128, 4096], BF16, kind="Internal").ap()

    dbg_outs = {}
    S = Sched(nc)

    with contextlib.ExitStack() as st:
        def sb(name, shape, dt):
            return st.enter_context(nc.sbuf_tensor(name, shape, dt))

        def psum(name, shape, dt):
            return st.enter_context(nc.psum_tensor(name, shape, dt))

        xt = [sb("xt%d" % i, [128, 4, D], F32) for i in range(2)]
        big = sb("big", [128, 16384], BF16)
        uT = big[:].rearrange("p (c t) -> p c t", c=32)
        nbt = sb("nbt", [128, 4, D], F32)
        nb = nbt[:]
        qk = big[:, 8192:16384].rearrange("p (b d) -> p b d", b=4)
        stage = [big[:, 8192 * i:8192 * (i + 1)].bitcast(F32) for i in range(2)]
        junk = sb("junk", [128, D], BF16)
        hT = sb("hT", [128, 8, 512], BF16)
        qT = sb("qT", [128, 8, 512], BF16)
        kAT = sb("kAT", [128, SEQ], BF16)
        kBT = sb("kBT", [128, 4, SEQ], BF16)
        vA = sb("vA", [128, 16, 2, 65], BF16)
        vB = sb("vB", [128, 16, 8, 65], BF16)
        rtmp = [sb("rtmp%d" % i, [128, 512], F32) for i in range(2)]
        PT = [sb("PT%d" % i, [128, 512], BF16) for i in range(6)]
        ws = [sb("ws%d" % i, [128, 4096], BF16) for i in range(NW)]
        maskB = sb("maskB_s", [128, 9, 512], BF16)
        maskA = sb("maskA_s", [128, 2, 128], BF16)
        ident = sb("ident_s", [128, 128], F32)
        identb = sb("identb", [128, 128], BF16)
        ggt = sb("ggt", [128, 2, 2, D], F32)
        modT = sb("modT", [128, 2, 4, 8], F32)
        gT = sb("gT", [128, 3, 8], F32)
        CC = sb("CC", [128, 32, 16], F32)
        SS = sb("SS", [128, 32, 16], F32)
        esink = sb("esink", [128, 8], F32)
        mhalf = sb("mhalf", [128, 8], F32)
        zb = sb("zb", [128, 1], F32)
        cT = sb("cT_s", [128, 16], F32)
        cond = sb("cond", [128, 16], F32)
        ctmp = sb("ctmp", [128, 16], F32)
        posi = sb("posi", [128, 32], I32)
        posf = sb("posf", [128, 32], F32)
        invf = sb("invf_s", [128, 8], F32)
        sel = sb("sel_s", [2, 2, 128], F32)
        st_ss = sb("st_ss", [128, 8], F32)
        st_v = sb("st_v", [128, 8], F32)
        st_r = sb("st_r", [128, 8], F32)
        ssy = [sb("ssy%d" % i, [128, 4], F32) for i in range(4)]
        den = [sb("den%d" % i, [128, 4], F32) for i in range(4)]
        ropB = [sb("ropB%d" % i, [128, 128], F32) for i in range(4)]
        ropX = [PT[i][:, 0:256].bitcast(F32) for i in range(4)]
        ropA = [PT[i][:, 256:512].bitcast(F32) for i in range(4)]
        rpro = qT[:, 0:4, :].rearrange("p a b -> p (a b)").bitcast(F32).rearrange("p (a b) -> p a b", a=4)
        ang = rpro[:, 0, :].rearrange("p (a b) -> p a b", b=8)
        ra = rpro[:, 1, :].rearrange("p (a b) -> p a b", b=8)
        rb = rpro[:, 2, :].rearrange("p (a b) -> p a b", b=8)
        rki = rpro[:, 3, :].bitcast(I32).rearrange("p (a b) -> p a b", b=8)

        ps = [psum("ps%d" % i, [128, 512], F32) for i in range(8)]
        psb = [p[:].bitcast(BF16) for p in ps]

        gpb = xt[1]
        modp = [rtmp[i][0:2, :] for i in range(2)]
        bada = junk[0:2, :].bitcast(F32)

        def P(i):
            return ("ps", i)

        def nbres(b):
            return [("nb", b, i) for i in range(4)]

        def nbhalf(b, mx):
            return [("nb", b, 2 * mx), ("nb", b, 2 * mx + 1)]

        def qkres(b):
            return [("uT", 16 + 4 * b + i) for i in range(4)]

        def stres(i):
            return [("uT", 16 * i + j) for j in range(16)]

        rings = {}

        def ring(name, n):
            v = rings.get(name, 0)
            rings[name] = v + 1
            return v % n

        def dbg(name, src_ap, shape, dt, reads):
            if debug is None or name not in debug:
                return
            d = nc.dram_tensor("dbg_" + name, shape, dt, kind="ExternalOutput").ap()
            dbg_outs[name] = d
            S.op("sp", lambda e: e.dma_start(out=d, in_=src_ap), reads=reads, writes=[("dbg", name)],
                 chan=("dbg", name))

        def ld(dst, src, res, ch):
            S.op("sp", lambda e: e.dma_start(out=dst, in_=src), writes=[res], chan=ch)

        ld(ident[:], ident_d, "ident", "k0")
        ld(maskB[:].rearrange("p a b -> p (a b)"), maskB_d, "maskB", "k1")
        ld(maskA[:].rearrange("p a b -> p (a b)"), maskA_d, "maskA", "k2")
        ld(sel[:].rearrange("p a b -> p (a b)"), sel_d, "sel", "k3")
        ld(invf[:], invf_d.partition_broadcast(128), "invf", "k4")
        ld(esink[:], sink_d.partition_broadcast(128), "esink0", "k5")
        ld(cT[:], cT_d, "cT", "k6")
        ld(posi[:], pos_d, "posi", "k7")
        ld(gT[:, 0, :], gpreT_d, "gT0", "k8")
        ld(gT[:, 1, :], gmlpT_d, "gT1", "k9")
        ld(gT[:, 2, :], gmixT_d, "gT2", "k10")
        ld(gpb[:, 0, :], gposta_d.partition_broadcast(128), ("xt", 1, 0), "k11")
        ld(gpb[:, 1, :], gpostm_d.partition_broadcast(128), ("xt", 1, 1), "k12")

        S.op("pool", lambda e: e.memset(mhalf[:], -0.5), writes=["mhalf"])
        S.op("pool", lambda e: e.memset(zb[:], 0.0), writes=["zb"])
        S.op("pool", lambda e: e.memset(vA[:, :, :, 64:65], 1.0), writes=[("vA", i) for i in range(16)])
        S.op("pool", lambda e: e.memset(vB[:, :, :, 64:65], 1.0), writes=[("vB", i) for i in range(16)])
        def piece_src(q):
            if q < 5:
                w = 512 if q < 4 else 256
                return w_in_d.rearrange("(k p) c -> p k c", p=128)[:, :, q * 512:q * 512 + w], (8, w)
            if q < 7:
                h = q - 5
                return w_out_d.rearrange("(k p) c -> p k c", p=128)[:, :, h * 512:(h + 1) * 512], (8, 512)
            if q < 15:
                g = q - 7
                return w_up_d.rearrange("(k p) c -> p k c", p=128)[:, :, g * 512:(g + 1) * 512], (8, 512)
            g = q - 15
            return w_down_d[g * 512:(g + 1) * 512, :].rearrange("(f p) d -> p f d", p=128), (4, D)

        for q in range(23):
            src, (a, bdim) = piece_src(q)
            dstv = wscr[q, :, 0:a * bdim].rearrange("p (a b) -> p a b", a=a)
            S.op("pool", lambda e, src=src, dstv=dstv: e.dma_start(out=dstv, in_=src),
                 writes=[("wscr", q), ("cvthrottle", q % 3)], chan=("cv", q))

        S.op("dve", lambda e: e.tensor_copy(out=identb[:], in_=ident[:]), reads=["ident"], writes=["identb"])
        S.op("act", lambda e: e.activation(out=esink[:], in_=esink[:], func=AF.Exp), reads=["esink0"],
             writes=["esink"])

        S.op("act", lambda e: e.activation(out=ctmp[:], in_=cT[:], func=AF.Exp, scale=-1.0), reads=["cT"],
             writes=["ctmp"])
        S.op("dve", lambda e: e.tensor_scalar(out=ctmp[:], in0=ctmp[:], scalar1=1.0, scalar2=None, op0=ALU.add),
             reads=["ctmp"], writes=["ctmp"])
        S.op("dve", lambda e: e.reciprocal(out=ctmp[:], in_=ctmp[:]), reads=["ctmp"], writes=["ctmp"])
        S.op("dve", lambda e: e.tensor_tensor(out=cond[:], in0=cT[:], in1=ctmp[:], op=ALU.mult),
             reads=["ctmp", "cT"], writes=["cond"])
        condv = cond[:].rearrange("p (k b) -> p k b", b=2)

        S.op("dve", lambda e: e.tensor_copy(out=posf[:], in_=posi[:]), reads=["posi"], writes=["posf"])
        S.op("dve", lambda e: e.tensor_tensor(out=ang, in0=posf[:].unsqueeze(2).to_broadcast([128, 32, 8]),
                                              in1=invf[:].unsqueeze(1).to_broadcast([128, 32, 8]), op=ALU.mult),
             reads=["posf", "invf"], writes=[("qT", 0)])
        S.op("dve", lambda e: e.tensor_scalar(out=ra, in0=ang, scalar1=float(1.0 / (2 * np.pi)), scalar2=None,
                                              op0=ALU.mult), reads=[("qT", 0)], writes=[("qT", 1)])
        S.op("dve", lambda e: e.tensor_copy(out=rki, in_=ra), reads=[("qT", 1)], writes=[("qT", 3)])
        S.op("dve", lambda e: e.tensor_copy(out=ra, in_=rki), reads=[("qT", 3)], writes=[("qT", 1)])
        S.op("dve", lambda e: e.scalar_tensor_tensor(out=rb, in0=ra, scalar=-TWO_PI_HI, in1=ang,
                                                     op0=ALU.mult, op1=ALU.add), reads=[("qT", 1), ("qT", 0)], writes=[("qT", 2)])
        S.op("dve", lambda e: e.scalar_tensor_tensor(out=rb, in0=ra, scalar=-TWO_PI_LO, in1=rb,
                                                     op0=ALU.mult, op1=ALU.add), reads=[("qT", 1), ("qT", 2)], writes=[("qT", 2)])

        def wrap(buf, tmp, name, tname):
            S.op("dve", lambda e: e.tensor_scalar(out=tmp, in0=buf, scalar1=PI_F, scalar2=-2.0 * np.pi,
                                                  op0=ALU.is_gt, op1=ALU.mult), reads=[name], writes=[tname])
            S.op("dve", lambda e: e.tensor_tensor(out=buf, in0=buf, in1=tmp, op=ALU.add),
                 reads=[name, tname], writes=[name])
            S.op("dve", lambda e: e.tensor_scalar(out=tmp, in0=buf, scalar1=-PI_F, scalar2=2.0 * np.pi,
                                                  op0=ALU.is_lt, op1=ALU.mult), reads=[name], writes=[tname])
            S.op("dve", lambda e: e.tensor_tensor(out=buf, in0=buf, in1=tmp, op=ALU.add),
                 reads=[name, tname], writes=[name])

        wrap(rb, ra, ("qT", 2), ("qT", 1))
        S.op("act", lambda e: e.activation(out=SS[:, :, 8:16], in_=rb, func=AF.Sin), reads=[("qT", 2)], writes=["SSb"])
        S.op("dve", lambda e: e.tensor_scalar(out=SS[:, :, 0:8], in0=SS[:, :, 8:16], scalar1=-1.0, scalar2=None,
                                              op0=ALU.mult), reads=["SSb"], writes=["SSa"])
        S.op("dve", lambda e: e.tensor_scalar(out=rb, in0=rb, scalar1=float(np.pi / 2), scalar2=None,
                                              op0=ALU.add), reads=[("qT", 2), "SSb"], writes=[("qT", 2)])
        wrap(rb, ra, ("qT", 2), ("qT", 1))
        S.op("act", lambda e: e.activation(out=CC[:, :, 0:8], in_=rb, func=AF.Sin), reads=[("qT", 2)], writes=["CCa"])
        S.op("dve", lambda e: e.tensor_copy(out=CC[:, :, 8:16], in_=CC[:, :, 0:8]), reads=["CCa"], writes=["CCb"])
        ROPE_RES = ["CCa", "CCb", "SSa", "SSb"]

        w_ada_v = w_ada_d.rearrange("(k p) c -> p k c", p=128)
        for n in range(12):
            si = n % 2
            stg = stage[si].rearrange("p (k c) -> p k c", k=8)
            S.op("sp", lambda e, stg=stg, n=n: e.dma_start(out=stg, in_=w_ada_v[:, :, n * 512:(n + 1) * 512]),
                 writes=stres(si), chan=("stg", si))
            S.op("sp", lambda e, n=n: e.dma_start(out=bada, in_=b_ada_d[:, n * 512:(n + 1) * 512].partition_broadcast(2)),
                 writes=["junk"], chan="bada")
            pm = 2 + (n % 2)

            def mmf(e, stg=stg, pm=pm):
                for k in range(8):
                    ins = e.matmul(ps[pm][0:2, :], lhsT=condv[:, k, :], rhs=stg[:, k, :], start=(k == 0), stop=(k == 7))
                return ins
            S.op("pe", mmf, reads=stres(si) + ["cond"], writes=[P(pm)])
            mp = modp[n % 2]
            S.op("dve", lambda e, mp=mp, pm=pm: e.tensor_tensor(out=mp, in0=ps[pm][0:2, :], in1=bada, op=ALU.add),
                 reads=[P(pm), "junk"], writes=[("rtmp", n % 2)])
            v, half = n // 2, n % 2
            if v in (2, 5):
                am = 0 if v == 2 else 1
                for s in range(2):
                    pg = 4 + s
                    S.op("pe", lambda e, mp=mp, s=s, pg=pg: e.matmul(ps[pg][:, :], lhsT=sel[:, s, :], rhs=mp,
                                                                       start=True, stop=True),
                         reads=[("rtmp", n % 2), "sel"], writes=[P(pg)])
                    S.op("dve", lambda e, s=s, pg=pg, am=am, half=half: e.tensor_tensor(
                        out=ggt[:, s, am, half * 512:(half + 1) * 512], in0=ps[pg][:, :],
                        in1=gpb[:, am, half * 512:(half + 1) * 512], op=ALU.mult),
                        reads=[P(pg), ("xt", 1, 0), ("xt", 1, 1)], writes=[("ggt", s, am, half)])
            else:
                vi = {0: 0, 1: 1, 3: 2, 4: 3}[v]
                pg = 6

                def trf(e, mp=mp):
                    for q in range(4):
                        ins = e.transpose(out=ps[pg][:, 2 * q:2 * q + 2], in_=mp[:, q * 128:(q + 1) * 128],
                                          identity=ident[0:2, 0:2])
                    return ins
                S.op("pe", trf, reads=[("rtmp", n % 2), "ident"], writes=[P(pg)])
                S.op("dve", lambda e, vi=vi, half=half: e.tensor_copy(
                    out=modT[:, :, vi, half * 4:half * 4 + 4],
                    in_=ps[pg][:, 0:8].rearrange("p (q s) -> p s q", s=2)),
                    reads=[P(pg)], writes=[("modT", vi, half)])
        for vi, gi in ((1, 0), (3, 1)):
            S.op("dve", lambda e, vi=vi: e.tensor_scalar(out=modT[:, :, vi, :], in0=modT[:, :, vi, :], scalar1=1.0,
                                                         scalar2=None, op0=ALU.add),
                 reads=[("modT", vi, 0), ("modT", vi, 1)], writes=[("modT", vi, 0), ("modT", vi, 1)])
            S.op("dve", lambda e, vi=vi, gi=gi: e.tensor_tensor(
                out=modT[:, :, vi, :], in0=modT[:, :, vi, :], in1=gT[:, gi, :].unsqueeze(1).to_broadcast([128, 2, 8]),
                op=ALU.mult), reads=[("modT", vi, 0), ("modT", vi, 1), "gT%d" % gi],
                writes=[("modT", vi, 0), ("modT", vi, 1)])
        MODT_RES = [("modT", vi, h) for vi in range(4) for h in range(2)]
        GGT_RES = [("ggt", s, am, h) for s in range(2) for am in range(2) for h in range(2)]

        wstate = {"n": 0}

        def load_piece(q, n_el=4096):
            sl = wstate["n"] % NW
            wstate["n"] += 1
            S.op("sp", lambda e: e.dma_start(out=ws[sl][:, 0:n_el], in_=wscr[q, :, 0:n_el]), reads=[("wscr", q)],
                 writes=[("ws", sl)], chan=("wl", sl))
            return sl

        x_v = x_d.rearrange("(t b p) d -> t p b d", p=128, b=4)
        out_v = out_d.rearrange("(t b p) d -> t p b d", p=128, b=4)

        def load_x(t):
            buf = t % 2
            S.op("sp", lambda e: e.dma_start(out=xt[buf][:], in_=x_v[t]), writes=[("xt", buf, b) for b in range(4)],
                 chan=("xl", buf))

        def prenorm_stats(buf):
            X = xt[buf]
            for b in range(4):
                S.op("act", lambda e, b=b: e.activation(out=junk[:], in_=X[:, b, :], func=AF.Square,
                                                        accum_out=st_ss[:, b:b + 1]),
                     reads=[("xt", buf, b)], writes=["junk", ("ss", b)])
                S.op("pool", lambda e, b=b: e.tensor_scalar(out=st_v[:, b:b + 1], in0=st_ss[:, b:b + 1], scalar1=1.0 / D,
                                                            scalar2=EPS, op0=ALU.mult, op1=ALU.add),
                     reads=[("ss", b)], writes=[("stv", b)])
                S.op("pool", lambda e, b=b: e.tensor_tensor(out=st_r[:, b:b + 1], in0=st_v[:, b:b + 1],
                                                            in1=mhalf[:, 0:1], op=ALU.pow),
                     reads=[("stv", b), "mhalf"], writes=[("str", b)])
                if b % 2 == 0:
                    S.op("dve", lambda e, b=b: e.tensor_scalar(out=nb[:, b, :], in0=X[:, b, :], scalar1=st_r[:, b:b + 1],
                                                               scalar2=None, op0=ALU.mult),
                         reads=[("xt", buf, b), ("str", b)], writes=nbres(b))
                else:
                    S.op("act", lambda e, b=b: e.activation(out=nb[:, b, :], in_=X[:, b, :], func=AF.Identity,
                                                            scale=st_r[:, b:b + 1], bias=zb[:]),
                         reads=[("xt", buf, b), ("str", b), "zb"], writes=nbres(b))

        def prenorm_tr(s, vi_sh, vi_gsc, banks=(0, 1)):
            transposes_to_hT(lambda k: (modT[:, s, vi_gsc, k:k + 1], modT[:, s, vi_sh, k:k + 1]), MODT_RES, banks=banks)

        def transposes_to_hT(scale_bias, extra_reads, fine=False, banks=(0, 1)):
            for k in range(8):
                pb = banks[k % 2]

                def trf(e, k=k, pb=pb):
                    for b in range(4):
                        ins = e.transpose(out=ps[pb][:, b * 128:(b + 1) * 128], in_=nb[:, b, k * 128:(k + 1) * 128],
                                          identity=ident[:])
                    return ins
                if fine:
                    rr = [("nb", b, k // 2) for b in range(4)]
                else:
                    rr = [r for b in range(4) for r in nbres(b)]
                S.op("pe", trf, reads=rr + ["ident"], writes=[P(pb)])
                sc, bi = scale_bias(k)
                if k % 2 == 0:
                    S.op("act", lambda e, k=k, pb=pb, sc=sc, bi=bi: e.activation(
                        out=hT[:, k, :], in_=ps[pb][:, :], func=AF.Identity, scale=sc, bias=(bi if bi is not None else zb[:])),
                        reads=[P(pb), "zb"] + extra_reads, writes=[("hT", k)])
                else:
                    if bi is not None:
                        S.op("dve", lambda e, k=k, pb=pb, sc=sc, bi=bi: e.tensor_scalar(
                            out=hT[:, k, :], in0=ps[pb][:, :], scalar1=sc, scalar2=bi, op0=ALU.mult, op1=ALU.add),
                            reads=[P(pb)] + extra_reads, writes=[("hT", k)])
                    else:
                        S.op("dve", lambda e, k=k, pb=pb, sc=sc: e.tensor_scalar(
                            out=hT[:, k, :], in0=ps[pb][:, :], scalar1=sc, scalar2=None, op0=ALU.mult),
                            reads=[P(pb)] + extra_reads, writes=[("hT", k)])

        def postnorm_residual(buf, b, banks, s, am, store_rows=None):
            X = xt[buf]
            r = ring("ssy", 4)
            for h in range(2):
                S.op("act", lambda e, h=h: e.activation(out=junk[:, 0:512], in_=ps[banks[h]][:, :], func=AF.Square,
                                                        accum_out=ssy[r][:, h:h + 1]),
                     reads=[P(banks[h])], writes=["junk", ("ssy", r, h)])
            for h in range(2):
                S.op("dve", lambda e, h=h: e.tensor_tensor(
                    out=nb[:, b, h * 512:(h + 1) * 512], in0=ps[banks[h]][:, :],
                    in1=ggt[:, s, am, h * 512:(h + 1) * 512], op=ALU.mult),
                    reads=[P(banks[h])] + GGT_RES, writes=nbhalf(b, h))
            S.op("pool", lambda e: e.tensor_tensor(out=ssy[r][:, 2:3], in0=ssy[r][:, 0:1], in1=ssy[r][:, 1:2], op=ALU.add),
                 reads=[("ssy", r, 0), ("ssy", r, 1)], writes=[("ssy", r, 2)])
            S.op("pool", lambda e: e.tensor_scalar(out=ssy[r][:, 2:3], in0=ssy[r][:, 2:3], scalar1=1.0 / D, scalar2=EPS,
                                                   op0=ALU.mult, op1=ALU.add),
                 reads=[("ssy", r, 2)], writes=[("ssy", r, 2)])
            S.op("pool", lambda e: e.tensor_tensor(out=ssy[r][:, 3:4], in0=ssy[r][:, 2:3], in1=mhalf[:, 0:1], op=ALU.pow),
                 reads=[("ssy", r, 2), "mhalf"], writes=[("ssy", r, 3)])
            S.op("dve", lambda e: e.scalar_tensor_tensor(out=X[:, b, :], in0=nb[:, b, :], scalar=ssy[r][:, 3:4],
                                                         in1=X[:, b, :], op0=ALU.mult, op1=ALU.add),
                 reads=nbres(b) + [("xt", buf, b), ("ssy", r, 3)], writes=[("xt", buf, b)])
            if store_rows is not None:
                t = store_rows
                S.op("pool", lambda e: e.dma_start(out=out_v[t][:, b, :], in_=X[:, b, :]), reads=[("xt", buf, b)],
                     writes=[("out", t, b)], chan=("os", buf))
                stores_done.append(("out", t, b))

        NT = nt
        stores_done = []
        if stop != "prologue":
            load_x(0)
        for t in range(NT if stop != "prologue" else 0):
            s, tau = t // 4, t % 4
            buf = t % 2
            if t == 0:
                prenorm_stats(buf)
                prenorm_tr(s, 0, 1)
            if t == 0:
                dbg("hT", hT[:].rearrange("p a b -> p (a b)"), [128, 4096], BF16, [("hT", k) for k in range(8)])

            if stop == "A":
                break
            for n in range(5):
                ncol = 512 if n < 4 else 256
                sl = load_piece(n, 8 * ncol)
                wv = ws[sl][:, 0:8 * ncol].rearrange("p (k c) -> p k c", k=8)
                for b in range(4):
                    pb = (n * 4 + b) % 4
                    blk = tau * 4 + b

                    def mmf(e, b=b, pb=pb, wv=wv, ncol=ncol):
                        for k in range(8):
                            ins = e.matmul(ps[pb][:, 0:ncol], lhsT=hT[:, k, b * 128:(b + 1) * 128], rhs=wv[:, k, :],
                                           start=(k == 0), stop=(k == 7))
                        return ins
                    S.op("pe", mmf, reads=[("ws", sl)] + [("hT", k) for k in range(8)], writes=[P(pb)])
                    if n == 3:
                        S.op("act", lambda e, pb=pb, blk=blk: e.activation(
                            out=vB[:, blk, :, 0:64], in_=ps[pb][:, :].rearrange("p (h d) -> p h d", h=8), func=AF.Copy),
                            reads=[P(pb)], writes=[("vB", blk)])
                        continue
                    nh = 8 if n < 3 else 2
                    col0 = {0: 0, 1: 512, 2: 1024, 4: 1536}[n]
                    pv = ps[pb][:, 0:nh * 64].rearrange("p (h d) -> p h d", h=nh)
                    dst = qk[:, b, col0:col0 + nh * 64].rearrange("p (h d) -> p h d", h=nh)
                    ri = ring("rop", 4)
                    tA = ropA[ri][:, 0:nh * 16].rearrange("p (h d) -> p h d", h=nh)
                    tB = ropB[ri][:, 0:nh * 16].rearrange("p (h d) -> p h d", h=nh)
                    cidx = s * 16 + blk
                    rX = ropX[ri][:, 0:nh * 16].rearrange("p (h d) -> p h d", h=nh)
                    S.op("act", lambda e, pv=pv, rX=rX: e.activation(out=rX, in_=pv[:, :, 0:16], func=AF.Copy),
                         reads=[P(pb)], writes=[("PT", ri)])
                    S.op("pool", lambda e, rX=rX, tA=tA, nh=nh, cidx=cidx: e.tensor_tensor(
                        out=tA, in0=rX, in1=CC[:, cidx, :].unsqueeze(1).to_broadcast([128, nh, 16]),
                        op=ALU.mult), reads=[("PT", ri)] + ROPE_RES, writes=[("PT", ri)])
                    S.op("pool", lambda e, rX=rX, tB=tB, nh=nh, cidx=cidx: e.tensor_tensor(
                        out=tB[:, :, 0:8], in0=rX[:, :, 8:16], in1=SS[:, cidx, 0:8].unsqueeze(1).to_broadcast([128, nh, 8]),
                        op=ALU.mult), reads=[("PT", ri)] + ROPE_RES, writes=[("ropB", ri, 0)])
                    S.op("pool", lambda e, rX=rX, tB=tB, nh=nh, cidx=cidx: e.tensor_tensor(
                        out=tB[:, :, 8:16], in0=rX[:, :, 0:8], in1=SS[:, cidx, 8:16].unsqueeze(1).to_broadcast([128, nh, 8]),
                        op=ALU.mult), reads=[("PT", ri)] + ROPE_RES, writes=[("ropB", ri, 1)])
                    S.op("pool", lambda e, dst=dst, tA=tA, tB=tB: e.tensor_tensor(out=dst[:, :, 0:16], in0=tA, in1=tB,
                                                                                  op=ALU.add),
                         reads=[("PT", ri), ("ropB", ri, 0), ("ropB", ri, 1)], writes=[("qkr", b, n)])
                    S.op("act", lambda e, dst=dst, pv=pv: e.activation(out=dst[:, :, 16:64], in_=pv[:, :, 16:64],
                                                                       func=AF.Copy),
                         reads=[P(pb)], writes=[("qkc", b, n)])
                    if n == 4:
                        S.op("act", lambda e, pb=pb, blk=blk: e.activation(
                            out=vA[:, blk, :, 0:64], in_=ps[pb][:, 128:256].rearrange("p (h d) -> p h d", h=2),
                            func=AF.Copy), reads=[P(pb)], writes=[("vA", blk)])
            for cc in range(13):
                pb = 4 + cc % 2

                def trf(e, cc=cc, pb=pb):
                    for b in range(4):
                        ins = e.transpose(out=psb[pb][:, b * 128:(b + 1) * 128], in_=qk[:, b, cc * 128:(cc + 1) * 128],
                                          identity=identb[:])
                    return ins
                nn = 4 if cc == 12 else cc // 4
                S.op("pe", trf, reads=[(kind, b, nn) for b in range(4) for kind in ("qkr", "qkc")] + ["identb"],
                     writes=[P(pb)])
                if cc < 8:
                    dstT, wres = qT[:, cc, :], [("qT", cc)]
                elif cc < 12:
                    dstT, wres = kBT[:, cc - 8, tau * 512:(tau + 1) * 512], [("kBT", cc - 8, tau)]
                else:
                    dstT, wres = kAT[:, tau * 512:(tau + 1) * 512], [("kAT", tau)]
                if cc % 2 == 0:
                    S.op("act", lambda e, pb=pb, dstT=dstT: e.activation(out=dstT, in_=psb[pb][:, 0:512], func=AF.Copy),
                         reads=[P(pb)], writes=wres)
                else:
                    S.op("dve", lambda e, pb=pb, dstT=dstT: e.tensor_copy(out=dstT, in_=psb[pb][:, 0:512]),
                         reads=[P(pb)], writes=wres)
            if t == 0:
                dbg("qT", qT[:].rearrange("p a b -> p (a b)"), [128, 4096], BF16, [("qT", k) for k in range(8)])
                dbg("kAT", kAT[:, 0:512], [128, 512], BF16, [("kAT", 0)])
                dbg("vB", vB[:, 0:4, :, :].rearrange("p a b c -> p (a b c)"), [128, 4 * 8 * 65], BF16,
                    [("vB", i) for i in range(4)])

            if stop == "B":
                break
            tiles = []
            for qb in range(4):
                i = tau * 4 + qb
                obs = [(3, 4, 5, 6)[ring("oA", 4)] for _ in range(2)]
                js = [j for j in (i - 1, i) if j >= 0]
                for jn, j in enumerate(js):
                    for g in range(2):
                        tiles.append(dict(kind="A", g=g, qb=qb, j=j, N=512, mask=(0 if j == i else 1), ob=obs[g],
                                          first=(jn == 0), last=(jn == len(js) - 1)))
            for hc in range(4):
                obs = ((5, 6), (3, 4))[ring("oB", 2)]
                nj = tau * 4 + 4
                for j in range(nj):
                    m = tau * 4 - j
                    q0 = max(0, -m) * 128
                    for hh in range(2):
                        tiles.append(dict(kind="B", h=2 * hc + hh, j=j, N=512 - q0, q0=q0,
                                          mi=(m + 3 if m <= 4 else 8), ob=obs[hh], first=(j == 0), last=(j == nj - 1)))

            def issue_score(T):
                sb_ = (0, 1, 2, 7)[ring("sbank", 4)]
                pr = ring("PT", 6)
                T["pr"] = pr
                N = T["N"]
                if T["kind"] == "A":
                    g, qb, j = T["g"], T["qb"], T["j"]
                    S.op("pe", lambda e: e.matmul(
                        ps[sb_][:, 0:512].rearrange("p (c q) -> p c q", c=4),
                        lhsT=kAT[g * 64:(g + 1) * 64, j * 128:(j + 1) * 128],
                        rhs=qT[g * 64:(g + 1) * 64, 0:4, qb * 128:(qb + 1) * 128], start=True, stop=True),
                        reads=[("kAT", j // 4)] + [("qT", c) for c in range(4)], writes=[P(sb_)])
                else:
                    h, j, q0 = T["h"], T["j"], T["q0"]
                    hp, hc = (h % 2) * 64, h // 2
                    S.op("pe", lambda e: e.matmul(
                        ps[sb_][:, 0:N], lhsT=kBT[hp:hp + 64, hc, j * 128:(j + 1) * 128],
                        rhs=qT[hp:hp + 64, 4 + hc, q0:512], start=True, stop=True),
                        reads=[("kBT", hc, j // 4), ("qT", 4 + hc)], writes=[P(sb_)])
                S.op("act", lambda e: e.activation(out=PT[pr][:, 0:N], in_=ps[sb_][:, 0:N], func=AF.Exp, scale=0.125),
                     reads=[P(sb_)], writes=[("PT", pr)])
                if T["kind"] == "A":
                    S.op("dve", lambda e: e.tensor_tensor(
                        out=PT[pr][:, :].rearrange("p (c q) -> p c q", c=4),
                        in0=PT[pr][:, :].rearrange("p (c q) -> p c q", c=4),
                        in1=maskA[:, T["mask"], :].unsqueeze(1).to_broadcast([128, 4, 128]), op=ALU.mult),
                        reads=[("PT", pr), "maskA"], writes=[("PT", pr)])
                else:
                    S.op("dve", lambda e: e.tensor_tensor(out=PT[pr][:, 0:N], in0=PT[pr][:, 0:N],
                                                          in1=maskB[:, T["mi"], T["q0"]:512], op=ALU.mult),
                         reads=[("PT", pr), "maskB"], writes=[("PT", pr)])

            def issue_pv(T):
                pr, ob = T["pr"], T["ob"]
                if T["kind"] == "A":
                    g, qb, j = T["g"], T["qb"], T["j"]

                    def pvf(e):
                        for c in range(4):
                            ins = e.matmul(ps[ob][:, c * 65:(c + 1) * 65], lhsT=PT[pr][:, c * 128:(c + 1) * 128],
                                           rhs=vA[:, j, g, :], start=(T["first"] and c == 0), stop=(T["last"] and c == 3),
                                           skip_group_check=True)
                        return ins
                    S.op("pe", pvf, reads=[("PT", pr), ("vA", j)], writes=[P(ob)])
                    if T["last"]:
                        di = ring("den", 4)
                        psv = ps[ob][:, 0:260].rearrange("p (c e) -> p c e", c=4)
                        S.op("dve", lambda e: e.tensor_tensor(out=den[di][:, :].unsqueeze(2), in0=psv[:, :, 64:65],
                                                              in1=esink[:, 4 * g:4 * g + 4].unsqueeze(2), op=ALU.add),
                             reads=[P(ob), "esink"], writes=[("den", di)])
                        S.op("dve", lambda e: e.reciprocal(out=den[di][:, :], in_=den[di][:, :]), reads=[("den", di)],
                             writes=[("den", di)])
                        S.op("dve", lambda e: e.tensor_tensor(
                            out=nb[:, qb, g * 256:(g + 1) * 256].rearrange("p (c d) -> p c d", c=4),
                            in0=psv[:, :, 0:64], in1=den[di][:, :].unsqueeze(2).to_broadcast([128, 4, 64]), op=ALU.mult),
                            reads=[P(ob), ("den", di)], writes=[("nb", qb, g)])
                else:
                    h, j, q0 = T["h"], T["j"], T["q0"]
                    qb0 = q0 // 128

                    def pvf(e):
                        for qb in range(qb0, 4):
                            ins = e.matmul(ps[ob][:, qb * 65:(qb + 1) * 65],
                                           lhsT=PT[pr][:, qb * 128 - q0:(qb + 1) * 128 - q0], rhs=vB[:, j, h, :],
                                           start=(T["first"] and qb == qb0), stop=(T["last"] and qb == 3),
                                           skip_group_check=True)
                        return ins
                    S.op("pe", pvf, reads=[("PT", pr), ("vB", j)], writes=[P(ob)])
                    if T["last"]:
                        di = ring("den", 4)
                        psv = ps[ob][:, 0:260].rearrange("p (c e) -> p c e", c=4)
                        S.op("dve", lambda e: e.reciprocal(out=den[di][:, :].unsqueeze(2), in_=psv[:, :, 64:65]),
                             reads=[P(ob)], writes=[("den", di)])
                        S.op("dve", lambda e: e.tensor_tensor(
                            out=nb[:, :, 512 + h * 64:512 + (h + 1) * 64], in0=psv[:, :, 0:64],
                            in1=den[di][:, :].unsqueeze(2).to_broadcast([128, 4, 64]), op=ALU.mult),
                            reads=[P(ob), ("den", di)], writes=[("nb", b, 2 + h // 4) for b in range(4)])

            def phaseD_half(mx):
                for b in range(4):
                    S.op("act", lambda e, b=b: e.activation(
                        out=junk[:, 0:512], in_=nb[:, b, mx * 512:(mx + 1) * 512], func=AF.Square,
                        accum_out=st_ss[:, 4 * mx + b:4 * mx + b + 1]), reads=nbhalf(b, mx),
                        writes=["junk", ("ss", 4 * mx + b)])
                S.op("dve", lambda e: e.tensor_scalar(out=st_v[:, 4 * mx:4 * mx + 4], in0=st_ss[:, 4 * mx:4 * mx + 4],
                                                      scalar1=1.0 / 512, scalar2=EPS, op0=ALU.mult, op1=ALU.add),
                     reads=[("ss", 4 * mx + i) for i in range(4)], writes=[("stv", 4 * mx + i) for i in range(4)])
                S.op("pool", lambda e: e.tensor_tensor(out=st_r[:, 4 * mx:4 * mx + 4], in0=st_v[:, 4 * mx:4 * mx + 4],
                                                       in1=mhalf[:, 0:4], op=ALU.pow),
                     reads=[("stv", 4 * mx + i) for i in range(4)] + ["mhalf"],
                     writes=[("str", 4 * mx + i) for i in range(4)])
                for b in range(4):
                    S.op("dve", lambda e, b=b: e.tensor_scalar(
                        out=nb[:, b, mx * 512:(mx + 1) * 512], in0=nb[:, b, mx * 512:(mx + 1) * 512],
                        scalar1=st_r[:, 4 * mx + b:4 * mx + b + 1], scalar2=None, op0=ALU.mult),
                        reads=nbhalf(b, mx) + [("str", 4 * mx + b)], writes=nbhalf(b, mx))

            LOOK = 2
            n_a = sum(1 for T in tiles if T["kind"] == "A")
            assert len(tiles) % 2 == 0 and n_a % 2 == 0
            npair = len(tiles) // 2
            for n in range(npair + LOOK):
                if n < npair:
                    issue_score(tiles[2 * n])
                    issue_score(tiles[2 * n + 1])
                if n >= LOOK:
                    issue_pv(tiles[2 * (n - LOOK)])
                    issue_pv(tiles[2 * (n - LOOK) + 1])
                    if 2 * (n - LOOK) + 1 == n_a - 1:
                        phaseD_half(0)
            if t == 0:
                dbg("mix", nb.rearrange("p a b -> p (a b)"), [128, 4096], F32, [r for b in range(4) for r in nbres(b)])

            if stop == "C":
                break
            if t + 1 < NT:
                load_x(t + 1)
            phaseD_half(1)
            transposes_to_hT(lambda k: (gT[:, 2, k:k + 1], None), ["gT2"], fine=True)

            sls = [load_piece(5 + h) for h in range(2)]
            for b in range(4):
                banks = (2 + 2 * (b % 3), 3 + 2 * (b % 3))
                for h in range(2):
                    wv = ws[sls[h]][:, :].rearrange("p (k c) -> p k c", k=8)

                    def mmf(e, b=b, h=h, wv=wv, banks=banks):
                        for k in range(8):
                            ins = e.matmul(ps[banks[h]][:, :], lhsT=hT[:, k, b * 128:(b + 1) * 128], rhs=wv[:, k, :],
                                           start=(k == 0), stop=(k == 7))
                        return ins
                    S.op("pe", mmf, reads=[("ws", sls[h])] + [("hT", k) for k in range(8)], writes=[P(banks[h])])
                postnorm_residual(buf, b, banks, s, 0)
            if t == 0:
                dbg("x1", xt[buf][:].rearrange("p a b -> p (a b)"), [128, 4096], F32, [("xt", buf, b) for b in range(4)])

            if stop == "E":
                break
            prenorm_stats(buf)
            prenorm_tr(s, 2, 3)
            hoist = (t + 1 < NT) and stop is None

            for g8 in range(8):
                if g8 == 1 and hoist:
                    prenorm_stats((t + 1) % 2)
                sl = load_piece(7 + g8)
                wv = ws[sl][:, :].rearrange("p (k c) -> p k c", k=8)
                for fc in range(4):
                    f = g8 * 4 + fc
                    pb = f % 4

                    def mmf(e, fc=fc, pb=pb, wv=wv):
                        for k in range(8):
                            ins = e.matmul(ps[pb][:, :], lhsT=wv[:, k, fc * 128:(fc + 1) * 128], rhs=hT[:, k, :],
                                           start=(k == 0), stop=(k == 7))
                        return ins
                    S.op("pe", mmf, reads=[("ws", sl)] + [("hT", k) for k in range(8)], writes=[P(pb)])
                    ri = f % 2
                    S.op("act", lambda e, pb=pb, ri=ri: e.activation(out=rtmp[ri][:], in_=ps[pb][:, :], func=AF.Relu),
                         reads=[P(pb)], writes=[("rtmp", ri)])
                    S.op("pool", lambda e, f=f, ri=ri: e.tensor_tensor(out=uT[:, f, :], in0=rtmp[ri][:], in1=rtmp[ri][:],
                                                                       op=ALU.mult),
                         reads=[("rtmp", ri)], writes=[("uT", f)])

            if stop == "G":
                break
            if hoist:
                prenorm_tr((t + 1) // 4, 0, 1, banks=(4, 5))
            for g8 in range(8):
                sl = load_piece(15 + g8)
                wv = ws[sl][:, :].rearrange("p (f d) -> p f d", f=4)
                for fc in range(4):
                    f = g8 * 4 + fc

                    def mmf(e, f=f, fc=fc, wv=wv):
                        for b in range(4):
                            for h in range(2):
                                ins = e.matmul(ps[2 * b + h][:, :], lhsT=uT[:, f, b * 128:(b + 1) * 128],
                                               rhs=wv[:, fc, h * 512:(h + 1) * 512], start=(f == 0), stop=(f == 31))
                        return ins
                    S.op("pe", mmf, reads=[("ws", sl), ("uT", f)], writes=[P(i) for i in range(8)])

            for b in range(4):
                postnorm_residual(buf, b, (2 * b, 2 * b + 1), s, 1, store_rows=t)
            if stop == "T0":
                break

        S.op("pool", None, reads=stores_done + [("dbg", n) for n in dbg_outs])
        S.emit()
    return nc, list(dbg_outs.keys())


def _consts():
    bf = ml_dtypes.bfloat16
    k = np.arange(128)[:, None]
    qi = np.arange(512)[None, :]
    mB = np.zeros((128, 9, 512), np.float32)
    for mi in range(9):
        m = mi - 3 if mi < 8 else 5
        d = 128 * m + qi - k
        mult = ((d >= 0) & (d <= 128)).astype(np.float32) \
            + ((d >= 0) & (d % 4 == 0) & (d <= 512)).astype(np.float32) \
            + ((d >= 0) & (d % 16 == 0) & (d <= 2048)).astype(np.float32)
        mB[:, mi, :] = mult
    q = np.arange(128)[None, :]
    mA = np.zeros((128, 2, 128), np.float32)
    mA[:, 0, :] = (k <= q)
    mA[:, 1, :] = (k > q)
    sel = np.zeros((2, 2, 128), np.float32)
    sel[0, 0, :] = 1.0
    sel[1, 1, :] = 1.0
    invf = (500000.0 ** (-np.arange(0, 16, 2, dtype=np.float32) / 16.0)).astype(np.float32)[None, :]
    return dict(ident=np.eye(128, dtype=np.float32), maskB=mB.reshape(128, -1).astype(bf),
                maskA=mA.reshape(128, -1).astype(bf), sel=sel.reshape(2, -1), invf=invf)


_QA_PERM = [0, 4, 1, 5, 2, 6, 3, 7]


def _prep_shared(w_ada, b_ada, g_attn_pre, g_attn_post, w_in, sink_a, g_mix_a, g_mix_b, w_out, g_mlp_pre,
                 g_mlp_post, w_up, w_down):
    w_in0 = w_in[0]
    qa = w_in0[:, 0:512].reshape(D, 8, 64)[:, _QA_PERM, :].reshape(D, 512)
    ka, va = w_in0[:, 512:640], w_in0[:, 640:768]
    qb, kb, vb = w_in0[:, 768:1280], w_in0[:, 1280:1792], w_in0[:, 1792:2304]
    w_in_p = np.ascontiguousarray(np.concatenate([qa, qb, kb, vb, ka, va], axis=1))

    def colT(v):
        return np.ascontiguousarray(v.reshape(8, 128).T)
    sh = dict(w_ada=np.ascontiguousarray(w_ada[0]), b_ada=np.ascontiguousarray(b_ada[0:1]),
              gpreT=colT(g_attn_pre[0]), gmlpT=colT(g_mlp_pre[0]),
              gmixT=colT(np.concatenate([g_mix_a[0], g_mix_b[0]])),
              gposta=np.ascontiguousarray(g_attn_post[0:1]), gpostm=np.ascontiguousarray(g_mlp_post[0:1]),
              w_in=w_in_p, w_out=np.ascontiguousarray(w_out[0]), w_up=np.ascontiguousarray(w_up[0]),
              w_down=np.ascontiguousarray(w_down[0]), sink=np.ascontiguousarray(sink_a[0:1]))
    sh.update(_consts())
    return sh


_NC_CACHE = {}


def kernel(x, c, positions, w_ada, b_ada, g_attn_pre, g_attn_post, w_in, sink_a, g_mix_a, g_mix_b, w_out,
           g_mlp_pre, g_mlp_post, w_up, w_down, _debug=None):
    x = np.asarray(x, np.float32)
    c = np.asarray(c, np.float32)
    positions = np.asarray(positions, np.int32)
    args = [np.asarray(a, np.float32) for a in (w_ada, b_ada, g_attn_pre, g_attn_post, w_in, sink_a, g_mix_a,
                                                g_mix_b, w_out, g_mlp_pre, g_mlp_post, w_up, w_down)]
    shared = _prep_shared(*args)
    key = tuple(_debug) if _debug else None
    if key not in _NC_CACHE:
        _NC_CACHE[key] = build_nc(_debug)
    nc, dbg_names = _NC_CACHE[key]
    in_maps = []
    for i in range(NCORES):
        m = dict(shared)
        m["x"] = np.ascontiguousarray(x[2 * i:2 * i + 2].reshape(2 * SEQ, D))
        cc = c[2 * i:2 * i + 2]
        m["cT"] = np.ascontiguousarray(cc.reshape(2, 8, 128).transpose(2, 1, 0).reshape(128, 16))
        pp = positions[2 * i:2 * i + 2]
        m["pos"] = np.ascontiguousarray(pp.reshape(2, 16, 128).transpose(2, 0, 1).reshape(128, 32))
        in_maps.append(m)
    res = run_bass_kernel_spmd(nc, in_maps, core_ids=list(range(NCORES)))
    out = np.stack([r["out"].reshape(2, SEQ, D) for r in res.results], axis=0).reshape(2 * NCORES, SEQ, D)
    if _debug:
        return out.astype(np.float32), [{n: r["dbg_" + n] for n in dbg_names} for r in res.results]
    return out.astype(np.float32)
```
